# Optimizing a Trainium2 kernel written in Bass

```python
import jax, jax.numpy as jnp
from jax import lax
import numpy as np

D_MODEL = 4096
BATCH = 4
SEQ = 2048
DEPTH = 1

D_MIX = D_MODEL
GLA_HEADS = 8
GLA_DK = D_MODEL // 32
GLA_DV = D_MODEL // 16
GLA_GATE_RANK = 16
GLA_GATE_NORMALIZER = 16.0
GLA_CHUNK = 64
MOBA_HEADS = 16
MOBA_HD = D_MODEL // 32
MOBA_BLOCK = 256
MOBA_TOPK = 3
MOBA_QCHUNK = 16
ROPE_THETA = 500000.0
ROPE_DIMS = MOBA_HD // 4
PEER_HEADS = 8
PEER_NKEYS = 128
PEER_EXPERTS = PEER_NKEYS * PEER_NKEYS
PEER_DQ = 256
PEER_TOPK = 16
PEER_TCHUNK = 128
EPS = 1e-6

GLA_QK_W = GLA_HEADS * GLA_DK
GLA_V_W = GLA_HEADS * GLA_DV
MOBA_W = MOBA_HEADS * MOBA_HD
IN_COLS = 2 * GLA_QK_W + 2 * GLA_V_W + GLA_GATE_RANK + 3 * MOBA_W
SPLIT_POINTS = (GLA_QK_W, 2 * GLA_QK_W, 2 * GLA_QK_W + GLA_V_W, 2 * GLA_QK_W + 2 * GLA_V_W,
                2 * GLA_QK_W + 2 * GLA_V_W + GLA_GATE_RANK,
                2 * GLA_QK_W + 2 * GLA_V_W + GLA_GATE_RANK + MOBA_W,
                2 * GLA_QK_W + 2 * GLA_V_W + GLA_GATE_RANK + 2 * MOBA_W)

kernel_name = "hymba_gla_moba_peer_layer"


def rmsnorm(x, g):
    xf = x.astype(jnp.float32)
    r = lax.rsqrt(jnp.mean(xf * xf, axis=-1, keepdims=True) + EPS)
    return (xf * r).astype(x.dtype) * g


def partial_rope(x, pos):
    half = ROPE_DIMS // 2
    inv = ROPE_THETA ** (-jnp.arange(half, dtype=jnp.float32) / half)
    ang = pos.astype(jnp.float32)[:, None] * inv[None, :]
    cos = jnp.cos(ang).astype(x.dtype)
    sin = jnp.sin(ang).astype(x.dtype)
    x1, x2, xp = x[..., :half], x[..., half:ROPE_DIMS], x[..., ROPE_DIMS:]
    return jnp.concatenate([x1 * cos - x2 * sin, x2 * cos + x1 * sin, xp], axis=-1)


def gla_mixer(q, k, v, g_out, gate_lr, w_gate_up, b_gate, gla_norm_g):
    B, T, _ = q.shape
    H, dk, dv, C = GLA_HEADS, GLA_DK, GLA_DV, GLA_CHUNK
    N = T // C
    f32 = jnp.float32
    log_a = jax.nn.log_sigmoid((gate_lr @ w_gate_up + b_gate).astype(f32)) / GLA_GATE_NORMALIZER

    def chunked(t, d):
        return t.reshape(B, N, C, H, d).transpose(0, 3, 1, 2, 4)

    qc = chunked(q, dk).astype(f32) * (dk ** -0.5)
    kc = chunked(k, dk).astype(f32)
    vc = chunked(v, dv).astype(f32)
    bc = jnp.cumsum(chunked(log_a, dk), axis=3)
    b_last = bc[..., -1:, :]
    q_dec = qc * jnp.exp(bc)
    k_intra = kc * jnp.exp(-bc)
    k_state = kc * jnp.exp(b_last - bc)
    causal = jnp.tril(jnp.ones((C, C), dtype=bool))
    att = jnp.where(causal, jnp.einsum('bhncd,bhnsd->bhncs', q_dec, k_intra), 0.0)
    o_intra = jnp.einsum('bhncs,bhnse->bhnce', att, vc)
    chunk_update = jnp.einsum('bhnsd,bhnse->nbhde', k_state, vc)
    chunk_decay = jnp.exp(b_last[..., 0, :]).transpose(2, 0, 1, 3)

    def step(S, inp):
        dec, upd = inp
        return dec[..., None] * S + upd, S

    S0 = jnp.zeros((B, H, dk, dv), f32)
    _, S_prev = lax.scan(step, S0, (chunk_decay, chunk_update))
    o_inter = jnp.einsum('bhncd,nbhde->bhnce', q_dec, S_prev)
    o = (o_intra + o_inter).transpose(0, 2, 3, 1, 4).reshape(B, T, H, dv)
    o = rmsnorm(o, gla_norm_g) * jax.nn.silu(g_out.reshape(B, T, H, dv).astype(f32))
    return o.reshape(B, T, H * dv).astype(v.dtype)


def moba_mixer(q, k, v, q_norm_g, k_norm_g):
    B, T, _ = q.shape
    H, hd, L, QC = MOBA_HEADS, MOBA_HD, MOBA_BLOCK, MOBA_QCHUNK
    Tp = -(-T // L) * L
    nb = Tp // L
    n_sel = min(MOBA_TOPK, nb)
    f32 = jnp.float32

    def heads(t):
        t = t.reshape(B, T, H, hd)
        t = jnp.pad(t, ((0, 0), (0, Tp - T), (0, 0), (0, 0)))
        return t.transpose(0, 2, 1, 3)

    pos = jnp.arange(Tp)
    qh = partial_rope(rmsnorm(heads(q), q_norm_g), pos) * (hd ** -0.5)
    kh = partial_rope(rmsnorm(heads(k), k_norm_g), pos)
    vh = heads(v)
    kb = kh.reshape(B, H, nb, L, hd)
    vb = vh.reshape(B, H, nb, L, hd)
    k_mean = jnp.mean(kb.astype(f32), axis=3)
    bi = jnp.arange(B)[:, None, None, None]
    hi = jnp.arange(H)[None, :, None, None]
    slot = jnp.arange(n_sel)

    def chunk_fn(c):
        start = c * QC
        blk = start // L
        qc = lax.dynamic_slice_in_dim(qh, start, QC, axis=2)
        gate = jnp.einsum('bhqd,bhnd->bhqn', qc.astype(f32), k_mean)
        gate = jnp.where(jnp.arange(nb) < blk, gate, -jnp.inf)
        _, sel = lax.top_k(gate, n_sel)
        slot_ok = slot < blk
        k_sel = kb[bi, hi, sel]
        v_sel = vb[bi, hi, sel]
        s_sel = jnp.einsum('bhqd,bhqskd->bhqsk', qc, k_sel, preferred_element_type=f32)
        s_sel = jnp.where(slot_ok[:, None], s_sel, -jnp.inf).reshape(B, H, QC, n_sel * L)
        k_own = lax.dynamic_index_in_dim(kb, blk, axis=2, keepdims=False)
        v_own = lax.dynamic_index_in_dim(vb, blk, axis=2, keepdims=False)
        qpos = start + jnp.arange(QC)
        kpos = blk * L + jnp.arange(L)
        s_own = jnp.einsum('bhqd,bhkd->bhqk', qc, k_own, preferred_element_type=f32)
        s_own = jnp.where(kpos[None, :] <= qpos[:, None], s_own, -jnp.inf)
        p = jax.nn.softmax(jnp.concatenate([s_own, s_sel], axis=-1), axis=-1).astype(v.dtype)
        p_own = p[..., :L]
        p_sel = p[..., L:].reshape(B, H, QC, n_sel, L)
        return (jnp.einsum('bhqk,bhkd->bhqd', p_own, v_own)
                + jnp.einsum('bhqsk,bhqskd->bhqd', p_sel, v_sel))

    out = lax.map(chunk_fn, jnp.arange(Tp // QC))
    out = out.transpose(1, 2, 0, 3, 4).reshape(B, H, Tp, hd)[:, :, :T]
    return out.transpose(0, 2, 1, 3).reshape(B, T, H * hd)


def peer_ffn(x, w_pq, peer_keys, peer_u, peer_v):
    B, T, D = x.shape
    H, K, NK, TC = PEER_HEADS, PEER_TOPK, PEER_NKEYS, PEER_TCHUNK
    N = B * T
    f32 = jnp.float32
    xt = x.reshape(N, D)
    q = (xt @ w_pq).reshape(N, H, 2, PEER_DQ // 2)
    s = jnp.einsum('thpd,hpkd->thpk', q.astype(f32), peer_keys.astype(f32))
    s1, i1 = lax.top_k(s[:, :, 0], K)
    s2, i2 = lax.top_k(s[:, :, 1], K)
    cand = (s1[..., :, None] + s2[..., None, :]).reshape(N, H, K * K)
    top_s, top_c = lax.top_k(cand, K)
    expert = (jnp.take_along_axis(i1, top_c // K, axis=-1) * NK
              + jnp.take_along_axis(i2, top_c % K, axis=-1))
    g = jax.nn.softmax(top_s, axis=-1)
    nc = N // TC

    def chunk_fn(inp):
        xc, ec, gc = inp
        ef = ec.reshape(TC, H * K)
        u = peer_u[ef]
        h = jax.nn.gelu(jnp.einsum('td,ted->te', xc, u), approximate=False)
        w = gc.reshape(TC, H * K).astype(xc.dtype) * h
        return jnp.einsum('te,ted->td', w, peer_v[ef])

    out = lax.map(chunk_fn, (xt.reshape(nc, TC, D), expert.reshape(nc, TC, H, K),
                             g.reshape(nc, TC, H, K)))
    return out.reshape(B, T, D)


def setup_inputs(seed: int = 0) -> dict:
    key = jax.random.key(seed)
    ks = jax.random.split(key, 16)
    f32 = jnp.float32

    def w(k, shape, fan_in):
        return jax.random.normal(k, shape, f32) * (fan_in ** -0.5)

    def gain(k, shape):
        return 1.0 + 0.02 * jax.random.normal(k, shape, f32)

    return {
        "x": jax.random.normal(ks[0], (BATCH, SEQ, D_MODEL), f32),
        "norm_mix_g": gain(ks[1], (DEPTH, D_MODEL)),
        "w_in": w(ks[2], (DEPTH, D_MODEL, IN_COLS), D_MODEL),
        "w_gate_up": w(ks[3], (DEPTH, GLA_GATE_RANK, GLA_QK_W), GLA_GATE_RANK),
        "b_gate": 0.1 * jax.random.normal(ks[4], (DEPTH, GLA_QK_W), f32),
        "gla_norm_g": gain(ks[5], (DEPTH, GLA_DV)),
        "q_norm_g": gain(ks[6], (DEPTH, MOBA_HD)),
        "k_norm_g": gain(ks[7], (DEPTH, MOBA_HD)),
        "w_out": w(ks[8], (DEPTH, D_MIX, D_MODEL), D_MIX),
        "norm_ffn_g": gain(ks[9], (DEPTH, D_MODEL)),
        "peer_wq": w(ks[10], (DEPTH, D_MODEL, PEER_HEADS * PEER_DQ), D_MODEL),
        "peer_keys": w(ks[11], (DEPTH, PEER_HEADS, 2, PEER_NKEYS, PEER_DQ // 2), PEER_DQ // 2),
        "peer_u": w(ks[12], (DEPTH, PEER_EXPERTS, D_MODEL), D_MODEL),
        "peer_v": w(ks[13], (DEPTH, PEER_EXPERTS, D_MODEL), PEER_HEADS * PEER_TOPK),
    }


def reference(x, norm_mix_g, w_in, w_gate_up, b_gate, gla_norm_g, q_norm_g, k_norm_g,
              w_out, norm_ffn_g, peer_wq, peer_keys, peer_u, peer_v):
    for l in range(DEPTH):
        h = rmsnorm(x, norm_mix_g[l])
        proj = h @ w_in[l]
        gq, gk, gv, gg, glr, mq, mk, mv = jnp.split(proj, SPLIT_POINTS, axis=-1)
        o_gla = gla_mixer(gq, gk, gv, gg, glr, w_gate_up[l], b_gate[l], gla_norm_g[l])
        o_moba = moba_mixer(mq, mk, mv, q_norm_g[l], k_norm_g[l])
        x = x + jnp.concatenate([o_gla, o_moba], axis=-1) @ w_out[l]
        x = x + peer_ffn(rmsnorm(x, norm_ffn_g[l]), peer_wq[l], peer_keys[l], peer_u[l], peer_v[l])
    return x
```

```python
import numpy as np
import concourse.bass as bass
import concourse.mybir as mybir
from concourse.bass_utils import run_bass_kernel_spmd

F32 = mybir.dt.float32
BF16 = mybir.dt.bfloat16
AF = mybir.ActivationFunctionType
ALU = mybir.AluOpType
AX = mybir.AxisListType

D = 4096
TOK = 1024
WIN = 2048
NEG = -32768.0
BIG = 1.0e30
EPS = 1e-6


class Buf:
    __slots__ = ("w", "rs", "x")

    def __init__(self):
        self.w = None
        self.rs = []
        self.x = False


class Tl:
    __slots__ = ("ap", "b")

    def __init__(self, ap, b=None):
        self.ap = ap
        self.b = b if b is not None else Buf()

    def __getitem__(self, k):
        return self.ap[k]


class Op:
    __slots__ = ("eng", "fn", "waits", "signal", "is_dma", "sem", "val")


def _bufs(xs):
    return [x.b if isinstance(x, Tl) else x for x in xs]


class Prog:
    ENGS = ("pe", "act", "dve", "pool", "sp")

    def __init__(self, nc, n_dma_sems=40):
        self.nc = nc
        self.ops = []
        self.n_dma_sems = n_dma_sems
        self.G = Buf()
        self.eng_obj = {"pe": nc.tensor, "act": nc.scalar, "dve": nc.vector,
                        "pool": nc.gpsimd, "sp": nc.sync}

    def op(self, eng, fn, reads=(), writes=(), dma=False, barrier=False):
        o = Op()
        o.eng, o.fn, o.is_dma, o.signal, o.sem, o.val = eng, fn, dma, False, None, None
        reads = _bufs(reads)
        writes = _bufs(writes)
        xr = [b for b in reads if b.x and b not in writes]
        if barrier:
            writes = writes + [self.G]
        else:
            reads = reads + [self.G]
        deps = {}
        oid = len(self.ops)
        for b in reads:
            if b.w is not None:
                deps[b.w] = True
        for b in writes:
            if b.w is not None:
                deps.setdefault(b.w, False)
            for r in b.rs:
                deps.setdefault(r, False)
        for b in xr:
            for r in b.rs:
                deps.setdefault(r, False)
        o.waits = deps
        self.ops.append(o)
        for b in writes:
            b.w = oid
            b.rs = []
        for b in reads:
            if b not in writes:
                b.rs.append(oid)
        return oid

    def dma(self, queue, out, in_, reads=(), writes=()):
        return self.op(queue, lambda e, o=out, i=in_: e.dma_start(out=o, in_=i), reads, writes, dma=True)

    def barrier(self, scratch):
        self.op("pool", lambda e, s=scratch: e.memset(s, 0.0), barrier=True)

    def emit(self):
        nc, ops = self.nc, self.ops
        eng_seq = {e: 0 for e in self.ENGS}
        nxt = 0
        nxt_sw = 0
        n_hw = self.n_dma_sems - 12
        slot_cnt = [0] * self.n_dma_sems
        slot_last = [None] * self.n_dma_sems
        key = [None] * len(ops)
        for i, o in enumerate(ops):
            if o.is_dma:
                if o.eng == "pool":
                    k = n_hw + nxt_sw
                    nxt_sw = (nxt_sw + 1) % (self.n_dma_sems - n_hw)
                else:
                    k = nxt
                    nxt = (nxt + 1) % n_hw
                slot_cnt[k] += 1
                key[i] = (("dma", k), slot_cnt[k])
                if slot_last[k] is not None:
                    o.waits[slot_last[k]] = True
                slot_last[k] = i
            else:
                eng_seq[o.eng] += 1
                key[i] = (o.eng, eng_seq[o.eng])
        seen = {e: {} for e in self.ENGS}
        final_waits = [None] * len(ops)
        for i, o in enumerate(ops):
            fw = {}
            sn = seen[o.eng]
            for p, raw in o.waits.items():
                po = ops[p]
                sk, order = key[p]
                if (not po.is_dma) and po.eng == o.eng:
                    if o.eng in ("pe", "sp") or not raw:
                        continue
                if sn.get(sk, 0) >= order:
                    continue
                if fw.get(sk, (0, None))[0] < order:
                    fw[sk] = (order, p)
            for sk, (order, p) in fw.items():
                sn[sk] = order
                ops[p].signal = True
            final_waits[i] = [p for (_, p) in fw.values()]
        LIMIT = 1500
        sems, cnt, nsem = {}, {}, [0]

        def fresh(sk):
            sems[sk] = nc.alloc_semaphore(name="s%d" % nsem[0])
            nsem[0] += 1
            cnt[sk] = 0
        for e in ("pe", "act", "dve", "pool"):
            fresh(e)
        for k in range(self.n_dma_sems):
            fresh(("dma", k))
        maxc = 0
        for i, o in enumerate(ops):
            e = self.eng_obj[o.eng]
            for p in final_waits[i]:
                e.wait_ge(ops[p].sem, ops[p].val)
            inst = o.fn(e)
            if o.is_dma or o.signal:
                sk = key[i][0] if o.is_dma else o.eng
                inc = 16 if o.is_dma else 1
                if cnt[sk] + inc > LIMIT:
                    fresh(sk)
                cnt[sk] += inc
                maxc = max(maxc, cnt[sk])
                o.sem, o.val = sems[sk], cnt[sk]
                inst.then_inc(sems[sk], inc)
        last = {}
        for i, o in enumerate(ops):
            if o.is_dma:
                last[key[i][0]] = o
        for sk, o in last.items():
            nc.sync.wait_ge(o.sem, o.val)
        return dict(n_ops=len(ops), max_cnt=maxc, n_sems=nsem[0])


class Arena:
    def __init__(self, ap_bf16):
        self.ap = ap_bf16
        self.n = ap_bf16.shape[1]
        self.off = 0

    def reset(self):
        self.off = 0

    def take(self, shape, dt):
        free = int(np.prod(shape[1:]))
        nb = free * (2 if dt == BF16 else 4)
        nb = (nb + 63) // 64 * 64
        assert self.off + nb // 2 <= self.n, ("arena overflow", shape, self.off, self.n)
        v = self.ap[0:shape[0], self.off:self.off + nb // 2]
        self.off += nb // 2
        if dt != BF16:
            v = v.bitcast(dt)
        v = v[:, 0:free]
        if len(shape) == 3:
            v = v.rearrange("p (a b) -> p a b", a=shape[1])
        elif len(shape) == 4:
            v = v.rearrange("p (a b c) -> p a b c", a=shape[1], b=shape[2])
        return Tl(v)


SEG = dict(gq=0, gk=1024, gv=2048, gg=4096, glr=6144, mq=6160, mk=8208, mv=10256)
IN_COLS = 12304


class Builder:
    def __init__(self, stop_after=99, dbg=False, only=None, small=()):
        self.stop_after = stop_after
        self.dbg = dbg
        self.only = only
        self.small = set(small)
        nc = bass.Bass("TRN2", target_bir_lowering=False)
        self.nc = nc
        self.P = Prog(nc)
        self._rot = {}

        def din(name, shape, dt=F32):
            if name in self.small:
                shape = [128, 128]
            return nc.dram_tensor(name, list(shape), dt, kind="ExternalInput").ap()

        self.xT = din("xT", [D, WIN])
        self.w_in = din("w_in", [D, IN_COLS])
        self.w_out = din("w_out", [D, D])
        self.w_pq = din("w_pq", [D, 2048])
        self.uT = din("uT", [D, 16384])
        self.pv = din("pv", [16384, D])
        self.keysT = din("keysT", [128, 16 * 128])
        self.wgu = din("wgu", [16, 1024])
        self.cvec = din("cvec", [128, 32 + 32 + 8 + 2 + 1 + 1])
        self.rope = din("rope", [128, 2 * WIN])
        self.mmask = din("mmask", [128, 3 * 64])
        self.cmat = din("cmat", [128, 4 * 128])
        self.ind = din("ind", [8, 8 * 128])
        self.band = din("band", [128, 2048])
        kind = "ExternalOutput" if dbg else "Internal"
        self.pT_d = Tl(nc.dram_tensor("pT_d", [IN_COLS, WIN], BF16, kind=kind).ap())
        self.vtm_d = Tl(nc.dram_tensor("vtm_d", [WIN, 4096], BF16, kind=kind).ap())
        self.x2T_d = Tl(nc.dram_tensor("x2T_d", [D, TOK], F32, kind=kind).ap())
        self.oT_d = Tl(nc.dram_tensor("oT_d", [D, TOK], BF16, kind=kind).ap()) if dbg else None
        self.wT_d = Tl(nc.dram_tensor("wT_d", [128, 128, TOK], BF16, kind="Internal").ap())
        self.outT = Tl(nc.dram_tensor("outT", [D, TOK], F32, kind="ExternalOutput").ap())

        def sb(name, shape, dt):
            return Tl(nc.alloc_sbuf_tensor(name, list(shape), dt).ap())

        self.A = Arena(nc.alloc_sbuf_tensor("arenaA", [128, 32768], BF16).ap())
        self.B = Arena(nc.alloc_sbuf_tensor("arenaB", [128, 32768], BF16).ap())
        self.C = Arena(nc.alloc_sbuf_tensor("arenaC", [128, 32768], BF16).ap())
        self.cm = sb("cm", [128, 512], BF16)
        self.ident = self.cm.ap[:, 0:128]
        self.tri = self.cm.ap[:, 128:256]
        self.RT = self.cm.ap[:, 256:384]
        self.ones = self.cm.ap[:, 384:512]
        self.cv = sb("cv", [128, 76], F32)
        self.nb = sb("nb", [128, 8], F32)
        self.indb = sb("indb", [8, 1024], BF16)
        self.scr = sb("scr", [128, 8], F32)
        self.stage = [sb("stage%d" % i, [128, 512], F32) for i in range(4)]
        self.xc = [sb("xc%d" % i, [128, 512], F32) for i in range(2)]
        self.cm_f = self.stage[0]
        self.ps = [Tl(nc.alloc_psum_tensor("ps%d" % i, [128, 512], F32).ap()) for i in range(8)]
        for p_ in self.ps:
            p_.b.x = True

    def rot(self, name, lst):
        i = self._rot.get(name, 0)
        self._rot[name] = i + 1
        return lst[i % len(lst)]

    def psr(self, n=4):
        return self.rot("ps%d" % n, self.ps[0:n])

    def mm(self, out, lhsT, rhs, start, stop, R, W):
        self.P.op("pe", lambda e, o=out, l=lhsT, r=rhs, s=start, t=stop:
                  e.matmul(o, lhsT=l, rhs=r, start=s, stop=t), R, W)

    def tr(self, out, in_, ident, R, W):
        self.P.op("pe", lambda e, o=out, i=in_, d=ident: e.transpose(o, i, d), R, W)

    def act(self, out, in_, func, R, W, scale=None, bias=None, accum=None):
        kw = {}
        if scale is not None:
            kw["scale"] = scale
        if bias is not None:
            kw["bias"] = bias
        if accum is not None:
            kw["accum_out"] = accum
        self.P.op("act", lambda e, o=out, i=in_, f=func, k=kw: e.activation(out=o, in_=i, func=f, **k), R, W)

    def tt(self, eng, out, in0, in1, op, R, W):
        self.P.op(eng, lambda e, o=out, a=in0, b=in1, p=op: e.tensor_tensor(out=o, in0=a, in1=b, op=p), R, W)

    def stt(self, out, in0, scalar, in1, op0, op1, R, W):
        self.P.op("dve", lambda e, o=out, a=in0, s=scalar, b=in1, p0=op0, p1=op1:
                  e.scalar_tensor_tensor(out=o, in0=a, scalar=s, in1=b, op0=p0, op1=p1), R, W)

    def ts(self, eng, out, in0, s1, s2, op0, op1, R, W):
        if op1 is None:
            self.P.op(eng, lambda e, o=out, a=in0, x=s1, p0=op0:
                      e.tensor_scalar(out=o, in0=a, scalar1=x, scalar2=None, op0=p0), R, W)
        else:
            self.P.op(eng, lambda e, o=out, a=in0, x=s1, y=s2, p0=op0, p1=op1:
                      e.tensor_scalar(out=o, in0=a, scalar1=x, scalar2=y, op0=p0, op1=p1), R, W)

    def cp(self, eng, out, in_, R, W):
        if eng == "act":
            self.P.op("act", lambda e, o=out, i=in_: e.copy(out=o, in_=i), R, W)
        else:
            self.P.op(eng, lambda e, o=out, i=in_: e.tensor_copy(out=o, in_=i), R, W)

    @staticmethod
    def interleave(gens):
        gens = list(gens)
        while gens:
            for g in list(gens):
                try:
                    next(g)
                except StopIteration:
                    gens.remove(g)

    def evac_eng(self):
        return self.rot("evac", ["act", "dve"])

    def recip(self, out, in_, R, W):
        self.P.op("dve", lambda e, o=out, i=in_: e.reciprocal(out=o, in_=i), R, W)

    def rstd_from_ssq(self, out_t, ssq_ps, n, inv_n):
        self.act(out_t.ap[:, 0:n], ssq_ps.ap[:, 0:n], AF.Sqrt, [ssq_ps, self.epst], [out_t], scale=inv_n, bias=self.epsc)
        self.recip(out_t.ap[:, 0:n], out_t.ap[:, 0:n], [out_t], [out_t])

    def setup(self):
        P = self.P
        P.dma("sp", self.cm_f.ap, self.cmat, writes=[self.cm_f])
        self.cp("dve", self.cm.ap, self.cm_f.ap, [self.cm_f], [self.cm])
        P.dma("sp", self.cv.ap, self.cvec, writes=[self.cv])
        self.ts("dve", self.nb.ap, self.cv.ap[:, 64:72], -1.0, None, ALU.mult, None, [self.cv], [self.nb])
        P.dma("pool", self.indb.ap, self.ind, writes=[self.indb])
        self.epst = Tl(self.nc.alloc_sbuf_tensor("epst", [128, 1], F32).ap())
        P.op("dve", lambda e, o=self.epst.ap: e.memset(o, EPS), [], [self.epst])
        self.epsc = self.epst.ap[:, 0:1]
        self.g1 = self.cv.ap[:, 0:32]
        self.g2 = self.cv.ap[:, 32:64]
        self.gng = self.cv.ap[:, 72:74]
        self.qg = self.cv.ap[:, 74:75]
        self.kg = self.cv.ap[:, 75:76]

    def load_wgroup(self, wbufs, wdram, c0, ncols):
        wb = self.rot("wg", wbufs)
        src = wdram[:, c0:c0 + ncols].rearrange("(c p) n -> p c n", p=128)
        half = 16
        self.P.dma("pool", wb.ap[:, 0:half, 0:ncols], src[:, 0:half, :], writes=[wb])
        self.P.dma("pool", wb.ap[:, half:32, 0:ncols], src[:, half:32, :], writes=[wb])
        return wb

    def phase1(self):
        P = self.P
        self.A.reset(); self.B.reset(); self.C.reset()
        hn = [self.A.take([128, 32, 512], BF16), self.A.take([128, 32, 512], BF16),
              self.B.take([128, 32, 512], BF16), self.B.take([128, 32, 512], BF16)]
        wbufs = [self.C.take([128, 32, 512], BF16), self.C.take([128, 32, 512], BF16)]
        sq = [Tl(self.stage[2].ap.bitcast(BF16)[:, 0:512], self.stage[2].b),
              Tl(self.stage[3].ap.bitcast(BF16)[:, 0:512], self.stage[3].b)]
        rstd = self.stage[0]
        for tq in range(4):
            ssq = self.ps[7]
            for kc in range(32):
                xc = self.rot("xc", self.xc)
                P.dma("sp", xc.ap, self.xT[kc * 128:(kc + 1) * 128, tq * 512:(tq + 1) * 512], writes=[xc])
                s = self.rot("sq", sq)
                self.act(s.ap, xc.ap, AF.Square, [xc], [s])
                self.mm(ssq.ap, self.ones, s.ap, kc == 0, kc == 31, [s, self.cm], [ssq])
            self.rstd_from_ssq(rstd, ssq, 512, 1.0 / D)
            for kc in range(32):
                xc = self.rot("xc", self.xc)
                P.dma("sp", xc.ap, self.xT[kc * 128:(kc + 1) * 128, tq * 512:(tq + 1) * 512], writes=[xc])
                self.stt(hn[tq].ap[:, kc, :], xc.ap, self.g1[:, kc:kc + 1], rstd.ap, ALU.mult, ALU.mult,
                         [xc, rstd, self.cv], [hn[tq]])
        stg = [Tl(self.stage[i].ap.bitcast(BF16)[:, 0:512], self.stage[i].b) for i in range(4)]
        groups = []
        for name, n, lay, allt in (("gk", 1024, "fm", True), ("gv", 2048, "tm", True), ("glr", 16, "fm", True),
                                   ("mk", 2048, "fm", True), ("mv", 2048, "tm", True),
                                   ("gq", 1024, "fm", False), ("gg", 2048, "fm", False), ("mq", 2048, "fm", False)):
            for g0 in range(0, n, 512):
                groups.append((name, SEG[name] + g0, min(512, n - g0), lay, allt, g0))
        for (name, c0, ncols, lay, allt, g0) in groups:
            wb = self.load_wgroup(wbufs, self.w_in, c0, ncols)
            tqs = range(4) if allt else range(2, 4)
            if lay == "fm":
                for tq in tqs:
                    for cc in range((ncols + 127) // 128):
                        m = min(128, ncols - cc * 128)
                        ps = self.psr(6)
                        for kc in range(32):
                            self.mm(ps.ap[0:m, :], wb.ap[:, kc, cc * 128:cc * 128 + m], hn[tq].ap[:, kc, :],
                                    kc == 0, kc == 31, [wb, hn[tq]], [ps])
                        st = self.rot("stg", stg)
                        self.cp(self.evac_eng(), st.ap[0:m, :], ps.ap[0:m, :], [ps], [st])
                        r0 = c0 + cc * 128
                        P.dma("sp", self.pT_d.ap[r0:r0 + m, tq * 512:(tq + 1) * 512], st.ap[0:m, :],
                              reads=[st], writes=[self.pT_d])
            else:
                vcol = (0 if name == "gv" else 2048) + g0
                for tq in tqs:
                    for t4 in range(4):
                        ps = self.psr(6)
                        for kc in range(32):
                            self.mm(ps.ap, hn[tq].ap[:, kc, t4 * 128:(t4 + 1) * 128], wb.ap[:, kc, :],
                                    kc == 0, kc == 31, [wb, hn[tq]], [ps])
                        st = self.rot("stg", stg)
                        self.cp(self.evac_eng(), st.ap, ps.ap, [ps], [st])
                        t0 = tq * 512 + t4 * 128
                        P.dma("sp", self.vtm_d.ap[t0:t0 + 128, vcol:vcol + 512], st.ap,
                              reads=[st], writes=[self.vtm_d])

    def phase2(self):
        P = self.P
        self.P.barrier(self.scr.ap)
        A, B, C = self.A, self.B, self.C
        A.reset(); B.reset(); C.reset()
        self.oT = C.take([128, 32, TOK], BF16)
        glrT = A.take([16, WIN], BF16)
        wgu = A.take([16, 1024], BF16)
        sp = A.take([128, WIN], F32)
        cum = A.take([128, WIN], F32)
        enb = A.take([128, WIN], F32)
        eb = A.take([128, TOK], F32)
        rmask = A.take([128, WIN], BF16)
        kT = A.take([128, WIN], BF16)
        qT = A.take([128, TOK], BF16)
        qdec = A.take([128, TOK], BF16)
        kin = A.take([128, WIN], BF16)
        kst = B.take([128, WIN], BF16)
        kstm = B.take([128, 16, 128], BF16)
        vtm = B.take([128, 16, 256], BF16)
        dlast = B.take([128, 16], F32)
        S = B.take([128, 256], F32)
        Sb = B.take([128, 256], BF16)
        attm = [B.take([128, 128], BF16), B.take([128, 128], BF16)]
        oTf = B.take([128, 2, TOK], F32)
        gT = B.take([128, 2, TOK], BF16)
        sqb = [B.take([128, 512], BF16), B.take([128, 512], BF16)]
        rstd = B.take([128, 512], F32)
        sg = B.take([128, 512], F32)
        tmp = B.take([128, 512], F32)
        P.dma("sp", glrT.ap, self.pT_d.ap[SEG["glr"]:SEG["glr"] + 16, :], reads=[self.pT_d], writes=[glrT])
        P.dma("pool", wgu.ap, self.wgu, writes=[wgu])
        P.op("pool", lambda e, o=rmask.ap: e.memset(o, 1.0), [], [rmask])
        P.op("pool", lambda e, o=rmask.ap[:, 0:WIN:128]: e.memset(o, 0.0), [], [rmask])
        for h in range(8):
            P.dma("sp", kT.ap, self.pT_d.ap[SEG["gk"] + h * 128:SEG["gk"] + (h + 1) * 128, :],
                  reads=[self.pT_d], writes=[kT])
            P.dma("sp", qT.ap, self.pT_d.ap[SEG["gq"] + h * 128:SEG["gq"] + (h + 1) * 128, TOK:WIN],
                  reads=[self.pT_d], writes=[qT])
            P.dma("sp", vtm.ap, self.vtm_d.ap[:, h * 256:(h + 1) * 256].rearrange("(n p) e -> p n e", p=128),
                  reads=[self.vtm_d], writes=[vtm])
            P.dma("sp", gT.ap, self.pT_d.ap[SEG["gg"] + h * 256:SEG["gg"] + (h + 1) * 256, TOK:WIN]
                  .rearrange("(c p) t -> p c t", p=128), reads=[self.pT_d], writes=[gT])
            for tq in range(4):
                ps = self.psr()
                self.mm(ps.ap, wgu.ap[:, h * 128:(h + 1) * 128], glrT.ap[:, tq * 512:(tq + 1) * 512], True, True,
                        [wgu, glrT], [ps])
                self.act(sp.ap[:, tq * 512:(tq + 1) * 512], ps.ap, AF.Exp, [ps, self.nb], [sp],
                         scale=-1.0, bias=self.nb.ap[:, h:h + 1])
            self.act(sp.ap, sp.ap, AF.Ln, [sp], [sp], bias=1.0)
            P.op("dve", lambda e, o=cum.ap, m=rmask.ap, s=sp.ap:
                 e.tensor_tensor_scan(out=o, data0=m, data1=s, initial=0.0, op0=ALU.mult, op1=ALU.add),
                 [rmask, sp], [cum])
            self.act(eb.ap, cum.ap[:, TOK:WIN], AF.Exp, [cum], [eb], scale=-1.0 / 16)
            self.act(enb.ap, cum.ap, AF.Exp, [cum], [enb], scale=1.0 / 16)
            self.act(dlast.ap, cum.ap[:, 127:WIN:128], AF.Exp, [cum], [dlast], scale=-1.0 / 16)
            self.stt(qdec.ap, qT.ap, 128.0 ** -0.5, eb.ap, ALU.mult, ALU.mult, [qT, eb], [qdec])
            self.tt("dve", kin.ap, kT.ap, enb.ap, ALU.mult, [kT, enb], [kin])
            self.tt("pool", kst.ap.rearrange("p (n s) -> p n s", n=16), kin.ap.rearrange("p (n s) -> p n s", n=16),
                    dlast.ap.unsqueeze(2).to_broadcast([128, 16, 128]), ALU.mult, [kin, dlast], [kst])
            for n4 in range(4):
                ps = self.psr()
                pb = ps.ap.bitcast(BF16)
                for i in range(4):
                    n = n4 * 4 + i
                    self.tr(pb[:, i * 128:(i + 1) * 128], kst.ap[:, n * 128:(n + 1) * 128], self.ident,
                            [kst, self.cm], [ps])
                self.cp(self.evac_eng(), kstm.ap[:, n4 * 4:(n4 + 1) * 4, :],
                        pb[:, 0:512].rearrange("p (a b) -> p a b", a=4), [ps], [kstm])
            P.op("dve", lambda e, o=S.ap: e.memset(o, 0.0), [], [S])
            P.op("dve", lambda e, o=Sb.ap: e.memset(o, 0.0), [], [Sb])
            for n in range(16):
                if n >= 8:
                    c = n - 8
                    aps = self.psr()
                    self.mm(aps.ap[:, 0:128], kin.ap[:, n * 128:(n + 1) * 128], qdec.ap[:, c * 128:(c + 1) * 128],
                            True, True, [kin, qdec], [aps])
                    am = self.rot("attm", attm)
                    self.tt("dve", am.ap, aps.ap[:, 0:128], self.tri, ALU.mult, [aps, self.cm], [am])
                    ops_ = self.psr()
                    for half in range(2):
                        self.mm(ops_.ap[:, half * 128:(half + 1) * 128], vtm.ap[:, n, half * 128:(half + 1) * 128],
                                am.ap, True, False, [vtm, am], [ops_])
                        self.mm(ops_.ap[:, half * 128:(half + 1) * 128], Sb.ap[:, half * 128:(half + 1) * 128],
                                qdec.ap[:, c * 128:(c + 1) * 128], False, True, [Sb, qdec], [ops_])
                    self.cp("act", oTf.ap[:, :, c * 128:(c + 1) * 128],
                            ops_.ap[:, 0:256].rearrange("p (a b) -> p a b", a=2), [ops_], [oTf])
                if n < 15:
                    ups = self.psr()
                    self.mm(ups.ap[:, 0:256], kstm.ap[:, n, :], vtm.ap[:, n, :], True, True, [kstm, vtm], [ups])
                    self.stt(S.ap, S.ap, dlast.ap[:, n:n + 1], ups.ap[:, 0:256], ALU.mult, ALU.add,
                             [S, dlast, ups], [S])
                    self.cp("act", Sb.ap, S.ap, [S], [Sb])
            for tq in range(2):
                ssq = self.psr()
                for half in range(2):
                    s = self.rot("sqb", sqb)
                    self.act(s.ap, oTf.ap[:, half, tq * 512:(tq + 1) * 512], AF.Square, [oTf], [s])
                    self.mm(ssq.ap, self.ones, s.ap, half == 0, half == 1, [s, self.cm], [ssq])
                self.rstd_from_ssq(rstd, ssq, 512, 1.0 / 256)
                for half in range(2):
                    self.act(sg.ap, gT.ap[:, half, tq * 512:(tq + 1) * 512], AF.Silu, [gT], [sg])
                    self.stt(tmp.ap, oTf.ap[:, half, tq * 512:(tq + 1) * 512], self.gng[:, half:half + 1], rstd.ap,
                             ALU.mult, ALU.mult, [oTf, rstd, self.cv], [tmp])
                    self.tt("dve", self.oT.ap[:, 2 * h + half, tq * 512:(tq + 1) * 512], tmp.ap, sg.ap, ALU.mult,
                            [tmp, sg], [self.oT])

    def phase3(self):
        P = self.P
        self.P.barrier(self.scr.ap)
        A, B = self.A, self.B
        A.reset(); B.reset()
        sets = []
        for _ in range(2):
            d = dict(kT=A.take([128, WIN], BF16), qT=A.take([128, TOK], BF16), krf=A.take([128, WIN], F32),
                     qrf=A.take([128, TOK], F32), krb=A.take([128, WIN], BF16), qrb=A.take([128, TOK], BF16),
                     vm=A.take([128, 16, 128], BF16), kmean=B.take([128, 8], F32), gm=B.take([128, 64], F32),
                     mx8=B.take([128, 64], F32), sel=B.take([128, 64], F32), biasb=B.take([128, 64], BF16),
                     biasT=B.take([8, TOK], BF16))
            sets.append(d)
        cosT = B.take([128, WIN], F32)
        sinT = B.take([128, WIN], F32)
        mm_ = B.take([128, 192], F32)
        pT = [B.take([128, 512], BF16), B.take([128, 512], BF16), B.take([128, 512], BF16)]
        rden = B.take([128, 512], F32)
        band = B.take([128, 2048], BF16)
        nr_scr = [(B.take([128, 512], BF16), B.take([128, 512], F32), B.take([128, 512], BF16),
                   B.take([128, 512], F32), B.take([128, 512], F32)) for _ in range(2)]
        P.dma("pool", band.ap, self.band, writes=[band])
        P.dma("sp", cosT.ap, self.rope[:, 0:WIN], writes=[cosT])
        P.dma("sp", sinT.ap, self.rope[:, WIN:2 * WIN], writes=[sinT])
        P.dma("sp", mm_.ap, self.mmask, writes=[mm_])
        M1, M2, M3 = mm_.ap[:, 0:64], mm_.ap[:, 64:128], mm_.ap[:, 128:192]
        scale = 128.0 ** -0.5
        prep_ps = [self.ps[2], self.ps[3], self.ps[6], self.ps[7]]
        attn_ps = [self.ps[0], self.ps[1]]

        def normrope(scr_, jobs):
            s, rstd, kn, t1, t2 = scr_
            for (src, n0, gcol, tab0, outf, outb) in jobs:
                self.act(s.ap, src.ap[:, n0:n0 + 512], AF.Square, [src], [s])
                yield
                ssq = self.rot("prep_ps", prep_ps)
                self.mm(ssq.ap, self.ones, s.ap, True, True, [s, self.cm], [ssq])
                yield
                self.act(rstd.ap, ssq.ap, AF.Sqrt, [ssq, self.epst], [rstd], scale=1.0 / 128, bias=self.epsc)
                yield
                self.recip(rstd.ap, rstd.ap, [rstd], [rstd])
                yield
                self.stt(kn.ap, src.ap[:, n0:n0 + 512], gcol, rstd.ap, ALU.mult, ALU.mult, [src, rstd, self.cv], [kn])
                yield
                rps = self.rot("prep_ps", prep_ps)
                self.mm(rps.ap, self.RT, kn.ap, True, True, [kn, self.cm], [rps])
                yield
                self.tt("dve", t1.ap, rps.ap, sinT.ap[:, tab0:tab0 + 512], ALU.mult, [rps, sinT], [t1])
                yield
                self.tt("pool", t2.ap, kn.ap, cosT.ap[:, tab0:tab0 + 512], ALU.mult, [kn, cosT], [t2])
                yield
                self.tt("pool", outf.ap[:, n0:n0 + 512], t1.ap, t2.ap, ALU.add, [t1, t2], [outf])
                yield
                self.cp("act", outb.ap[:, n0:n0 + 512], outf.ap[:, n0:n0 + 512], [outf], [outb])
                yield

        def inter_gen(gens):
            gens = list(gens)
            while gens:
                for g in list(gens):
                    try:
                        next(g)
                        yield
                    except StopIteration:
                        gens.remove(g)

        def prep(h, d):
            kT, qT, krf, qrf, krb, qrb, vm = d["kT"], d["qT"], d["krf"], d["qrf"], d["krb"], d["qrb"], d["vm"]
            kmean, gm, mx8, sel, biasb, biasT = d["kmean"], d["gm"], d["mx8"], d["sel"], d["biasb"], d["biasT"]
            P.dma("sp", kT.ap, self.pT_d.ap[SEG["mk"] + h * 128:SEG["mk"] + (h + 1) * 128, :],
                  reads=[self.pT_d], writes=[kT])
            P.dma("sp", qT.ap, self.pT_d.ap[SEG["mq"] + h * 128:SEG["mq"] + (h + 1) * 128, TOK:WIN],
                  reads=[self.pT_d], writes=[qT])
            P.dma("sp", vm.ap, self.vtm_d.ap[:, 2048 + h * 128:2048 + (h + 1) * 128]
                  .rearrange("(n p) e -> p n e", p=128), reads=[self.vtm_d], writes=[vm])
            yield
            jobs = [(kT, tq * 512, self.kg, tq * 512, krf, krb) for tq in range(4)] + \
                   [(qT, tq * 512, self.qg, TOK + tq * 512, qrf, qrb) for tq in range(2)]
            for _ in inter_gen([normrope(nr_scr[i], jobs[i::2]) for i in range(2)]):
                yield
            P.op("dve", lambda e, o=kmean.ap, i=krf.ap.rearrange("p (n s) -> p n s", n=8):
                 e.tensor_reduce(out=o, in_=i, axis=AX.X, op=ALU.add), [krf], [kmean])
            yield
            gps = self.rot("prep_ps", prep_ps)
            for jj in range(8):
                self.mm(gps.ap[:, jj * 8:(jj + 1) * 8], qrf.ap[:, jj * 128:(jj + 1) * 128], kmean.ap, True, True,
                        [qrf, kmean], [gps])
            yield
            self.tt("dve", gm.ap, gps.ap[:, 0:64], M1, ALU.mult, [gps, mm_], [gm])
            yield
            self.tt("dve", gm.ap, gm.ap, M2, ALU.add, [gm, mm_], [gm])
            yield
            for jj in range(8):
                P.op("dve", lambda e, o=mx8.ap[:, jj * 8:(jj + 1) * 8], i=gm.ap[:, jj * 8:(jj + 1) * 8]:
                     e.max(out=o, in_=i), [gm], [mx8])
            yield
            self.tt("dve", sel.ap.rearrange("p (a b) -> p a b", a=8), gm.ap.rearrange("p (a b) -> p a b", a=8),
                    mx8.ap.rearrange("p (a b) -> p a b", a=8)[:, :, 3:4].to_broadcast([128, 8, 8]), ALU.is_ge,
                    [gm, mx8], [sel])
            yield
            self.ts("dve", sel.ap, sel.ap, -1.0, -NEG, ALU.add, ALU.mult, [sel], [sel])
            yield
            self.tt("dve", biasb.ap, sel.ap, M3, ALU.add, [sel, mm_], [biasb])
            yield
            bps = self.rot("prep_ps", prep_ps)
            bpb = bps.ap.bitcast(BF16)
            for jj in range(8):
                self.tr(bpb[0:8, jj * 128:(jj + 1) * 128], biasb.ap[:, jj * 8:(jj + 1) * 8], self.ident,
                        [biasb, self.cm], [bps])
            yield
            self.cp("act", biasT.ap, bpb[0:8, 0:TOK], [bps], [biasT])
            yield

        def attn(h, d):
            krb, qrb, vm, biasT = d["krb"], d["qrb"], d["vm"], d["biasT"]
            ops_, dps = self.ps[4], self.ps[5]
            for gq in range(2):
                qs = qrb.ap[:, gq * 512:(gq + 1) * 512]
                kt0 = 8 + gq * 4
                nkt = kt0 + 4

                def scores(kt, qs=qs, gq=gq):
                    sps = self.rot("attn_ps", attn_ps)
                    self.mm(sps.ap, krb.ap[:, kt * 128:(kt + 1) * 128], qs, True, False, [krb, qrb], [sps])
                    self.mm(sps.ap, self.indb.ap[:, (kt // 2) * 128:(kt // 2 + 1) * 128],
                            biasT.ap[:, gq * 512:(gq + 1) * 512], False, True, [self.indb, biasT], [sps])
                    return sps
                nxt = scores(0)
                yield
                for kt in range(nkt):
                    sps = nxt
                    if kt + 1 < nkt:
                        nxt = scores(kt + 1)
                        yield
                    p = self.rot("pT", pT)
                    self.act(p.ap, sps.ap, AF.Exp, [sps], [p], scale=scale)
                    yield
                    if kt >= kt0:
                        dd = kt - kt0
                        self.tt("dve", p.ap, p.ap, band.ap[:, dd * 512:(dd + 1) * 512], ALU.mult, [p, band], [p])
                        yield
                    self.mm(ops_.ap, vm.ap[:, kt, :], p.ap, kt == 0, kt == nkt - 1, [vm, p], [ops_])
                    self.mm(dps.ap, self.ones, p.ap, kt == 0, kt == nkt - 1, [self.cm, p], [dps])
                    yield
                self.recip(rden.ap, dps.ap, [dps], [rden])
                yield
                self.tt("dve", self.oT.ap[:, 16 + h, gq * 512:(gq + 1) * 512], ops_.ap, rden.ap, ALU.mult,
                        [ops_, rden], [self.oT])
                yield

        self.interleave([prep(0, sets[0])])
        for h in range(16):
            gens = [attn(h, sets[h % 2])]
            if h + 1 < 16:
                gens.append(prep(h + 1, sets[(h + 1) % 2]))
            self.interleave(gens)

    def phase4(self):
        P = self.P
        self.P.barrier(self.scr.ap)
        A, B = self.A, self.B
        A.reset(); B.reset()
        self.xn2 = A.take([128, 32, TOK], BF16)
        xn2 = self.xn2
        wbufs = [B.take([128, 32, 512], BF16), B.take([128, 32, 512], BF16)]
        if self.dbg:
            for c in range(32):
                P.dma("sp", self.oT_d.ap[c * 128:(c + 1) * 128, :], self.oT.ap[:, c, :], reads=[self.oT],
                      writes=[self.oT_d])
        sqb = [Tl(self.stage[2].ap.bitcast(BF16)[:, 0:512], self.stage[2].b),
               Tl(self.stage[3].ap.bitcast(BF16)[:, 0:512], self.stage[3].b)]
        x2f = [self.stage[0], self.stage[1]]
        pend = None
        for dg in range(8):
            wb = self.load_wgroup(wbufs, self.w_out, dg * 512, 512)
            for tq in range(2):
                for cc in range(4):
                    c = dg * 4 + cc
                    ps = self.psr(6)
                    for kc in range(32):
                        self.mm(ps.ap, wb.ap[:, kc, cc * 128:(cc + 1) * 128], self.oT.ap[:, kc, tq * 512:(tq + 1) * 512],
                                kc == 0, kc == 31, [wb, self.oT], [ps])
                    xc = self.rot("xc", self.xc)
                    P.dma("sp", xc.ap, self.xT[c * 128:(c + 1) * 128, TOK + tq * 512:TOK + (tq + 1) * 512], writes=[xc])
                    xf = self.rot("x2f", x2f)
                    self.tt("dve", xf.ap, ps.ap, xc.ap, ALU.add, [ps, xc], [xf])
                    P.dma("sp", self.x2T_d.ap[c * 128:(c + 1) * 128, tq * 512:(tq + 1) * 512], xf.ap, reads=[xf],
                          writes=[self.x2T_d])
                    self.cp("act", xn2.ap[:, c, tq * 512:(tq + 1) * 512], xf.ap, [xf], [xn2])
                    s = self.rot("sq4", sqb)
                    self.act(s.ap, xf.ap, AF.Square, [xf], [s])
                    if pend is not None:
                        pend()
                    pend = (lambda s=s, c=c, tq=tq: self.mm(self.ps[6 + tq].ap, self.ones, s.ap, c == 0, c == 31,
                                                            [s, self.cm], [self.ps[6 + tq]]))
        pend()
        rs = [self.stage[0], self.stage[1]]
        for tq in range(2):
            self.rstd_from_ssq(rs[tq], self.ps[6 + tq], 512, 1.0 / D)
        for c in range(32):
            for tq in range(2):
                v = xn2.ap[:, c, tq * 512:(tq + 1) * 512]
                self.stt(v, v, self.g2[:, c:c + 1], rs[tq].ap, ALU.mult, ALU.mult, [xn2, rs[tq], self.cv], [xn2])

    def phase5(self):
        P = self.P
        self.P.barrier(self.scr.ap)
        B, C = self.B, self.C
        B.reset(); C.reset()
        if not hasattr(self, "xn2"):
            self.A.reset()
            self.xn2 = self.A.take([128, 32, TOK], BF16)
            P.op("pool", lambda e, o=self.xn2.ap: e.memset(o, 0.5), [], [self.xn2])
        xn2 = self.xn2
        self.wb6 = [B.take([128, 32, 256], BF16), B.take([128, 32, 256], BF16)]
        wbufs = self.wb6
        self.thr = B.take([128, 64], F32)
        self.b_mark = B.off
        qpT = B.take([128, 16, 512], BF16)
        self.a_arr = C.take([128, 8, 8, 128], F32)
        self.b_arr = C.take([128, 8, 8, 128], F32)
        a_arr, b_arr = self.a_arr, self.b_arr

        def sb(name, shape, dt):
            return B.take(shape, dt)
        KT = sb("KT", [128, 2048], BF16)
        NCH = 2
        scr5 = []
        for i in range(NCH):
            scr5.append((sb("T12", [128, 32], F32), sb("tmp5", [128, 256], F32), sb("cand", [128, 256], F32),
                         sb("c24", [128, 24], F32), sb("e16", [128, 16], F32), sb("sc5", [128, 8], F32)))
        for i in range(4):
            P.dma("pool", KT.ap[:, i * 512:(i + 1) * 512], self.keysT[:, i * 512:(i + 1) * 512], writes=[KT])
        for tt_ in range(8):
            if tt_ % 4 == 0:
                tq = tt_ // 4
                for g in range(8):
                    wb = self.load_wgroup(wbufs, self.w_pq, g * 256, 256)
                    for cc in range(2):
                        ps = self.psr(6)
                        for kc in range(32):
                            self.mm(ps.ap, wb.ap[:, kc, cc * 128:(cc + 1) * 128],
                                    xn2.ap[:, kc, tq * 512:(tq + 1) * 512], kc == 0, kc == 31, [wb, xn2], [ps])
                        self.cp("act", qpT.ap[:, g * 2 + cc, :], ps.ap, [ps], [qpT])
            t0 = (tt_ % 4) * 128
            for h4 in range(4):
                ps = self.psr(6)
                for i in range(4):
                    hp = h4 * 4 + i
                    self.mm(ps.ap[:, i * 128:(i + 1) * 128], qpT.ap[:, hp, t0:t0 + 128],
                            KT.ap[:, hp * 128:(hp + 1) * 128], True, True, [qpT, KT], [ps])
                for p_, arr in ((0, a_arr), (1, b_arr)):
                    self.cp("act", arr.ap[:, tt_, h4 * 2:h4 * 2 + 2, :],
                            ps.ap.rearrange("p (h two k) -> p h two k", two=2, k=128)[:, :, p_, :], [ps], [arr])
        for tt_ in range(8):
            def chain(h, T12, tmp, cand, c24, e16, sc, tt_=tt_):
                for p_, arr in ((0, a_arr), (1, b_arr)):
                    s_ = arr.ap[:, tt_, h, :]
                    o = p_ * 16
                    P.op("dve", lambda e, o_=T12.ap[:, o:o + 8], i=s_: e.max(out=o_, in_=i), [arr], [T12])
                    yield
                    P.op("dve", lambda e, o_=tmp.ap[:, 0:128], r=T12.ap[:, o:o + 8], i=s_:
                         e.match_replace(out=o_, in_to_replace=r, in_values=i, imm_value=-BIG), [T12, arr], [tmp])
                    yield
                    P.op("dve", lambda e, o_=T12.ap[:, o + 8:o + 16], i=tmp.ap[:, 0:128]: e.max(out=o_, in_=i),
                         [tmp], [T12])
                    yield
                self.tt("dve", cand.ap.rearrange("p (a b) -> p a b", a=16),
                        T12.ap[:, 0:16].unsqueeze(2).to_broadcast([128, 16, 16]),
                        T12.ap[:, 16:32].unsqueeze(1).to_broadcast([128, 16, 16]), ALU.add, [T12], [cand])
                yield
                P.op("dve", lambda e, o_=c24.ap[:, 0:8], i=cand.ap: e.max(out=o_, in_=i), [cand], [c24])
                yield
                P.op("dve", lambda e, o_=tmp.ap, r=c24.ap[:, 0:8], i=cand.ap:
                     e.match_replace(out=o_, in_to_replace=r, in_values=i, imm_value=-BIG), [c24, cand], [tmp])
                yield
                P.op("dve", lambda e, o_=c24.ap[:, 8:16], i=tmp.ap: e.max(out=o_, in_=i), [tmp], [c24])
                yield
                P.op("dve", lambda e, o_=cand.ap, r=c24.ap[:, 8:16], i=tmp.ap:
                     e.match_replace(out=o_, in_to_replace=r, in_values=i, imm_value=-BIG), [c24, tmp], [cand])
                yield
                P.op("dve", lambda e, o_=c24.ap[:, 16:24], i=cand.ap: e.max(out=o_, in_=i), [cand], [c24])
                yield
                self.ts("dve", sc.ap[:, 0:1], c24.ap[:, 0:1], -1.0, None, ALU.mult, None, [c24], [sc])
                yield
                self.act(e16.ap, c24.ap[:, 0:16], AF.Exp, [c24, sc], [e16, sc], bias=sc.ap[:, 0:1],
                         accum=sc.ap[:, 1:2])
                yield
                self.act(sc.ap[:, 2:3], sc.ap[:, 1:2], AF.Ln, [sc], [sc])
                yield
                self.tt("dve", sc.ap[:, 3:4], c24.ap[:, 0:1], sc.ap[:, 2:3], ALU.add, [c24, sc], [sc])
                yield
                self.tt("dve", sc.ap[:, 4:5], c24.ap[:, 15:16], c24.ap[:, 16:17], ALU.add, [c24], [sc])
                yield
                self.stt(self.thr.ap[:, tt_ * 8 + h:tt_ * 8 + h + 1], sc.ap[:, 4:5], 0.5, sc.ap[:, 3:4],
                         ALU.mult, ALU.subtract, [sc], [self.thr])
                yield
                self.ts("dve", a_arr.ap[:, tt_, h, :], a_arr.ap[:, tt_, h, :], sc.ap[:, 3:4], None, ALU.subtract, None,
                        [a_arr, sc], [a_arr])
                yield
            for h0 in range(0, 8, NCH):
                self.interleave([chain(h0 + i, *scr5[i]) for i in range(NCH)])

    def phase6(self):
        P = self.P
        nc = self.nc
        xn2, a_arr, b_arr, thr = self.xn2, self.a_arr, self.b_arr, self.thr

        def sb(name, shape, dt):
            return self.B.take(shape, dt)
        self.P.barrier(self.scr.ap)
        self.B.off = self.b_mark
        c8 = [sb("c8_%d" % i, [128, 8], F32) for i in range(2)]
        E = [sb("E%d" % i, [128, 128], F32) for i in range(6)]
        Gm = [sb("Gm%d" % i, [128, 128], BF16) for i in range(6)]
        gl = [sb("gl%d" % i, [128, 512], F32) for i in range(2)]
        Gs = [sb("Gs%d" % i, [128, 512], BF16) for i in range(2)]
        PH0 = 5
        ea3 = sb("ea3", [128, 8, 8 - PH0, 128], F32)
        EB3 = sb("EB3", [128, 8, 8 - PH0, 128], BF16)
        Ep = [sb("Ep%d" % i, [128, 128], BF16) for i in range(4)]
        for tt_ in range(8):
            self.act(ea3.ap[:, tt_], a_arr.ap[:, tt_, PH0:8, :], AF.Exp, [a_arr], [ea3])
            self.act(EB3.ap[:, tt_], b_arr.ap[:, tt_, PH0:8, :], AF.Exp, [b_arr], [EB3])
        wst = [Tl(self.stage[i].ap.bitcast(BF16)[:, 0:512], self.stage[i].b) for i in range(4)]
        wbufs = self.wb6
        wb = None
        for i1 in range(128):
            if i1 % 2 == 0:
                wb = self.load_wgroup(wbufs, self.uT, i1 * 128, 256)
            for tq in range(2):
                hps = self.psr(4)
                kc = 0
                gacc = self.rot("gps", self.ps[4:8])
                for t4 in range(4):
                    tt_ = tq * 4 + t4
                    c = self.rot("c8", c8)
                    self.tt("pool", c.ap, thr.ap[:, tt_ * 8:(tt_ + 1) * 8], a_arr.ap[:, tt_, :, i1], ALU.subtract,
                            [thr, a_arr], [c])
                    for h in range(8):
                        if h >= PH0:
                            e_ = self.rot("Ep", Ep)
                            self.ts("pool", e_.ap, EB3.ap[:, tt_, h - PH0, :], ea3.ap[:, tt_, h - PH0, i1:i1 + 1], 1.0,
                                    ALU.mult, ALU.mult, [EB3, ea3], [e_])
                        else:
                            e_ = self.rot("E", E)
                            self.act(e_.ap, b_arr.ap[:, tt_, h, :], AF.Exp, [b_arr, a_arr], [e_],
                                     bias=a_arr.ap[:, tt_, h, i1:i1 + 1])
                        g_ = self.rot("Gm", Gm)
                        self.stt(g_.ap, b_arr.ap[:, tt_, h, :], c.ap[:, h:h + 1], e_.ap, ALU.is_ge, ALU.mult,
                                 [b_arr, c, e_], [g_])
                        self.mm(gacc.ap[:, t4 * 128:(t4 + 1) * 128], self.ident, g_.ap, h == 0, h == 7,
                                [g_, self.cm], [gacc])
                        self.mm(hps.ap, wb.ap[:, kc, (i1 % 2) * 128:(i1 % 2 + 1) * 128],
                                xn2.ap[:, kc, tq * 512:(tq + 1) * 512], kc == 0, kc == 31, [wb, xn2], [hps])
                        kc += 1
                gs_ = self.rot("Gs", Gs)
                self.cp("act", gs_.ap, gacc.ap, [gacc], [gs_])
                gps = self.rot("gps", self.ps[4:8])
                gpb = gps.ap.bitcast(BF16)
                for t4 in range(4):
                    self.tr(gpb[:, t4 * 128:(t4 + 1) * 128], gs_.ap[:, t4 * 128:(t4 + 1) * 128], self.ident,
                            [gs_, self.cm], [gps])
                g = self.rot("gl", gl)
                self.act(g.ap, hps.ap, AF.Gelu, [hps], [g])
                w = self.rot("wst", wst)
                self.tt("dve", w.ap, g.ap, gpb[:, 0:512], ALU.mult, [g, gps], [w])
                P.dma("sp", self.wT_d.ap[i1, :, tq * 512:(tq + 1) * 512], w.ap, reads=[w], writes=[self.wT_d])
        self.P.barrier(self.scr.ap)
        B = self.B
        B.reset()
        wt = [B.take([128, TOK], BF16) for _ in range(4)]
        vb = [B.take([128, 512], BF16) for _ in range(8)]
        self.A.reset(); self.C.reset()
        res = [self.A.take([128, TOK], BF16) for _ in range(32)] + [self.C.take([128, TOK], BF16) for _ in range(32)]
        while True:
            try:
                res.append(B.take([128, TOK], BF16))
            except AssertionError:
                break
        NRES = len(res)
        for dg in range(8):
            for i1 in range(128):
                if i1 < NRES:
                    w = res[i1]
                    if dg == 0:
                        P.dma("sp", w.ap, self.wT_d.ap[i1], reads=[self.wT_d], writes=[w])
                else:
                    w = self.rot("wt", wt)
                    P.dma("sp", w.ap, self.wT_d.ap[i1], reads=[self.wT_d], writes=[w])
                v = self.rot("vb", vb)
                P.dma("pool", v.ap, self.pv[i1 * 128:(i1 + 1) * 128, dg * 512:(dg + 1) * 512], writes=[v])
                for cc in range(4):
                    for tq in range(2):
                        ps = self.ps[cc * 2 + tq]
                        self.mm(ps.ap, v.ap[:, cc * 128:(cc + 1) * 128], w.ap[:, tq * 512:(tq + 1) * 512],
                                i1 == 0, i1 == 127, [v, w], [ps])
            for cc in range(4):
                for tq in range(2):
                    c = dg * 4 + cc
                    ps = self.ps[cc * 2 + tq]
                    xc = self.rot("xc", self.xc)
                    P.dma("sp", xc.ap, self.x2T_d.ap[c * 128:(c + 1) * 128, tq * 512:(tq + 1) * 512],
                          reads=[self.x2T_d], writes=[xc])
                    o = self.rot("o6", self.stage)
                    self.tt("dve", o.ap, ps.ap, xc.ap, ALU.add, [ps, xc], [o])
                    P.dma("sp", self.outT.ap[c * 128:(c + 1) * 128, tq * 512:(tq + 1) * 512], o.ap, reads=[o],
                          writes=[self.outT])

    def build(self):
        self.setup()
        phases = [self.phase1, self.phase2, self.phase3, self.phase4, self.phase5, self.phase6]
        for i, ph in enumerate(phases):
            if i + 1 > self.stop_after:
                break
            if self.only is not None and (i + 1) not in self.only:
                continue
            ph()
        if self.stop_after < 6:
            z = self.stage[0]
            self.P.op("dve", lambda e, o=z.ap: e.memset(o, 0.0), [], [z])
            self.P.dma("sp", self.outT.ap[0:128, 0:512], z.ap, reads=[z], writes=[self.outT])
        self.stats = self.P.emit()
        return self.nc


def _consts():
    ident = np.eye(128, dtype=np.float32)
    tri = (np.arange(128)[:, None] <= np.arange(128)[None, :]).astype(np.float32)
    RT = np.zeros((128, 128), np.float32)
    for d in range(16):
        RT[d + 16, d] = -1.0
        RT[d, d + 16] = 1.0
    ones = np.ones((128, 128), np.float32)
    cmat = np.concatenate([ident, tri, RT, ones], axis=1)
    ind = np.zeros((8, 8, 128), np.float32)
    for n in range(8):
        ind[n, n, :] = 1.0
    k = np.arange(128)[:, None]
    q = np.arange(512)[None, :]
    band = np.concatenate([((q - k) >= 128 * d).astype(np.float32) for d in range(4)], axis=1)
    return np.ascontiguousarray(cmat), np.ascontiguousarray(ind.reshape(8, 1024)), np.ascontiguousarray(band)


def _rope_tables(s):
    half = 16
    inv = (np.float32(500000.0) ** (-np.arange(half, dtype=np.float32) / np.float32(half))).astype(np.float32)
    pos = (np.arange(WIN) + (s - 1) * TOK).astype(np.float32)
    ang = (pos[:, None] * inv[None, :]).astype(np.float32)
    cos = np.cos(ang).astype(np.float32).T
    sin = np.sin(ang).astype(np.float32).T
    cosT = np.ones((128, WIN), np.float32)
    sinT = np.zeros((128, WIN), np.float32)
    cosT[0:16] = cos
    cosT[16:32] = cos
    sinT[0:16] = sin
    sinT[16:32] = sin
    return np.ascontiguousarray(np.concatenate([cosT, sinT], axis=1))


def _moba_masks(s):
    M1 = np.zeros((8, 8), np.float32)
    M2 = np.full((8, 8), -BIG, np.float32)
    M3 = np.full((8, 8), NEG, np.float32)
    nmin = 4 * (1 - s)
    for jj in range(8):
        r = (8 + jj) // 2
        for n in range(8):
            if nmin <= n < r:
                M1[jj, n] = 1.0
                M2[jj, n] = 0.0
                M3[jj, n] = 0.0
            elif n == r:
                M2[jj, n] = BIG
                M3[jj, n] = 0.0
    m = np.concatenate([M1.reshape(-1), M2.reshape(-1), M3.reshape(-1)])
    return np.ascontiguousarray(np.broadcast_to(m[None, :], (128, 192))).astype(np.float32)


def make_in_maps(x, norm_mix_g, w_in, w_gate_up, b_gate, gla_norm_g, q_norm_g, k_norm_g,
                 w_out, norm_ffn_g, peer_wq, peer_keys, peer_u, peer_v):
    f = np.float32
    x = np.asarray(x, f)
    cmat, ind, band = _consts()
    cvec = np.concatenate([
        np.asarray(norm_mix_g, f)[0].reshape(32, 128).T,
        np.asarray(norm_ffn_g, f)[0].reshape(32, 128).T,
        np.asarray(b_gate, f)[0].reshape(8, 128).T,
        np.asarray(gla_norm_g, f)[0].reshape(2, 128).T,
        np.asarray(q_norm_g, f)[0].reshape(1, 128).T,
        np.asarray(k_norm_g, f)[0].reshape(1, 128).T], axis=1)
    cvec = np.ascontiguousarray(cvec)
    keysT = np.ascontiguousarray(np.asarray(peer_keys, f)[0].transpose(3, 0, 1, 2).reshape(128, 2048))
    uT = np.ascontiguousarray(np.asarray(peer_u, f)[0].T)
    shared = dict(w_in=np.ascontiguousarray(np.asarray(w_in, f)[0]),
                  w_out=np.ascontiguousarray(np.asarray(w_out, f)[0]),
                  w_pq=np.ascontiguousarray(np.asarray(peer_wq, f)[0]),
                  uT=uT, pv=np.ascontiguousarray(np.asarray(peer_v, f)[0]),
                  keysT=keysT, wgu=np.ascontiguousarray(np.asarray(w_gate_up, f)[0]),
                  cvec=cvec, cmat=cmat, ind=ind, band=band)
    ropes = [_rope_tables(0), _rope_tables(1)]
    masks = [_moba_masks(0), _moba_masks(1)]
    maps = []
    for c in range(8):
        b, s = c // 2, c % 2
        xT = np.zeros((D, WIN), f)
        if s == 0:
            xT[:, TOK:] = x[b, 0:TOK].T
        else:
            xT[:, :] = x[b].T
        m = dict(shared)
        m.update(xT=xT, rope=ropes[s], mmask=masks[s])
        maps.append(m)
    return maps


def kernel(x, norm_mix_g, w_in, w_gate_up, b_gate, gla_norm_g, q_norm_g, k_norm_g,
           w_out, norm_ffn_g, peer_wq, peer_keys, peer_u, peer_v):
    maps = make_in_maps(x, norm_mix_g, w_in, w_gate_up, b_gate, gla_norm_g, q_norm_g, k_norm_g,
                        w_out, norm_ffn_g, peer_wq, peer_keys, peer_u, peer_v)
    bld = Builder()
    nc = bld.build()
    res = run_bass_kernel_spmd(nc, maps, core_ids=list(range(8)))
    out = np.zeros((4, 2048, D), np.float32)
    for c in range(8):
        b, s = c // 2, c % 2
        out[b, s * TOK:(s + 1) * TOK, :] = np.asarray(res.results[c]["outT"]).T
    return out
```

```python
import numpy as np
import concourse.bass as bass
import concourse.mybir as mybir
from concourse.bass_utils import run_bass_kernel_spmd

F32 = mybir.dt.float32
BF16 = mybir.dt.bfloat16
AF = mybir.ActivationFunctionType
ALU = mybir.AluOpType
AX = mybir.AxisListType

D = 4096
TOK = 1024
WIN = 2048
NEG = -32768.0
BIG = 1.0e30
EPS = 1e-6


class Buf:
    __slots__ = ("w", "rs", "x")

    def __init__(self):
        self.w = None
        self.rs = []
        self.x = False


class Tl:
    __slots__ = ("ap", "b")

    def __init__(self, ap, b=None):
        self.ap = ap
        self.b = b if b is not None else Buf()

    def __getitem__(self, k):
        return self.ap[k]


class Op:
    __slots__ = ("eng", "fn", "waits", "signal", "is_dma", "sem", "val")


def _bufs(xs):
    return [x.b if isinstance(x, Tl) else x for x in xs]


class Prog:
    ENGS = ("pe", "act", "dve", "pool", "sp")

    def __init__(self, nc, n_dma_sems=40):
        self.nc = nc
        self.ops = []
        self.n_dma_sems = n_dma_sems
        self.G = Buf()
        self.eng_obj = {"pe": nc.tensor, "act": nc.scalar, "dve": nc.vector,
                        "pool": nc.gpsimd, "sp": nc.sync}

    def op(self, eng, fn, reads=(), writes=(), dma=False, barrier=False):
        o = Op()
        o.eng, o.fn, o.is_dma, o.signal, o.sem, o.val = eng, fn, dma, False, None, None
        reads = _bufs(reads)
        writes = _bufs(writes)
        xr = [b for b in reads if b.x and b not in writes]
        if barrier:
            writes = writes + [self.G]
        else:
            reads = reads + [self.G]
        deps = {}
        oid = len(self.ops)
        for b in reads:
            if b.w is not None:
                deps[b.w] = True
        for b in writes:
            if b.w is not None:
                deps.setdefault(b.w, False)
            for r in b.rs:
                deps.setdefault(r, False)
        for b in xr:
            for r in b.rs:
                deps.setdefault(r, False)
        o.waits = deps
        self.ops.append(o)
        for b in writes:
            b.w = oid
            b.rs = []
        for b in reads:
            if b not in writes:
                b.rs.append(oid)
        return oid

    def dma(self, queue, out, in_, reads=(), writes=()):
        return self.op(queue, lambda e, o=out, i=in_: e.dma_start(out=o, in_=i), reads, writes, dma=True)

    def barrier(self, scratch):
        self.op("pool", lambda e, s=scratch: e.memset(s, 0.0), barrier=True)

    def emit(self):
        nc, ops = self.nc, self.ops
        eng_seq = {e: 0 for e in self.ENGS}
        nxt = 0
        nxt_sw = 0
        n_hw = self.n_dma_sems - 12
        slot_cnt = [0] * self.n_dma_sems
        slot_last = [None] * self.n_dma_sems
        key = [None] * len(ops)
        for i, o in enumerate(ops):
            if o.is_dma:
                if o.eng == "pool":
                    k = n_hw + nxt_sw
                    nxt_sw = (nxt_sw + 1) % (self.n_dma_sems - n_hw)
                else:
                    k = nxt
                    nxt = (nxt + 1) % n_hw
                slot_cnt[k] += 1
                key[i] = (("dma", k), slot_cnt[k])
                if slot_last[k] is not None:
                    o.waits[slot_last[k]] = True
                slot_last[k] = i
            else:
                eng_seq[o.eng] += 1
                key[i] = (o.eng, eng_seq[o.eng])
        seen = {e: {} for e in self.ENGS}
        final_waits = [None] * len(ops)
        for i, o in enumerate(ops):
            fw = {}
            sn = seen[o.eng]
            for p, raw in o.waits.items():
                po = ops[p]
                sk, order = key[p]
                if (not po.is_dma) and po.eng == o.eng:
                    if o.eng in ("pe", "sp") or not raw:
                        continue
                if sn.get(sk, 0) >= order:
                    continue
                if fw.get(sk, (0, None))[0] < order:
                    fw[sk] = (order, p)
            for sk, (order, p) in fw.items():
                sn[sk] = order
                ops[p].signal = True
            final_waits[i] = [p for (_, p) in fw.values()]
        LIMIT = 1500
        sems, cnt, nsem = {}, {}, [0]

        def fresh(sk):
            sems[sk] = nc.alloc_semaphore(name="s%d" % nsem[0])
            nsem[0] += 1
            cnt[sk] = 0
        for e in ("pe", "act", "dve", "pool"):
            fresh(e)
        for k in range(self.n_dma_sems):
            fresh(("dma", k))
        maxc = 0
        for i, o in enumerate(ops):
            e = self.eng_obj[o.eng]
            for p in final_waits[i]:
                e.wait_ge(ops[p].sem, ops[p].val)
            inst = o.fn(e)
            if o.is_dma or o.signal:
                sk = key[i][0] if o.is_dma else o.eng
                inc = 16 if o.is_dma else 1
                if cnt[sk] + inc > LIMIT:
                    fresh(sk)
                cnt[sk] += inc
                maxc = max(maxc, cnt[sk])
                o.sem, o.val = sems[sk], cnt[sk]
                inst.then_inc(sems[sk], inc)
        last = {}
        for i, o in enumerate(ops):
            if o.is_dma:
                last[key[i][0]] = o
        for sk, o in last.items():
            nc.sync.wait_ge(o.sem, o.val)
        return dict(n_ops=len(ops), max_cnt=maxc, n_sems=nsem[0])


class Arena:
    def __init__(self, ap_bf16):
        self.ap = ap_bf16
        self.n = ap_bf16.shape[1]
        self.off = 0

    def reset(self):
        self.off = 0

    def take(self, shape, dt):
        free = int(np.prod(shape[1:]))
        nb = free * (2 if dt == BF16 else 4)
        nb = (nb + 63) // 64 * 64
        assert self.off + nb // 2 <= self.n, ("arena overflow", shape, self.off, self.n)
        v = self.ap[0:shape[0], self.off:self.off + nb // 2]
        self.off += nb // 2
        if dt != BF16:
            v = v.bitcast(dt)
        v = v[:, 0:free]
        if len(shape) == 3:
            v = v.rearrange("p (a b) -> p a b", a=shape[1])
        elif len(shape) == 4:
            v = v.rearrange("p (a b c) -> p a b c", a=shape[1], b=shape[2])
        return Tl(v)


SEG = dict(gq=0, gk=1024, gv=2048, gg=4096, glr=6144, mq=6160, mk=8208, mv=10256)
IN_COLS = 12304


class Builder:
    def __init__(self, stop_after=99, dbg=False, only=None, small=()):
        self.stop_after = stop_after
        self.dbg = dbg
        self.only = only
        self.small = set(small)
        nc = bass.Bass("TRN2", target_bir_lowering=False)
        self.nc = nc
        self.P = Prog(nc)
        self._rot = {}

        def din(name, shape, dt=F32):
            if name in self.small:
                shape = [128, 128]
            return nc.dram_tensor(name, list(shape), dt, kind="ExternalInput").ap()

        self.xT = din("xT", [D, WIN])
        self.w_in = din("w_in", [D, IN_COLS])
        self.w_out = din("w_out", [D, D])
        self.w_pq = din("w_pq", [D, 2048])
        self.uT = din("uT", [D, 16384])
        self.pv = din("pv", [16384, D])
        self.keysT = din("keysT", [128, 16 * 128])
        self.wgu = din("wgu", [16, 1024])
        self.cvec = din("cvec", [128, 32 + 32 + 8 + 2 + 1 + 1])
        self.rope = din("rope", [128, 2 * WIN])
        self.mmask = din("mmask", [128, 3 * 64])
        self.cmat = din("cmat", [128, 4 * 128])
        self.ind = din("ind", [8, 8 * 128])
        self.band = din("band", [128, 2048])
        kind = "ExternalOutput" if dbg else "Internal"
        self.pT_d = Tl(nc.dram_tensor("pT_d", [IN_COLS, WIN], BF16, kind=kind).ap())
        self.vtm_d = Tl(nc.dram_tensor("vtm_d", [WIN, 4096], BF16, kind=kind).ap())
        self.x2T_d = Tl(nc.dram_tensor("x2T_d", [D, TOK], F32, kind=kind).ap())
        self.oT_d = Tl(nc.dram_tensor("oT_d", [D, TOK], BF16, kind=kind).ap()) if dbg else None
        self.wT_d = Tl(nc.dram_tensor("wT_d", [128, 128, TOK], BF16, kind="Internal").ap())
        self.outT = Tl(nc.dram_tensor("outT", [D, TOK], F32, kind="ExternalOutput").ap())

        def sb(name, shape, dt):
            return Tl(nc.alloc_sbuf_tensor(name, list(shape), dt).ap())

        self.A = Arena(nc.alloc_sbuf_tensor("arenaA", [128, 32768], BF16).ap())
        self.B = Arena(nc.alloc_sbuf_tensor("arenaB", [128, 32768], BF16).ap())
        self.C = Arena(nc.alloc_sbuf_tensor("arenaC", [128, 32768], BF16).ap())
        self.cm = sb("cm", [128, 512], BF16)
        self.ident = self.cm.ap[:, 0:128]
        self.tri = self.cm.ap[:, 128:256]
        self.RT = self.cm.ap[:, 256:384]
        self.ones = self.cm.ap[:, 384:512]
        self.cv = sb("cv", [128, 76], F32)
        self.nb = sb("nb", [128, 8], F32)
        self.indb = sb("indb", [8, 1024], BF16)
        self.scr = sb("scr", [128, 8], F32)
        self.stage = [sb("stage%d" % i, [128, 512], F32) for i in range(4)]
        self.xc = [sb("xc%d" % i, [128, 512], F32) for i in range(2)]
        self.cm_f = self.stage[0]
        self.ps = [Tl(nc.alloc_psum_tensor("ps%d" % i, [128, 512], F32).ap()) for i in range(8)]
        for p_ in self.ps:
            p_.b.x = True

    def rot(self, name, lst):
        i = self._rot.get(name, 0)
        self._rot[name] = i + 1
        return lst[i % len(lst)]

    def psr(self, n=4):
        return self.rot("ps%d" % n, self.ps[0:n])

    def mm(self, out, lhsT, rhs, start, stop, R, W):
        self.P.op("pe", lambda e, o=out, l=lhsT, r=rhs, s=start, t=stop:
                  e.matmul(o, lhsT=l, rhs=r, start=s, stop=t), R, W)

    def tr(self, out, in_, ident, R, W):
        self.P.op("pe", lambda e, o=out, i=in_, d=ident: e.transpose(o, i, d), R, W)

    def act(self, out, in_, func, R, W, scale=None, bias=None, accum=None):
        kw = {}
        if scale is not None:
            kw["scale"] = scale
        if bias is not None:
            kw["bias"] = bias
        if accum is not None:
            kw["accum_out"] = accum
        self.P.op("act", lambda e, o=out, i=in_, f=func, k=kw: e.activation(out=o, in_=i, func=f, **k), R, W)

    def tt(self, eng, out, in0, in1, op, R, W):
        self.P.op(eng, lambda e, o=out, a=in0, b=in1, p=op: e.tensor_tensor(out=o, in0=a, in1=b, op=p), R, W)

    def stt(self, out, in0, scalar, in1, op0, op1, R, W):
        self.P.op("dve", lambda e, o=out, a=in0, s=scalar, b=in1, p0=op0, p1=op1:
                  e.scalar_tensor_tensor(out=o, in0=a, scalar=s, in1=b, op0=p0, op1=p1), R, W)

    def ts(self, eng, out, in0, s1, s2, op0, op1, R, W):
        if op1 is None:
            self.P.op(eng, lambda e, o=out, a=in0, x=s1, p0=op0:
                      e.tensor_scalar(out=o, in0=a, scalar1=x, scalar2=None, op0=p0), R, W)
        else:
            self.P.op(eng, lambda e, o=out, a=in0, x=s1, y=s2, p0=op0, p1=op1:
                      e.tensor_scalar(out=o, in0=a, scalar1=x, scalar2=y, op0=p0, op1=p1), R, W)

    def cp(self, eng, out, in_, R, W):
        if eng == "act":
            self.P.op("act", lambda e, o=out, i=in_: e.copy(out=o, in_=i), R, W)
        else:
            self.P.op(eng, lambda e, o=out, i=in_: e.tensor_copy(out=o, in_=i), R, W)

    @staticmethod
    def interleave(gens):
        gens = list(gens)
        while gens:
            for g in list(gens):
                try:
                    next(g)
                except StopIteration:
                    gens.remove(g)

    def evac_eng(self):
        return self.rot("evac", ["act", "dve"])

    def recip(self, out, in_, R, W):
        self.P.op("dve", lambda e, o=out, i=in_: e.reciprocal(out=o, in_=i), R, W)

    def rstd_from_ssq(self, out_t, ssq_ps, n, inv_n):
        self.act(out_t.ap[:, 0:n], ssq_ps.ap[:, 0:n], AF.Sqrt, [ssq_ps, self.epst], [out_t], scale=inv_n, bias=self.epsc)
        self.recip(out_t.ap[:, 0:n], out_t.ap[:, 0:n], [out_t], [out_t])

    def setup(self):
        P = self.P
        P.dma("sp", self.cm_f.ap, self.cmat, writes=[self.cm_f])
        self.cp("dve", self.cm.ap, self.cm_f.ap, [self.cm_f], [self.cm])
        P.dma("sp", self.cv.ap, self.cvec, writes=[self.cv])
        self.ts("dve", self.nb.ap, self.cv.ap[:, 64:72], -1.0, None, ALU.mult, None, [self.cv], [self.nb])
        P.dma("pool", self.indb.ap, self.ind, writes=[self.indb])
        self.epst = Tl(self.nc.alloc_sbuf_tensor("epst", [128, 1], F32).ap())
        P.op("dve", lambda e, o=self.epst.ap: e.memset(o, EPS), [], [self.epst])
        self.epsc = self.epst.ap[:, 0:1]
        self.g1 = self.cv.ap[:, 0:32]
        self.g2 = self.cv.ap[:, 32:64]
        self.gng = self.cv.ap[:, 72:74]
        self.qg = self.cv.ap[:, 74:75]
        self.kg = self.cv.ap[:, 75:76]

    def load_wgroup(self, wbufs, wdram, c0, ncols):
        wb = self.rot("wg", wbufs)
        src = wdram[:, c0:c0 + ncols].rearrange("(c p) n -> p c n", p=128)
        half = 16
        self.P.dma("pool", wb.ap[:, 0:half, 0:ncols], src[:, 0:half, :], writes=[wb])
        self.P.dma("pool", wb.ap[:, half:32, 0:ncols], src[:, half:32, :], writes=[wb])
        return wb

    def phase1(self):
        P = self.P
        self.A.reset(); self.B.reset(); self.C.reset()
        hn = [self.A.take([128, 32, 512], BF16), self.A.take([128, 32, 512], BF16),
              self.B.take([128, 32, 512], BF16), self.B.take([128, 32, 512], BF16)]
        wbufs = [self.C.take([128, 32, 512], BF16), self.C.take([128, 32, 512], BF16)]
        sq = [Tl(self.stage[2].ap.bitcast(BF16)[:, 0:512], self.stage[2].b),
              Tl(self.stage[3].ap.bitcast(BF16)[:, 0:512], self.stage[3].b)]
        rstd = self.stage[0]
        for tq in (2, 3, 0, 1):
            ssq = self.ps[7]
            for kc in range(32):
                xc = self.rot("xc", self.xc)
                P.dma("sp", xc.ap, self.xT[kc * 128:(kc + 1) * 128, tq * 512:(tq + 1) * 512], writes=[xc])
                s = self.rot("sq", sq)
                self.act(s.ap, xc.ap, AF.Square, [xc], [s])
                self.mm(ssq.ap, self.ones, s.ap, kc == 0, kc == 31, [s, self.cm], [ssq])
            self.rstd_from_ssq(rstd, ssq, 512, 1.0 / D)
            for kc in range(32):
                xc = self.rot("xc", self.xc)
                P.dma("sp", xc.ap, self.xT[kc * 128:(kc + 1) * 128, tq * 512:(tq + 1) * 512], writes=[xc])
                self.stt(hn[tq].ap[:, kc, :], xc.ap, self.g1[:, kc:kc + 1], rstd.ap, ALU.mult, ALU.mult,
                         [xc, rstd, self.cv], [hn[tq]])
        stg = [Tl(self.stage[i].ap.bitcast(BF16)[:, 0:512], self.stage[i].b) for i in range(4)]
        groups = []
        for name, n, lay, allt in (("gq", 1024, "fm", False), ("gg", 2048, "fm", False), ("mq", 2048, "fm", False),
                                   ("gk", 1024, "fm", True), ("gv", 2048, "tm", True), ("glr", 16, "fm", True),
                                   ("mk", 2048, "fm", True), ("mv", 2048, "tm", True)):
            for g0 in range(0, n, 512):
                groups.append((name, SEG[name] + g0, min(512, n - g0), lay, allt, g0))
        for (name, c0, ncols, lay, allt, g0) in groups:
            wb = self.load_wgroup(wbufs, self.w_in, c0, ncols)
            tqs = range(4) if allt else range(2, 4)
            if lay == "fm":
                for tq in tqs:
                    for cc in range((ncols + 127) // 128):
                        m = min(128, ncols - cc * 128)
                        ps = self.psr(6)
                        for kc in range(32):
                            self.mm(ps.ap[0:m, :], wb.ap[:, kc, cc * 128:cc * 128 + m], hn[tq].ap[:, kc, :],
                                    kc == 0, kc == 31, [wb, hn[tq]], [ps])
                        st = self.rot("stg", stg)
                        self.cp(self.evac_eng(), st.ap[0:m, :], ps.ap[0:m, :], [ps], [st])
                        r0 = c0 + cc * 128
                        P.dma("sp", self.pT_d.ap[r0:r0 + m, tq * 512:(tq + 1) * 512], st.ap[0:m, :],
                              reads=[st], writes=[self.pT_d])
            else:
                vcol = (0 if name == "gv" else 2048) + g0
                for tq in tqs:
                    for t4 in range(4):
                        ps = self.psr(6)
                        for kc in range(32):
                            self.mm(ps.ap, hn[tq].ap[:, kc, t4 * 128:(t4 + 1) * 128], wb.ap[:, kc, :],
                                    kc == 0, kc == 31, [wb, hn[tq]], [ps])
                        st = self.rot("stg", stg)
                        self.cp(self.evac_eng(), st.ap, ps.ap, [ps], [st])
                        t0 = tq * 512 + t4 * 128
                        P.dma("sp", self.vtm_d.ap[t0:t0 + 128, vcol:vcol + 512], st.ap,
                              reads=[st], writes=[self.vtm_d])

    def phase2(self):
        P = self.P
        self.P.barrier(self.scr.ap)
        A, B, C = self.A, self.B, self.C
        A.reset(); B.reset(); C.reset()
        self.oT = C.take([128, 32, TOK], BF16)
        glrT = A.take([16, WIN], BF16)
        wgu = A.take([16, 1024], BF16)
        sp = A.take([128, WIN], F32)
        cum = A.take([128, WIN], F32)
        enb = A.take([128, WIN], F32)
        eb = A.take([128, TOK], F32)
        rmask = A.take([128, WIN], BF16)
        kT = A.take([128, WIN], BF16)
        qT = A.take([128, TOK], BF16)
        qdec = A.take([128, TOK], BF16)
        kin = A.take([128, WIN], BF16)
        kst = B.take([128, WIN], BF16)
        kstm = B.take([128, 16, 128], BF16)
        vtm = B.take([128, 16, 256], BF16)
        dlast = B.take([128, 16], F32)
        S = B.take([128, 256], F32)
        Sb = B.take([128, 256], BF16)
        attm = [B.take([128, 128], BF16), B.take([128, 128], BF16)]
        oTf = B.take([128, 2, TOK], F32)
        gT = B.take([128, 2, TOK], BF16)
        sqb = [B.take([128, 512], BF16), B.take([128, 512], BF16)]
        rstd = B.take([128, 512], F32)
        sg = B.take([128, 512], F32)
        tmp = B.take([128, 512], F32)
        P.dma("sp", glrT.ap, self.pT_d.ap[SEG["glr"]:SEG["glr"] + 16, :], reads=[self.pT_d], writes=[glrT])
        P.dma("pool", wgu.ap, self.wgu, writes=[wgu])
        P.op("pool", lambda e, o=rmask.ap: e.memset(o, 1.0), [], [rmask])
        P.op("pool", lambda e, o=rmask.ap[:, 0:WIN:128]: e.memset(o, 0.0), [], [rmask])
        for h in range(8):
            P.dma("sp", kT.ap, self.pT_d.ap[SEG["gk"] + h * 128:SEG["gk"] + (h + 1) * 128, :],
                  reads=[self.pT_d], writes=[kT])
            P.dma("sp", qT.ap, self.pT_d.ap[SEG["gq"] + h * 128:SEG["gq"] + (h + 1) * 128, TOK:WIN],
                  reads=[self.pT_d], writes=[qT])
            P.dma("sp", vtm.ap, self.vtm_d.ap[:, h * 256:(h + 1) * 256].rearrange("(n p) e -> p n e", p=128),
                  reads=[self.vtm_d], writes=[vtm])
            P.dma("sp", gT.ap, self.pT_d.ap[SEG["gg"] + h * 256:SEG["gg"] + (h + 1) * 256, TOK:WIN]
                  .rearrange("(c p) t -> p c t", p=128), reads=[self.pT_d], writes=[gT])
            for tq in range(4):
                ps = self.psr()
                self.mm(ps.ap, wgu.ap[:, h * 128:(h + 1) * 128], glrT.ap[:, tq * 512:(tq + 1) * 512], True, True,
                        [wgu, glrT], [ps])
                self.act(sp.ap[:, tq * 512:(tq + 1) * 512], ps.ap, AF.Exp, [ps, self.nb], [sp],
                         scale=-1.0, bias=self.nb.ap[:, h:h + 1])
            self.act(sp.ap, sp.ap, AF.Ln, [sp], [sp], bias=1.0)
            P.op("dve", lambda e, o=cum.ap, m=rmask.ap, s=sp.ap:
                 e.tensor_tensor_scan(out=o, data0=m, data1=s, initial=0.0, op0=ALU.mult, op1=ALU.add),
                 [rmask, sp], [cum])
            self.act(eb.ap, cum.ap[:, TOK:WIN], AF.Exp, [cum], [eb], scale=-1.0 / 16)
            self.act(enb.ap, cum.ap, AF.Exp, [cum], [enb], scale=1.0 / 16)
            self.act(dlast.ap, cum.ap[:, 127:WIN:128], AF.Exp, [cum], [dlast], scale=-1.0 / 16)
            self.stt(qdec.ap, qT.ap, 128.0 ** -0.5, eb.ap, ALU.mult, ALU.mult, [qT, eb], [qdec])
            self.tt("dve", kin.ap, kT.ap, enb.ap, ALU.mult, [kT, enb], [kin])
            self.tt("pool", kst.ap.rearrange("p (n s) -> p n s", n=16), kin.ap.rearrange("p (n s) -> p n s", n=16),
                    dlast.ap.unsqueeze(2).to_broadcast([128, 16, 128]), ALU.mult, [kin, dlast], [kst])
            for n4 in range(4):
                ps = self.psr()
                pb = ps.ap.bitcast(BF16)
                for i in range(4):
                    n = n4 * 4 + i
                    self.tr(pb[:, i * 128:(i + 1) * 128], kst.ap[:, n * 128:(n + 1) * 128], self.ident,
                            [kst, self.cm], [ps])
                self.cp(self.evac_eng(), kstm.ap[:, n4 * 4:(n4 + 1) * 4, :],
                        pb[:, 0:512].rearrange("p (a b) -> p a b", a=4), [ps], [kstm])
            P.op("dve", lambda e, o=S.ap: e.memset(o, 0.0), [], [S])
            P.op("dve", lambda e, o=Sb.ap: e.memset(o, 0.0), [], [Sb])
            for n in range(16):
                if n >= 8:
                    c = n - 8
                    aps = self.psr()
                    self.mm(aps.ap[:, 0:128], kin.ap[:, n * 128:(n + 1) * 128], qdec.ap[:, c * 128:(c + 1) * 128],
                            True, True, [kin, qdec], [aps])
                    am = self.rot("attm", attm)
                    self.tt("dve", am.ap, aps.ap[:, 0:128], self.tri, ALU.mult, [aps, self.cm], [am])
                    ops_ = self.psr()
                    for half in range(2):
                        self.mm(ops_.ap[:, half * 128:(half + 1) * 128], vtm.ap[:, n, half * 128:(half + 1) * 128],
                                am.ap, True, False, [vtm, am], [ops_])
                        self.mm(ops_.ap[:, half * 128:(half + 1) * 128], Sb.ap[:, half * 128:(half + 1) * 128],
                                qdec.ap[:, c * 128:(c + 1) * 128], False, True, [Sb, qdec], [ops_])
                    self.cp("act", oTf.ap[:, :, c * 128:(c + 1) * 128],
                            ops_.ap[:, 0:256].rearrange("p (a b) -> p a b", a=2), [ops_], [oTf])
                if n < 15:
                    ups = self.psr()
                    self.mm(ups.ap[:, 0:256], kstm.ap[:, n, :], vtm.ap[:, n, :], True, True, [kstm, vtm], [ups])
                    self.stt(S.ap, S.ap, dlast.ap[:, n:n + 1], ups.ap[:, 0:256], ALU.mult, ALU.add,
                             [S, dlast, ups], [S])
                    self.cp("act", Sb.ap, S.ap, [S], [Sb])
            for tq in range(2):
                ssq = self.psr()
                for half in range(2):
                    s = self.rot("sqb", sqb)
                    self.act(s.ap, oTf.ap[:, half, tq * 512:(tq + 1) * 512], AF.Square, [oTf], [s])
                    self.mm(ssq.ap, self.ones, s.ap, half == 0, half == 1, [s, self.cm], [ssq])
                self.rstd_from_ssq(rstd, ssq, 512, 1.0 / 256)
                for half in range(2):
                    self.act(sg.ap, gT.ap[:, half, tq * 512:(tq + 1) * 512], AF.Silu, [gT], [sg])
                    self.stt(tmp.ap, oTf.ap[:, half, tq * 512:(tq + 1) * 512], self.gng[:, half:half + 1], rstd.ap,
                             ALU.mult, ALU.mult, [oTf, rstd, self.cv], [tmp])
                    self.tt("dve", self.oT.ap[:, 2 * h + half, tq * 512:(tq + 1) * 512], tmp.ap, sg.ap, ALU.mult,
                            [tmp, sg], [self.oT])

    def phase3(self):
        P = self.P
        self.P.barrier(self.scr.ap)
        A, B = self.A, self.B
        A.reset(); B.reset()
        sets = []
        for _ in range(2):
            d = dict(kT=A.take([128, WIN], BF16), qT=A.take([128, TOK], BF16), krf=A.take([128, WIN], F32),
                     qrf=A.take([128, TOK], F32), krb=A.take([128, WIN], BF16), qrb=A.take([128, TOK], BF16),
                     vm=A.take([128, 16, 128], BF16), kmean=B.take([128, 8], F32), gm=B.take([128, 64], F32),
                     mx8=B.take([128, 64], F32), sel=B.take([128, 64], F32), biasb=B.take([128, 64], BF16),
                     biasT=B.take([8, TOK], BF16))
            sets.append(d)
        cosT = B.take([128, WIN], F32)
        sinT = B.take([128, WIN], F32)
        mm_ = B.take([128, 192], F32)
        pT = [B.take([128, 512], BF16), B.take([128, 512], BF16), B.take([128, 512], BF16)]
        rden = B.take([128, 512], F32)
        band = B.take([128, 2048], BF16)
        nr_scr = [(B.take([128, 512], BF16), B.take([128, 512], F32), B.take([128, 512], BF16),
                   B.take([128, 512], F32), B.take([128, 512], F32)) for _ in range(2)]
        P.dma("pool", band.ap, self.band, writes=[band])
        P.dma("sp", cosT.ap, self.rope[:, 0:WIN], writes=[cosT])
        P.dma("sp", sinT.ap, self.rope[:, WIN:2 * WIN], writes=[sinT])
        P.dma("sp", mm_.ap, self.mmask, writes=[mm_])
        M1, M2, M3 = mm_.ap[:, 0:64], mm_.ap[:, 64:128], mm_.ap[:, 128:192]
        scale = 128.0 ** -0.5
        prep_ps = [self.ps[2], self.ps[3], self.ps[6], self.ps[7]]
        attn_ps = [self.ps[0], self.ps[1]]

        def normrope(scr_, jobs):
            s, rstd, kn, t1, t2 = scr_
            for (src, n0, gcol, tab0, outf, outb) in jobs:
                self.act(s.ap, src.ap[:, n0:n0 + 512], AF.Square, [src], [s])
                yield
                ssq = self.rot("prep_ps", prep_ps)
                self.mm(ssq.ap, self.ones, s.ap, True, True, [s, self.cm], [ssq])
                yield
                self.act(rstd.ap, ssq.ap, AF.Sqrt, [ssq, self.epst], [rstd], scale=1.0 / 128, bias=self.epsc)
                yield
                self.recip(rstd.ap, rstd.ap, [rstd], [rstd])
                yield
                self.stt(kn.ap, src.ap[:, n0:n0 + 512], gcol, rstd.ap, ALU.mult, ALU.mult, [src, rstd, self.cv], [kn])
                yield
                rps = self.rot("prep_ps", prep_ps)
                self.mm(rps.ap, self.RT, kn.ap, True, True, [kn, self.cm], [rps])
                yield
                self.tt("dve", t1.ap, rps.ap, sinT.ap[:, tab0:tab0 + 512], ALU.mult, [rps, sinT], [t1])
                yield
                self.tt("pool", t2.ap, kn.ap, cosT.ap[:, tab0:tab0 + 512], ALU.mult, [kn, cosT], [t2])
                yield
                self.tt("pool", outf.ap[:, n0:n0 + 512], t1.ap, t2.ap, ALU.add, [t1, t2], [outf])
                yield
                self.cp("act", outb.ap[:, n0:n0 + 512], outf.ap[:, n0:n0 + 512], [outf], [outb])
                yield

        def inter_gen(gens):
            gens = list(gens)
            while gens:
                for g in list(gens):
                    try:
                        next(g)
                        yield
                    except StopIteration:
                        gens.remove(g)

        def prep(h, d):
            kT, qT, krf, qrf, krb, qrb, vm = d["kT"], d["qT"], d["krf"], d["qrf"], d["krb"], d["qrb"], d["vm"]
            kmean, gm, mx8, sel, biasb, biasT = d["kmean"], d["gm"], d["mx8"], d["sel"], d["biasb"], d["biasT"]
            P.dma("sp", kT.ap, self.pT_d.ap[SEG["mk"] + h * 128:SEG["mk"] + (h + 1) * 128, :],
                  reads=[self.pT_d], writes=[kT])
            P.dma("sp", qT.ap, self.pT_d.ap[SEG["mq"] + h * 128:SEG["mq"] + (h + 1) * 128, TOK:WIN],
                  reads=[self.pT_d], writes=[qT])
            P.dma("sp", vm.ap, self.vtm_d.ap[:, 2048 + h * 128:2048 + (h + 1) * 128]
                  .rearrange("(n p) e -> p n e", p=128), reads=[self.vtm_d], writes=[vm])
            yield
            jobs = [(kT, tq * 512, self.kg, tq * 512, krf, krb) for tq in range(4)] + \
                   [(qT, tq * 512, self.qg, TOK + tq * 512, qrf, qrb) for tq in range(2)]
            for _ in inter_gen([normrope(nr_scr[i], jobs[i::2]) for i in range(2)]):
                yield
            P.op("dve", lambda e, o=kmean.ap, i=krf.ap.rearrange("p (n s) -> p n s", n=8):
                 e.tensor_reduce(out=o, in_=i, axis=AX.X, op=ALU.add), [krf], [kmean])
            yield
            gps = self.rot("prep_ps", prep_ps)
            for jj in range(8):
                self.mm(gps.ap[:, jj * 8:(jj + 1) * 8], qrf.ap[:, jj * 128:(jj + 1) * 128], kmean.ap, True, True,
                        [qrf, kmean], [gps])
            yield
            self.tt("dve", gm.ap, gps.ap[:, 0:64], M1, ALU.mult, [gps, mm_], [gm])
            yield
            self.tt("dve", gm.ap, gm.ap, M2, ALU.add, [gm, mm_], [gm])
            yield
            for jj in range(8):
                P.op("dve", lambda e, o=mx8.ap[:, jj * 8:(jj + 1) * 8], i=gm.ap[:, jj * 8:(jj + 1) * 8]:
                     e.max(out=o, in_=i), [gm], [mx8])
            yield
            self.tt("dve", sel.ap.rearrange("p (a b) -> p a b", a=8), gm.ap.rearrange("p (a b) -> p a b", a=8),
                    mx8.ap.rearrange("p (a b) -> p a b", a=8)[:, :, 3:4].to_broadcast([128, 8, 8]), ALU.is_ge,
                    [gm, mx8], [sel])
            yield
            self.ts("dve", sel.ap, sel.ap, -1.0, -NEG, ALU.add, ALU.mult, [sel], [sel])
            yield
            self.tt("dve", biasb.ap, sel.ap, M3, ALU.add, [sel, mm_], [biasb])
            yield
            bps = self.rot("prep_ps", prep_ps)
            bpb = bps.ap.bitcast(BF16)
            for jj in range(8):
                self.tr(bpb[0:8, jj * 128:(jj + 1) * 128], biasb.ap[:, jj * 8:(jj + 1) * 8], self.ident,
                        [biasb, self.cm], [bps])
            yield
            self.cp("act", biasT.ap, bpb[0:8, 0:TOK], [bps], [biasT])
            yield

        def attn(h, d):
            krb, qrb, vm, biasT = d["krb"], d["qrb"], d["vm"], d["biasT"]
            ops_, dps = self.ps[4], self.ps[5]
            for gq in range(2):
                qs = qrb.ap[:, gq * 512:(gq + 1) * 512]
                kt0 = 8 + gq * 4
                nkt = kt0 + 4

                def scores(kt, qs=qs, gq=gq):
                    sps = self.rot("attn_ps", attn_ps)
                    self.mm(sps.ap, krb.ap[:, kt * 128:(kt + 1) * 128], qs, True, False, [krb, qrb], [sps])
                    self.mm(sps.ap, self.indb.ap[:, (kt // 2) * 128:(kt // 2 + 1) * 128],
                            biasT.ap[:, gq * 512:(gq + 1) * 512], False, True, [self.indb, biasT], [sps])
                    return sps
                nxt = scores(0)
                yield
                for kt in range(nkt):
                    sps = nxt
                    if kt + 1 < nkt:
                        nxt = scores(kt + 1)
                        yield
                    p = self.rot("pT", pT)
                    self.act(p.ap, sps.ap, AF.Exp, [sps], [p], scale=scale)
                    yield
                    if kt >= kt0:
                        dd = kt - kt0
                        self.tt("dve", p.ap, p.ap, band.ap[:, dd * 512:(dd + 1) * 512], ALU.mult, [p, band], [p])
                        yield
                    self.mm(ops_.ap, vm.ap[:, kt, :], p.ap, kt == 0, kt == nkt - 1, [vm, p], [ops_])
                    self.mm(dps.ap, self.ones, p.ap, kt == 0, kt == nkt - 1, [self.cm, p], [dps])
                    yield
                self.recip(rden.ap, dps.ap, [dps], [rden])
                yield
                self.tt("dve", self.oT.ap[:, 16 + h, gq * 512:(gq + 1) * 512], ops_.ap, rden.ap, ALU.mult,
                        [ops_, rden], [self.oT])
                yield

        self.interleave([prep(0, sets[0])])
        for h in range(16):
            gens = [attn(h, sets[h % 2])]
            if h + 1 < 16:
                gens.append(prep(h + 1, sets[(h + 1) % 2]))
            self.interleave(gens)

    def phase4(self):
        P = self.P
        self.P.barrier(self.scr.ap)
        A, B = self.A, self.B
        A.reset(); B.reset()
        self.xn2 = A.take([128, 32, TOK], BF16)
        xn2 = self.xn2
        wbufs = [B.take([128, 32, 512], BF16), B.take([128, 32, 512], BF16)]
        if self.dbg:
            for c in range(32):
                P.dma("sp", self.oT_d.ap[c * 128:(c + 1) * 128, :], self.oT.ap[:, c, :], reads=[self.oT],
                      writes=[self.oT_d])
        sqb = [Tl(self.stage[2].ap.bitcast(BF16)[:, 0:512], self.stage[2].b),
               Tl(self.stage[3].ap.bitcast(BF16)[:, 0:512], self.stage[3].b)]
        x2f = [self.stage[0], self.stage[1]]
        pend = None
        for dg in range(8):
            wb = self.load_wgroup(wbufs, self.w_out, dg * 512, 512)
            for tq in range(2):
                for cc in range(4):
                    c = dg * 4 + cc
                    ps = self.psr(6)
                    for kc in range(32):
                        self.mm(ps.ap, wb.ap[:, kc, cc * 128:(cc + 1) * 128], self.oT.ap[:, kc, tq * 512:(tq + 1) * 512],
                                kc == 0, kc == 31, [wb, self.oT], [ps])
                    xc = self.rot("xc", self.xc)
                    P.dma("sp", xc.ap, self.xT[c * 128:(c + 1) * 128, TOK + tq * 512:TOK + (tq + 1) * 512], writes=[xc])
                    xf = self.rot("x2f", x2f)
                    self.tt("dve", xf.ap, ps.ap, xc.ap, ALU.add, [ps, xc], [xf])
                    P.dma("sp", self.x2T_d.ap[c * 128:(c + 1) * 128, tq * 512:(tq + 1) * 512], xf.ap, reads=[xf],
                          writes=[self.x2T_d])
                    self.cp("act", xn2.ap[:, c, tq * 512:(tq + 1) * 512], xf.ap, [xf], [xn2])
                    s = self.rot("sq4", sqb)
                    self.act(s.ap, xf.ap, AF.Square, [xf], [s])
                    if pend is not None:
                        pend()
                    pend = (lambda s=s, c=c, tq=tq: self.mm(self.ps[6 + tq].ap, self.ones, s.ap, c == 0, c == 31,
                                                            [s, self.cm], [self.ps[6 + tq]]))
        pend()
        rs = [self.stage[0], self.stage[1]]
        for tq in range(2):
            self.rstd_from_ssq(rs[tq], self.ps[6 + tq], 512, 1.0 / D)
        for c in range(32):
            for tq in range(2):
                v = xn2.ap[:, c, tq * 512:(tq + 1) * 512]
                self.stt(v, v, self.g2[:, c:c + 1], rs[tq].ap, ALU.mult, ALU.mult, [xn2, rs[tq], self.cv], [xn2])

    def phase5(self):
        P = self.P
        self.P.barrier(self.scr.ap)
        B, C = self.B, self.C
        B.reset(); C.reset()
        if not hasattr(self, "xn2"):
            self.A.reset()
            self.xn2 = self.A.take([128, 32, TOK], BF16)
            P.op("pool", lambda e, o=self.xn2.ap: e.memset(o, 0.5), [], [self.xn2])
        xn2 = self.xn2
        self.wb6 = [B.take([128, 32, 256], BF16), B.take([128, 32, 256], BF16)]
        wbufs = self.wb6
        self.thr = B.take([128, 64], F32)
        self.b_mark = B.off
        qpT = B.take([128, 16, 512], BF16)
        self.a_arr = C.take([128, 8, 8, 128], F32)
        self.b_arr = C.take([128, 8, 8, 128], F32)
        a_arr, b_arr = self.a_arr, self.b_arr

        def sb(name, shape, dt):
            return B.take(shape, dt)
        KT = sb("KT", [128, 2048], BF16)
        NCH = 2
        scr5 = []
        for i in range(NCH):
            scr5.append((sb("T12", [128, 32], F32), sb("tmp5", [128, 256], F32), sb("cand", [128, 256], F32),
                         sb("c24", [128, 24], F32), sb("e16", [128, 16], F32), sb("sc5", [128, 8], F32)))
        for i in range(4):
            P.dma("pool", KT.ap[:, i * 512:(i + 1) * 512], self.keysT[:, i * 512:(i + 1) * 512], writes=[KT])
        for tt_ in range(8):
            if tt_ % 4 == 0:
                tq = tt_ // 4
                for g in range(8):
                    wb = self.load_wgroup(wbufs, self.w_pq, g * 256, 256)
                    for cc in range(2):
                        ps = self.psr(6)
                        for kc in range(32):
                            self.mm(ps.ap, wb.ap[:, kc, cc * 128:(cc + 1) * 128],
                                    xn2.ap[:, kc, tq * 512:(tq + 1) * 512], kc == 0, kc == 31, [wb, xn2], [ps])
                        self.cp("act", qpT.ap[:, g * 2 + cc, :], ps.ap, [ps], [qpT])
            t0 = (tt_ % 4) * 128
            for h4 in range(4):
                ps = self.psr(6)
                for i in range(4):
                    hp = h4 * 4 + i
                    self.mm(ps.ap[:, i * 128:(i + 1) * 128], qpT.ap[:, hp, t0:t0 + 128],
                            KT.ap[:, hp * 128:(hp + 1) * 128], True, True, [qpT, KT], [ps])
                for p_, arr in ((0, a_arr), (1, b_arr)):
                    self.cp("act", arr.ap[:, tt_, h4 * 2:h4 * 2 + 2, :],
                            ps.ap.rearrange("p (h two k) -> p h two k", two=2, k=128)[:, :, p_, :], [ps], [arr])
        for tt_ in range(8):
            def chain(h, T12, tmp, cand, c24, e16, sc, tt_=tt_):
                for p_, arr in ((0, a_arr), (1, b_arr)):
                    s_ = arr.ap[:, tt_, h, :]
                    o = p_ * 16
                    P.op("dve", lambda e, o_=T12.ap[:, o:o + 8], i=s_: e.max(out=o_, in_=i), [arr], [T12])
                    yield
                    P.op("dve", lambda e, o_=tmp.ap[:, 0:128], r=T12.ap[:, o:o + 8], i=s_:
                         e.match_replace(out=o_, in_to_replace=r, in_values=i, imm_value=-BIG), [T12, arr], [tmp])
                    yield
                    P.op("dve", lambda e, o_=T12.ap[:, o + 8:o + 16], i=tmp.ap[:, 0:128]: e.max(out=o_, in_=i),
                         [tmp], [T12])
                    yield
                self.tt("dve", cand.ap.rearrange("p (a b) -> p a b", a=16),
                        T12.ap[:, 0:16].unsqueeze(2).to_broadcast([128, 16, 16]),
                        T12.ap[:, 16:32].unsqueeze(1).to_broadcast([128, 16, 16]), ALU.add, [T12], [cand])
                yield
                P.op("dve", lambda e, o_=c24.ap[:, 0:8], i=cand.ap: e.max(out=o_, in_=i), [cand], [c24])
                yield
                P.op("dve", lambda e, o_=tmp.ap, r=c24.ap[:, 0:8], i=cand.ap:
                     e.match_replace(out=o_, in_to_replace=r, in_values=i, imm_value=-BIG), [c24, cand], [tmp])
                yield
                P.op("dve", lambda e, o_=c24.ap[:, 8:16], i=tmp.ap: e.max(out=o_, in_=i), [tmp], [c24])
                yield
                P.op("dve", lambda e, o_=cand.ap, r=c24.ap[:, 8:16], i=tmp.ap:
                     e.match_replace(out=o_, in_to_replace=r, in_values=i, imm_value=-BIG), [c24, tmp], [cand])
                yield
                P.op("dve", lambda e, o_=c24.ap[:, 16:24], i=cand.ap: e.max(out=o_, in_=i), [cand], [c24])
                yield
                self.ts("dve", sc.ap[:, 0:1], c24.ap[:, 0:1], -1.0, None, ALU.mult, None, [c24], [sc])
                yield
                self.act(e16.ap, c24.ap[:, 0:16], AF.Exp, [c24, sc], [e16, sc], bias=sc.ap[:, 0:1],
                         accum=sc.ap[:, 1:2])
                yield
                self.act(sc.ap[:, 2:3], sc.ap[:, 1:2], AF.Ln, [sc], [sc])
                yield
                self.tt("dve", sc.ap[:, 3:4], c24.ap[:, 0:1], sc.ap[:, 2:3], ALU.add, [c24, sc], [sc])
                yield
                self.tt("dve", sc.ap[:, 4:5], c24.ap[:, 15:16], c24.ap[:, 16:17], ALU.add, [c24], [sc])
                yield
                self.stt(self.thr.ap[:, tt_ * 8 + h:tt_ * 8 + h + 1], sc.ap[:, 4:5], 0.5, sc.ap[:, 3:4],
                         ALU.mult, ALU.subtract, [sc], [self.thr])
                yield
                self.ts("dve", a_arr.ap[:, tt_, h, :], a_arr.ap[:, tt_, h, :], sc.ap[:, 3:4], None, ALU.subtract, None,
                        [a_arr, sc], [a_arr])
                yield
            for h0 in range(0, 8, NCH):
                self.interleave([chain(h0 + i, *scr5[i]) for i in range(NCH)])

    def phase6(self):
        P = self.P
        nc = self.nc
        xn2, a_arr, b_arr, thr = self.xn2, self.a_arr, self.b_arr, self.thr

        def sb(name, shape, dt):
            return self.B.take(shape, dt)
        self.P.barrier(self.scr.ap)
        self.B.off = self.b_mark
        c8 = [sb("c8_%d" % i, [128, 8], F32) for i in range(2)]
        E = [sb("E%d" % i, [128, 128], F32) for i in range(6)]
        Gm = [sb("Gm%d" % i, [128, 128], BF16) for i in range(6)]
        gl = [sb("gl%d" % i, [128, 512], F32) for i in range(2)]
        Gs = [sb("Gs%d" % i, [128, 512], BF16) for i in range(2)]
        PH0 = 8
        if PH0 < 8:
            ea3 = sb("ea3", [128, 8, 8 - PH0, 128], F32)
            EB3 = sb("EB3", [128, 8, 8 - PH0, 128], BF16)
            Ep = [sb("Ep%d" % i, [128, 128], BF16) for i in range(4)]
            for tt_ in range(8):
                self.act(ea3.ap[:, tt_], a_arr.ap[:, tt_, PH0:8, :], AF.Exp, [a_arr], [ea3])
                self.act(EB3.ap[:, tt_], b_arr.ap[:, tt_, PH0:8, :], AF.Exp, [b_arr], [EB3])
        wst = [Tl(self.stage[i].ap.bitcast(BF16)[:, 0:512], self.stage[i].b) for i in range(4)]
        wbufs = self.wb6
        wb = None
        for i1 in range(128):
            if i1 % 2 == 0:
                wb = self.load_wgroup(wbufs, self.uT, i1 * 128, 256)
            for tq in range(2):
                hps = self.psr(4)
                kc = 0
                gacc = self.rot("gps", self.ps[4:8])
                for t4 in range(4):
                    tt_ = tq * 4 + t4
                    c = self.rot("c8", c8)
                    self.tt("dve", c.ap, thr.ap[:, tt_ * 8:(tt_ + 1) * 8], a_arr.ap[:, tt_, :, i1], ALU.subtract,
                            [thr, a_arr], [c])
                    for h in range(8):
                        if h >= PH0:
                            e_ = self.rot("Ep", Ep)
                            self.ts("pool", e_.ap, EB3.ap[:, tt_, h - PH0, :], ea3.ap[:, tt_, h - PH0, i1:i1 + 1], 1.0,
                                    ALU.mult, ALU.mult, [EB3, ea3], [e_])
                        else:
                            e_ = self.rot("E", E)
                            self.act(e_.ap, b_arr.ap[:, tt_, h, :], AF.Exp, [b_arr, a_arr], [e_],
                                     bias=a_arr.ap[:, tt_, h, i1:i1 + 1])
                        g_ = self.rot("Gm", Gm)
                        self.stt(g_.ap, b_arr.ap[:, tt_, h, :], c.ap[:, h:h + 1], e_.ap, ALU.is_ge, ALU.mult,
                                 [b_arr, c, e_], [g_])
                        self.mm(gacc.ap[:, t4 * 128:(t4 + 1) * 128], self.ident, g_.ap, h == 0, h == 7,
                                [g_, self.cm], [gacc])
                        self.mm(hps.ap, wb.ap[:, kc, (i1 % 2) * 128:(i1 % 2 + 1) * 128],
                                xn2.ap[:, kc, tq * 512:(tq + 1) * 512], kc == 0, kc == 31, [wb, xn2], [hps])
                        kc += 1
                gs_ = self.rot("Gs", Gs)
                self.cp("dve", gs_.ap, gacc.ap, [gacc], [gs_])
                gps = self.rot("gps", self.ps[4:8])
                gpb = gps.ap.bitcast(BF16)
                for t4 in range(4):
                    self.tr(gpb[:, t4 * 128:(t4 + 1) * 128], gs_.ap[:, t4 * 128:(t4 + 1) * 128], self.ident,
                            [gs_, self.cm], [gps])
                g = self.rot("gl", gl)
                self.act(g.ap, hps.ap, AF.Gelu, [hps], [g])
                w = self.rot("wst", wst)
                self.tt("dve", w.ap, g.ap, gpb[:, 0:512], ALU.mult, [g, gps], [w])
                P.dma("sp", self.wT_d.ap[i1, :, tq * 512:(tq + 1) * 512], w.ap, reads=[w], writes=[self.wT_d])
        self.P.barrier(self.scr.ap)
        B = self.B
        B.reset()
        wt = [B.take([128, TOK], BF16) for _ in range(4)]
        vb = [B.take([128, 512], BF16) for _ in range(8)]
        self.A.reset(); self.C.reset()
        res = [self.A.take([128, TOK], BF16) for _ in range(32)] + [self.C.take([128, TOK], BF16) for _ in range(32)]
        while True:
            try:
                res.append(B.take([128, TOK], BF16))
            except AssertionError:
                break
        NRES = len(res)
        for dg in range(8):
            for i1 in range(128):
                if i1 < NRES:
                    w = res[i1]
                    if dg == 0:
                        P.dma("sp", w.ap, self.wT_d.ap[i1], reads=[self.wT_d], writes=[w])
                else:
                    w = self.rot("wt", wt)
                    P.dma("sp", w.ap, self.wT_d.ap[i1], reads=[self.wT_d], writes=[w])
                v = self.rot("vb", vb)
                P.dma("pool", v.ap, self.pv[i1 * 128:(i1 + 1) * 128, dg * 512:(dg + 1) * 512], writes=[v])
                for cc in range(4):
                    for tq in range(2):
                        ps = self.ps[cc * 2 + tq]
                        self.mm(ps.ap, v.ap[:, cc * 128:(cc + 1) * 128], w.ap[:, tq * 512:(tq + 1) * 512],
                                i1 == 0, i1 == 127, [v, w], [ps])
            for cc in range(4):
                for tq in range(2):
                    c = dg * 4 + cc
                    ps = self.ps[cc * 2 + tq]
                    xc = self.rot("xc", self.xc)
                    P.dma("sp", xc.ap, self.x2T_d.ap[c * 128:(c + 1) * 128, tq * 512:(tq + 1) * 512],
                          reads=[self.x2T_d], writes=[xc])
                    o = self.rot("o6", self.stage)
                    self.tt("dve", o.ap, ps.ap, xc.ap, ALU.add, [ps, xc], [o])
                    P.dma("sp", self.outT.ap[c * 128:(c + 1) * 128, tq * 512:(tq + 1) * 512], o.ap, reads=[o],
                          writes=[self.outT])

    def build(self):
        self.setup()
        phases = [self.phase1, self.phase2, self.phase3, self.phase4, self.phase5, self.phase6]
        for i, ph in enumerate(phases):
            if i + 1 > self.stop_after:
                break
            if self.only is not None and (i + 1) not in self.only:
                continue
            ph()
        if self.stop_after < 6:
            z = self.stage[0]
            self.P.op("dve", lambda e, o=z.ap: e.memset(o, 0.0), [], [z])
            self.P.dma("sp", self.outT.ap[0:128, 0:512], z.ap, reads=[z], writes=[self.outT])
        self.stats = self.P.emit()
        return self.nc


def _consts():
    ident = np.eye(128, dtype=np.float32)
    tri = (np.arange(128)[:, None] <= np.arange(128)[None, :]).astype(np.float32)
    RT = np.zeros((128, 128), np.float32)
    for d in range(16):
        RT[d + 16, d] = -1.0
        RT[d, d + 16] = 1.0
    ones = np.ones((128, 128), np.float32)
    cmat = np.concatenate([ident, tri, RT, ones], axis=1)
    ind = np.zeros((8, 8, 128), np.float32)
    for n in range(8):
        ind[n, n, :] = 1.0
    k = np.arange(128)[:, None]
    q = np.arange(512)[None, :]
    band = np.concatenate([((q - k) >= 128 * d).astype(np.float32) for d in range(4)], axis=1)
    return np.ascontiguousarray(cmat), np.ascontiguousarray(ind.reshape(8, 1024)), np.ascontiguousarray(band)


def _rope_tables(s):
    half = 16
    inv = (np.float32(500000.0) ** (-np.arange(half, dtype=np.float32) / np.float32(half))).astype(np.float32)
    pos = (np.arange(WIN) + (s - 1) * TOK).astype(np.float32)
    ang = (pos[:, None] * inv[None, :]).astype(np.float32)
    cos = np.cos(ang).astype(np.float32).T
    sin = np.sin(ang).astype(np.float32).T
    cosT = np.ones((128, WIN), np.float32)
    sinT = np.zeros((128, WIN), np.float32)
    cosT[0:16] = cos
    cosT[16:32] = cos
    sinT[0:16] = sin
    sinT[16:32] = sin
    return np.ascontiguousarray(np.concatenate([cosT, sinT], axis=1))


def _moba_masks(s):
    M1 = np.zeros((8, 8), np.float32)
    M2 = np.full((8, 8), -BIG, np.float32)
    M3 = np.full((8, 8), NEG, np.float32)
    nmin = 4 * (1 - s)
    for jj in range(8):
        r = (8 + jj) // 2
        for n in range(8):
            if nmin <= n < r:
                M1[jj, n] = 1.0
                M2[jj, n] = 0.0
                M3[jj, n] = 0.0
            elif n == r:
                M2[jj, n] = BIG
                M3[jj, n] = 0.0
    m = np.concatenate([M1.reshape(-1), M2.reshape(-1), M3.reshape(-1)])
    return np.ascontiguousarray(np.broadcast_to(m[None, :], (128, 192))).astype(np.float32)


def make_in_maps(x, norm_mix_g, w_in, w_gate_up, b_gate, gla_norm_g, q_norm_g, k_norm_g,
                 w_out, norm_ffn_g, peer_wq, peer_keys, peer_u, peer_v):
    f = np.float32
    x = np.asarray(x, f)
    cmat, ind, band = _consts()
    cvec = np.concatenate([
        np.asarray(norm_mix_g, f)[0].reshape(32, 128).T,
        np.asarray(norm_ffn_g, f)[0].reshape(32, 128).T,
        np.asarray(b_gate, f)[0].reshape(8, 128).T,
        np.asarray(gla_norm_g, f)[0].reshape(2, 128).T,
        np.asarray(q_norm_g, f)[0].reshape(1, 128).T,
        np.asarray(k_norm_g, f)[0].reshape(1, 128).T], axis=1)
    cvec = np.ascontiguousarray(cvec)
    keysT = np.ascontiguousarray(np.asarray(peer_keys, f)[0].transpose(3, 0, 1, 2).reshape(128, 2048))
    uT = np.ascontiguousarray(np.asarray(peer_u, f)[0].T)
    shared = dict(w_in=np.ascontiguousarray(np.asarray(w_in, f)[0]),
                  w_out=np.ascontiguousarray(np.asarray(w_out, f)[0]),
                  w_pq=np.ascontiguousarray(np.asarray(peer_wq, f)[0]),
                  uT=uT, pv=np.ascontiguousarray(np.asarray(peer_v, f)[0]),
                  keysT=keysT, wgu=np.ascontiguousarray(np.asarray(w_gate_up, f)[0]),
                  cvec=cvec, cmat=cmat, ind=ind, band=band)
    ropes = [_rope_tables(0), _rope_tables(1)]
    masks = [_moba_masks(0), _moba_masks(1)]
    maps = []
    for c in range(8):
        b, s = c // 2, c % 2
        xT = np.zeros((D, WIN), f)
        if s == 0:
            xT[:, TOK:] = x[b, 0:TOK].T
        else:
            xT[:, :] = x[b].T
        m = dict(shared)
        m.update(xT=xT, rope=ropes[s], mmask=masks[s])
        maps.append(m)
    return maps


def kernel(x, norm_mix_g, w_in, w_gate_up, b_gate, gla_norm_g, q_norm_g, k_norm_g,
           w_out, norm_ffn_g, peer_wq, peer_keys, peer_u, peer_v):
    maps = make_in_maps(x, norm_mix_g, w_in, w_gate_up, b_gate, gla_norm_g, q_norm_g, k_norm_g,
                        w_out, norm_ffn_g, peer_wq, peer_keys, peer_u, peer_v)
    bld = Builder()
    nc = bld.build()
    res = run_bass_kernel_spmd(nc, maps, core_ids=list(range(8)))
    out = np.zeros((4, 2048, D), np.float32)
    for c in range(8):
        b, s = c // 2, c % 2
        out[b, s * TOK:(s + 1) * TOK, :] = np.asarray(res.results[c]["outT"]).T
    return out
```

```python
import numpy as np
import concourse.bass as bass
import concourse.mybir as mybir
from concourse.bass_utils import run_bass_kernel_spmd

F32 = mybir.dt.float32
BF16 = mybir.dt.bfloat16
AF = mybir.ActivationFunctionType
ALU = mybir.AluOpType
AX = mybir.AxisListType

D = 4096
TOK = 1024
WIN = 2048
NEG = -32768.0
BIG = 1.0e30
EPS = 1e-6


class Buf:
    __slots__ = ("w", "rs", "x")

    def __init__(self):
        self.w = None
        self.rs = []
        self.x = False


class Tl:
    __slots__ = ("ap", "b")

    def __init__(self, ap, b=None):
        self.ap = ap
        self.b = b if b is not None else Buf()

    def __getitem__(self, k):
        return self.ap[k]


class Op:
    __slots__ = ("eng", "fn", "waits", "signal", "is_dma", "sem", "val")


def _bufs(xs):
    return [x.b if isinstance(x, Tl) else x for x in xs]


class Prog:
    ENGS = ("pe", "act", "dve", "pool", "sp")

    def __init__(self, nc, n_dma_sems=40):
        self.nc = nc
        self.ops = []
        self.n_dma_sems = n_dma_sems
        self.G = Buf()
        self.eng_obj = {"pe": nc.tensor, "act": nc.scalar, "dve": nc.vector,
                        "pool": nc.gpsimd, "sp": nc.sync}

    def op(self, eng, fn, reads=(), writes=(), dma=False, barrier=False):
        o = Op()
        o.eng, o.fn, o.is_dma, o.signal, o.sem, o.val = eng, fn, dma, False, None, None
        reads = _bufs(reads)
        writes = _bufs(writes)
        xr = [b for b in reads if b.x and b not in writes]
        if barrier:
            writes = writes + [self.G]
        else:
            reads = reads + [self.G]
        deps = {}
        oid = len(self.ops)
        for b in reads:
            if b.w is not None:
                deps[b.w] = True
        for b in writes:
            if b.w is not None:
                deps.setdefault(b.w, False)
            for r in b.rs:
                deps.setdefault(r, False)
        for b in xr:
            for r in b.rs:
                deps.setdefault(r, False)
        o.waits = deps
        self.ops.append(o)
        for b in writes:
            b.w = oid
            b.rs = []
        for b in reads:
            if b not in writes:
                b.rs.append(oid)
        return oid

    def dma(self, queue, out, in_, reads=(), writes=()):
        return self.op(queue, lambda e, o=out, i=in_: e.dma_start(out=o, in_=i), reads, writes, dma=True)

    def barrier(self, scratch):
        self.op("pool", lambda e, s=scratch: e.memset(s, 0.0), barrier=True)

    def emit(self):
        nc, ops = self.nc, self.ops
        eng_seq = {e: 0 for e in self.ENGS}
        nxt = 0
        nxt_sw = 0
        n_hw = self.n_dma_sems - 12
        slot_cnt = [0] * self.n_dma_sems
        slot_last = [None] * self.n_dma_sems
        key = [None] * len(ops)
        for i, o in enumerate(ops):
            if o.is_dma:
                if o.eng == "pool":
                    k = n_hw + nxt_sw
                    nxt_sw = (nxt_sw + 1) % (self.n_dma_sems - n_hw)
                else:
                    k = nxt
                    nxt = (nxt + 1) % n_hw
                slot_cnt[k] += 1
                key[i] = (("dma", k), slot_cnt[k])
                if slot_last[k] is not None:
                    o.waits[slot_last[k]] = True
                slot_last[k] = i
            else:
                eng_seq[o.eng] += 1
                key[i] = (o.eng, eng_seq[o.eng])
        seen = {e: {} for e in self.ENGS}
        final_waits = [None] * len(ops)
        for i, o in enumerate(ops):
            fw = {}
            sn = seen[o.eng]
            for p, raw in o.waits.items():
                po = ops[p]
                sk, order = key[p]
                if (not po.is_dma) and po.eng == o.eng:
                    if o.eng in ("pe", "sp"):
                        continue
                if sn.get(sk, 0) >= order:
                    continue
                if fw.get(sk, (0, None))[0] < order:
                    fw[sk] = (order, p)
            for sk, (order, p) in fw.items():
                sn[sk] = order
                ops[p].signal = True
            final_waits[i] = [p for (_, p) in fw.values()]
        LIMIT = 1500
        sems, cnt, nsem = {}, {}, [0]

        def fresh(sk):
            sems[sk] = nc.alloc_semaphore(name="s%d" % nsem[0])
            nsem[0] += 1
            cnt[sk] = 0
        for e in ("pe", "act", "dve", "pool"):
            fresh(e)
        for k in range(self.n_dma_sems):
            fresh(("dma", k))
        maxc = 0
        for i, o in enumerate(ops):
            e = self.eng_obj[o.eng]
            for p in final_waits[i]:
                e.wait_ge(ops[p].sem, ops[p].val)
            inst = o.fn(e)
            if o.is_dma or o.signal:
                sk = key[i][0] if o.is_dma else o.eng
                inc = 16 if o.is_dma else 1
                if cnt[sk] + inc > LIMIT:
                    fresh(sk)
                cnt[sk] += inc
                maxc = max(maxc, cnt[sk])
                o.sem, o.val = sems[sk], cnt[sk]
                inst.then_inc(sems[sk], inc)
        last = {}
        for i, o in enumerate(ops):
            if o.is_dma:
                last[key[i][0]] = o
        for sk, o in last.items():
            nc.sync.wait_ge(o.sem, o.val)
        return dict(n_ops=len(ops), max_cnt=maxc, n_sems=nsem[0])


class Arena:
    def __init__(self, ap_bf16):
        self.ap = ap_bf16
        self.n = ap_bf16.shape[1]
        self.off = 0

    def reset(self):
        self.off = 0

    def take(self, shape, dt):
        free = int(np.prod(shape[1:]))
        nb = free * (2 if dt == BF16 else 4)
        nb = (nb + 63) // 64 * 64
        assert self.off + nb // 2 <= self.n, ("arena overflow", shape, self.off, self.n)
        v = self.ap[0:shape[0], self.off:self.off + nb // 2]
        self.off += nb // 2
        if dt != BF16:
            v = v.bitcast(dt)
        v = v[:, 0:free]
        if len(shape) == 3:
            v = v.rearrange("p (a b) -> p a b", a=shape[1])
        elif len(shape) == 4:
            v = v.rearrange("p (a b c) -> p a b c", a=shape[1], b=shape[2])
        return Tl(v)


SEG = dict(gq=0, gk=1024, gv=2048, gg=4096, glr=6144, mq=6160, mk=8208, mv=10256)
IN_COLS = 12304


class Builder:
    def __init__(self, stop_after=99, dbg=False, only=None, small=()):
        self.stop_after = stop_after
        self.dbg = dbg
        self.only = only
        self.small = set(small)
        nc = bass.Bass("TRN2", target_bir_lowering=False)
        self.nc = nc
        self.P = Prog(nc)
        self._rot = {}

        def din(name, shape, dt=F32):
            if name in self.small:
                shape = [128, 128]
            return nc.dram_tensor(name, list(shape), dt, kind="ExternalInput").ap()

        self.xT = din("xT", [D, WIN])
        self.w_in = din("w_in", [D, IN_COLS])
        self.w_out = din("w_out", [D, D])
        self.w_pq = din("w_pq", [D, 2048])
        self.uT = din("uT", [D, 16384])
        self.pv = din("pv", [16384, D])
        self.keysT = din("keysT", [128, 16 * 128])
        self.wgu = din("wgu", [16, 1024])
        self.cvec = din("cvec", [128, 32 + 32 + 8 + 2 + 1 + 1])
        self.rope = din("rope", [128, 2 * WIN])
        self.mmask = din("mmask", [128, 3 * 64])
        self.cmat = din("cmat", [128, 4 * 128])
        self.ind = din("ind", [8, 8 * 128])
        self.band = din("band", [128, 2048])
        kind = "ExternalOutput" if dbg else "Internal"
        self.pT_d = Tl(nc.dram_tensor("pT_d", [IN_COLS, WIN], BF16, kind=kind).ap())
        self.vtm_d = Tl(nc.dram_tensor("vtm_d", [WIN, 4096], BF16, kind=kind).ap())
        self.x2T_d = Tl(nc.dram_tensor("x2T_d", [D, TOK], F32, kind=kind).ap())
        self.oT_d = Tl(nc.dram_tensor("oT_d", [D, TOK], BF16, kind=kind).ap()) if dbg else None
        self.wT_d = Tl(nc.dram_tensor("wT_d", [128, 128, TOK], BF16, kind="Internal").ap())
        self.outT = Tl(nc.dram_tensor("outT", [D, TOK], F32, kind="ExternalOutput").ap())

        def sb(name, shape, dt):
            return Tl(nc.alloc_sbuf_tensor(name, list(shape), dt).ap())

        self.A = Arena(nc.alloc_sbuf_tensor("arenaA", [128, 32768], BF16).ap())
        self.B = Arena(nc.alloc_sbuf_tensor("arenaB", [128, 32768], BF16).ap())
        self.C = Arena(nc.alloc_sbuf_tensor("arenaC", [128, 32768], BF16).ap())
        self.cm = sb("cm", [128, 512], BF16)
        self.ident = self.cm.ap[:, 0:128]
        self.tri = self.cm.ap[:, 128:256]
        self.RT = self.cm.ap[:, 256:384]
        self.ones = self.cm.ap[:, 384:512]
        self.cv = sb("cv", [128, 76], F32)
        self.nb = sb("nb", [128, 8], F32)
        self.indb = sb("indb", [8, 1024], BF16)
        self.scr = sb("scr", [128, 8], F32)
        self.stage = [sb("stage%d" % i, [128, 512], F32) for i in range(4)]
        self.xc = [sb("xc%d" % i, [128, 512], F32) for i in range(2)]
        self.cm_f = self.stage[0]
        self.ps = [Tl(nc.alloc_psum_tensor("ps%d" % i, [128, 512], F32).ap()) for i in range(8)]
        for p_ in self.ps:
            p_.b.x = True

    def rot(self, name, lst):
        i = self._rot.get(name, 0)
        self._rot[name] = i + 1
        return lst[i % len(lst)]

    def psr(self, n=4):
        return self.rot("ps%d" % n, self.ps[0:n])

    def mm(self, out, lhsT, rhs, start, stop, R, W):
        self.P.op("pe", lambda e, o=out, l=lhsT, r=rhs, s=start, t=stop:
                  e.matmul(o, lhsT=l, rhs=r, start=s, stop=t), R, W)

    def tr(self, out, in_, ident, R, W):
        self.P.op("pe", lambda e, o=out, i=in_, d=ident: e.transpose(o, i, d), R, W)

    def act(self, out, in_, func, R, W, scale=None, bias=None, accum=None):
        kw = {}
        if scale is not None:
            kw["scale"] = scale
        if bias is not None:
            kw["bias"] = bias
        if accum is not None:
            kw["accum_out"] = accum
        self.P.op("act", lambda e, o=out, i=in_, f=func, k=kw: e.activation(out=o, in_=i, func=f, **k), R, W)

    def tt(self, eng, out, in0, in1, op, R, W):
        self.P.op(eng, lambda e, o=out, a=in0, b=in1, p=op: e.tensor_tensor(out=o, in0=a, in1=b, op=p), R, W)

    def stt(self, out, in0, scalar, in1, op0, op1, R, W):
        self.P.op("dve", lambda e, o=out, a=in0, s=scalar, b=in1, p0=op0, p1=op1:
                  e.scalar_tensor_tensor(out=o, in0=a, scalar=s, in1=b, op0=p0, op1=p1), R, W)

    def ts(self, eng, out, in0, s1, s2, op0, op1, R, W):
        if op1 is None:
            self.P.op(eng, lambda e, o=out, a=in0, x=s1, p0=op0:
                      e.tensor_scalar(out=o, in0=a, scalar1=x, scalar2=None, op0=p0), R, W)
        else:
            self.P.op(eng, lambda e, o=out, a=in0, x=s1, y=s2, p0=op0, p1=op1:
                      e.tensor_scalar(out=o, in0=a, scalar1=x, scalar2=y, op0=p0, op1=p1), R, W)

    def cp(self, eng, out, in_, R, W):
        if eng == "act":
            self.P.op("act", lambda e, o=out, i=in_: e.copy(out=o, in_=i), R, W)
        else:
            self.P.op(eng, lambda e, o=out, i=in_: e.tensor_copy(out=o, in_=i), R, W)

    @staticmethod
    def interleave(gens):
        gens = list(gens)
        while gens:
            for g in list(gens):
                try:
                    next(g)
                except StopIteration:
                    gens.remove(g)

    def evac_eng(self):
        return self.rot("evac", ["act", "dve"])

    def recip(self, out, in_, R, W):
        self.P.op("dve", lambda e, o=out, i=in_: e.reciprocal(out=o, in_=i), R, W)

    def rstd_from_ssq(self, out_t, ssq_ps, n, inv_n):
        self.act(out_t.ap[:, 0:n], ssq_ps.ap[:, 0:n], AF.Sqrt, [ssq_ps, self.epst], [out_t], scale=inv_n, bias=self.epsc)
        self.recip(out_t.ap[:, 0:n], out_t.ap[:, 0:n], [out_t], [out_t])

    def setup(self):
        P = self.P
        P.dma("sp", self.cm_f.ap, self.cmat, writes=[self.cm_f])
        self.cp("dve", self.cm.ap, self.cm_f.ap, [self.cm_f], [self.cm])
        P.dma("sp", self.cv.ap, self.cvec, writes=[self.cv])
        self.ts("dve", self.nb.ap, self.cv.ap[:, 64:72], -1.0, None, ALU.mult, None, [self.cv], [self.nb])
        P.dma("pool", self.indb.ap, self.ind, writes=[self.indb])
        self.epst = Tl(self.nc.alloc_sbuf_tensor("epst", [128, 1], F32).ap())
        P.op("dve", lambda e, o=self.epst.ap: e.memset(o, EPS), [], [self.epst])
        self.epsc = self.epst.ap[:, 0:1]
        self.g1 = self.cv.ap[:, 0:32]
        self.g2 = self.cv.ap[:, 32:64]
        self.gng = self.cv.ap[:, 72:74]
        self.qg = self.cv.ap[:, 74:75]
        self.kg = self.cv.ap[:, 75:76]

    def load_wgroup(self, wbufs, wdram, c0, ncols):
        wb = self.rot("wg", wbufs)
        src = wdram[:, c0:c0 + ncols].rearrange("(c p) n -> p c n", p=128)
        half = 16
        self.P.dma("pool", wb.ap[:, 0:half, 0:ncols], src[:, 0:half, :], writes=[wb])
        self.P.dma("pool", wb.ap[:, half:32, 0:ncols], src[:, half:32, :], writes=[wb])
        return wb

    def phase1(self):
        P = self.P
        self.A.reset(); self.B.reset(); self.C.reset()
        hn = [self.A.take([128, 32, 512], BF16), self.A.take([128, 32, 512], BF16),
              self.B.take([128, 32, 512], BF16), self.B.take([128, 32, 512], BF16)]
        wbufs = [self.C.take([128, 32, 512], BF16), self.C.take([128, 32, 512], BF16)]
        def halves(t):
            v = t.ap.bitcast(BF16)
            return [Tl(v[:, 0:512]), Tl(v[:, 512:1024])]
        sq = halves(self.stage[2])
        stg = halves(self.stage[1]) + halves(self.stage[3])
        rstd = self.stage[0]

        def norm_gen(tq):
            ssq = self.ps[7]
            for kc in range(32):
                xc = self.rot("xc", self.xc)
                P.dma("sp", xc.ap, self.xT[kc * 128:(kc + 1) * 128, tq * 512:(tq + 1) * 512], writes=[xc])
                s = self.rot("sq", sq)
                self.act(s.ap, xc.ap, AF.Square, [xc], [s])
                self.mm(ssq.ap, self.ones, s.ap, kc == 0, kc == 31, [s, self.cm], [ssq])
                yield
            self.rstd_from_ssq(rstd, ssq, 512, 1.0 / D)
            yield
            for kc in range(32):
                xc = self.rot("xc", self.xc)
                P.dma("sp", xc.ap, self.xT[kc * 128:(kc + 1) * 128, tq * 512:(tq + 1) * 512], writes=[xc])
                self.stt(hn[tq].ap[:, kc, :], xc.ap, self.g1[:, kc:kc + 1], rstd.ap, ALU.mult, ALU.mult,
                         [xc, rstd, self.cv], [hn[tq]])
                yield

        groups = []
        for name, n, lay, allt in (("gq", 1024, "fm", False), ("gg", 2048, "fm", False), ("mq", 2048, "fm", False),
                                   ("gk", 1024, "fm", True), ("gv", 2048, "tm", True), ("glr", 16, "fm", True),
                                   ("mk", 2048, "fm", True), ("mv", 2048, "tm", True)):
            for g0 in range(0, n, 512):
                groups.append((name, SEG[name] + g0, min(512, n - g0), lay, allt, g0))

        def proj_gen():
            for (name, c0, ncols, lay, allt, g0) in groups:
                wb = self.load_wgroup(wbufs, self.w_in, c0, ncols)
                tqs = range(4) if allt else range(2, 4)
                if lay == "fm":
                    for tq in tqs:
                        for cc in range((ncols + 127) // 128):
                            m = min(128, ncols - cc * 128)
                            ps = self.psr(6)
                            for kc in range(32):
                                self.mm(ps.ap[0:m, :], wb.ap[:, kc, cc * 128:cc * 128 + m], hn[tq].ap[:, kc, :],
                                        kc == 0, kc == 31, [wb, hn[tq]], [ps])
                            st = self.rot("stg", stg)
                            self.cp(self.evac_eng(), st.ap[0:m, :], ps.ap[0:m, :], [ps], [st])
                            r0 = c0 + cc * 128
                            P.dma("sp", self.pT_d.ap[r0:r0 + m, tq * 512:(tq + 1) * 512], st.ap[0:m, :],
                                  reads=[st], writes=[self.pT_d])
                            yield
                else:
                    vcol = (0 if name == "gv" else 2048) + g0
                    for tq in tqs:
                        for t4 in range(4):
                            ps = self.psr(6)
                            for kc in range(32):
                                self.mm(ps.ap, hn[tq].ap[:, kc, t4 * 128:(t4 + 1) * 128], wb.ap[:, kc, :],
                                        kc == 0, kc == 31, [wb, hn[tq]], [ps])
                            st = self.rot("stg", stg)
                            self.cp(self.evac_eng(), st.ap, ps.ap, [ps], [st])
                            t0 = tq * 512 + t4 * 128
                            P.dma("sp", self.vtm_d.ap[t0:t0 + 128, vcol:vcol + 512], st.ap,
                                  reads=[st], writes=[self.vtm_d])
                            yield

        for tq in (2, 3):
            for _ in norm_gen(tq):
                pass

        def chain2():
            for tq in (0, 1):
                for _ in norm_gen(tq):
                    yield
        ng = chain2()
        for _ in proj_gen():
            for _ in range(4):
                next(ng, None)
        for _ in ng:
            pass

    def phase2(self):
        P = self.P
        self.P.barrier(self.scr.ap)
        A, B, C = self.A, self.B, self.C
        A.reset(); B.reset(); C.reset()
        self.oT = C.take([128, 32, TOK], BF16)
        glrT = A.take([16, WIN], BF16)
        wgu = A.take([16, 1024], BF16)
        sp = A.take([128, WIN], F32)
        cum = A.take([128, WIN], F32)
        enb = A.take([128, WIN], F32)
        eb = A.take([128, TOK], F32)
        rmask = A.take([128, WIN], BF16)
        kT = A.take([128, WIN], BF16)
        qT = A.take([128, TOK], BF16)
        qdec = A.take([128, TOK], BF16)
        kin = A.take([128, WIN], BF16)
        kst = B.take([128, WIN], BF16)
        kstm = B.take([128, 16, 128], BF16)
        vtm = B.take([128, 16, 256], BF16)
        dlast = B.take([128, 16], F32)
        S = B.take([128, 256], F32)
        Sb = B.take([128, 256], BF16)
        attm = [B.take([128, 128], BF16), B.take([128, 128], BF16)]
        oTf = B.take([128, 2, TOK], F32)
        gT = B.take([128, 2, TOK], BF16)
        sqb = [B.take([128, 512], BF16), B.take([128, 512], BF16)]
        rstd = B.take([128, 512], F32)
        sg = B.take([128, 512], F32)
        tmp = B.take([128, 512], F32)
        P.dma("sp", glrT.ap, self.pT_d.ap[SEG["glr"]:SEG["glr"] + 16, :], reads=[self.pT_d], writes=[glrT])
        P.dma("pool", wgu.ap, self.wgu, writes=[wgu])
        P.op("pool", lambda e, o=rmask.ap: e.memset(o, 1.0), [], [rmask])
        P.op("pool", lambda e, o=rmask.ap[:, 0:WIN:128]: e.memset(o, 0.0), [], [rmask])
        for h in range(8):
            P.dma("sp", kT.ap, self.pT_d.ap[SEG["gk"] + h * 128:SEG["gk"] + (h + 1) * 128, :],
                  reads=[self.pT_d], writes=[kT])
            P.dma("sp", qT.ap, self.pT_d.ap[SEG["gq"] + h * 128:SEG["gq"] + (h + 1) * 128, TOK:WIN],
                  reads=[self.pT_d], writes=[qT])
            P.dma("sp", vtm.ap, self.vtm_d.ap[:, h * 256:(h + 1) * 256].rearrange("(n p) e -> p n e", p=128),
                  reads=[self.vtm_d], writes=[vtm])
            P.dma("sp", gT.ap, self.pT_d.ap[SEG["gg"] + h * 256:SEG["gg"] + (h + 1) * 256, TOK:WIN]
                  .rearrange("(c p) t -> p c t", p=128), reads=[self.pT_d], writes=[gT])
            for tq in range(4):
                ps = self.psr()
                self.mm(ps.ap, wgu.ap[:, h * 128:(h + 1) * 128], glrT.ap[:, tq * 512:(tq + 1) * 512], True, True,
                        [wgu, glrT], [ps])
                self.act(sp.ap[:, tq * 512:(tq + 1) * 512], ps.ap, AF.Exp, [ps, self.nb], [sp],
                         scale=-1.0, bias=self.nb.ap[:, h:h + 1])
            self.act(sp.ap, sp.ap, AF.Ln, [sp], [sp], bias=1.0)
            P.op("dve", lambda e, o=cum.ap, m=rmask.ap, s=sp.ap:
                 e.tensor_tensor_scan(out=o, data0=m, data1=s, initial=0.0, op0=ALU.mult, op1=ALU.add),
                 [rmask, sp], [cum])
            self.act(eb.ap, cum.ap[:, TOK:WIN], AF.Exp, [cum], [eb], scale=-1.0 / 16)
            self.act(enb.ap, cum.ap, AF.Exp, [cum], [enb], scale=1.0 / 16)
            self.act(dlast.ap, cum.ap[:, 127:WIN:128], AF.Exp, [cum], [dlast], scale=-1.0 / 16)
            self.stt(qdec.ap, qT.ap, 128.0 ** -0.5, eb.ap, ALU.mult, ALU.mult, [qT, eb], [qdec])
            self.tt("dve", kin.ap, kT.ap, enb.ap, ALU.mult, [kT, enb], [kin])
            self.tt("pool", kst.ap.rearrange("p (n s) -> p n s", n=16), kin.ap.rearrange("p (n s) -> p n s", n=16),
                    dlast.ap.unsqueeze(2).to_broadcast([128, 16, 128]), ALU.mult, [kin, dlast], [kst])
            for n4 in range(4):
                ps = self.psr()
                pb = ps.ap.bitcast(BF16)
                for i in range(4):
                    n = n4 * 4 + i
                    self.tr(pb[:, i * 128:(i + 1) * 128], kst.ap[:, n * 128:(n + 1) * 128], self.ident,
                            [kst, self.cm], [ps])
                self.cp(self.evac_eng(), kstm.ap[:, n4 * 4:(n4 + 1) * 4, :],
                        pb[:, 0:512].rearrange("p (a b) -> p a b", a=4), [ps], [kstm])
            P.op("dve", lambda e, o=S.ap: e.memset(o, 0.0), [], [S])
            P.op("dve", lambda e, o=Sb.ap: e.memset(o, 0.0), [], [Sb])
            for n in range(16):
                if n >= 8:
                    c = n - 8
                    aps = self.psr()
                    self.mm(aps.ap[:, 0:128], kin.ap[:, n * 128:(n + 1) * 128], qdec.ap[:, c * 128:(c + 1) * 128],
                            True, True, [kin, qdec], [aps])
                    am = self.rot("attm", attm)
                    self.tt("dve", am.ap, aps.ap[:, 0:128], self.tri, ALU.mult, [aps, self.cm], [am])
                    ops_ = self.psr()
                    for half in range(2):
                        self.mm(ops_.ap[:, half * 128:(half + 1) * 128], vtm.ap[:, n, half * 128:(half + 1) * 128],
                                am.ap, True, False, [vtm, am], [ops_])
                        self.mm(ops_.ap[:, half * 128:(half + 1) * 128], Sb.ap[:, half * 128:(half + 1) * 128],
                                qdec.ap[:, c * 128:(c + 1) * 128], False, True, [Sb, qdec], [ops_])
                    self.cp("act", oTf.ap[:, :, c * 128:(c + 1) * 128],
                            ops_.ap[:, 0:256].rearrange("p (a b) -> p a b", a=2), [ops_], [oTf])
                if n < 15:
                    ups = self.psr()
                    self.mm(ups.ap[:, 0:256], kstm.ap[:, n, :], vtm.ap[:, n, :], True, True, [kstm, vtm], [ups])
                    self.stt(S.ap, S.ap, dlast.ap[:, n:n + 1], ups.ap[:, 0:256], ALU.mult, ALU.add,
                             [S, dlast, ups], [S])
                    self.cp("act", Sb.ap, S.ap, [S], [Sb])
            for tq in range(2):
                ssq = self.psr()
                for half in range(2):
                    s = self.rot("sqb", sqb)
                    self.act(s.ap, oTf.ap[:, half, tq * 512:(tq + 1) * 512], AF.Square, [oTf], [s])
                    self.mm(ssq.ap, self.ones, s.ap, half == 0, half == 1, [s, self.cm], [ssq])
                self.rstd_from_ssq(rstd, ssq, 512, 1.0 / 256)
                for half in range(2):
                    self.act(sg.ap, gT.ap[:, half, tq * 512:(tq + 1) * 512], AF.Silu, [gT], [sg])
                    self.stt(tmp.ap, oTf.ap[:, half, tq * 512:(tq + 1) * 512], self.gng[:, half:half + 1], rstd.ap,
                             ALU.mult, ALU.mult, [oTf, rstd, self.cv], [tmp])
                    self.tt("dve", self.oT.ap[:, 2 * h + half, tq * 512:(tq + 1) * 512], tmp.ap, sg.ap, ALU.mult,
                            [tmp, sg], [self.oT])

    def phase3(self):
        P = self.P
        self.P.barrier(self.scr.ap)
        A, B = self.A, self.B
        A.reset(); B.reset()
        sets = []
        for _ in range(2):
            d = dict(kT=A.take([128, WIN], BF16), qT=A.take([128, TOK], BF16), krf=A.take([128, WIN], F32),
                     qrf=A.take([128, TOK], F32), krb=A.take([128, WIN], BF16), qrb=A.take([128, TOK], BF16),
                     vm=A.take([128, 16, 128], BF16), kmean=B.take([128, 8], F32), gm=B.take([128, 64], F32),
                     mx8=B.take([128, 64], F32), sel=B.take([128, 64], F32), biasb=B.take([128, 64], BF16),
                     biasT=B.take([8, TOK], BF16))
            sets.append(d)
        cosT = B.take([128, WIN], F32)
        sinT = B.take([128, WIN], F32)
        mm_ = B.take([128, 192], F32)
        pT = [B.take([128, 512], BF16), B.take([128, 512], BF16), B.take([128, 512], BF16)]
        rden = B.take([128, 512], F32)
        band = B.take([128, 2048], BF16)
        nr_scr = [(B.take([128, 512], BF16), B.take([128, 512], F32), B.take([128, 512], BF16),
                   B.take([128, 512], F32), B.take([128, 512], F32)) for _ in range(2)]
        P.dma("pool", band.ap, self.band, writes=[band])
        P.dma("sp", cosT.ap, self.rope[:, 0:WIN], writes=[cosT])
        P.dma("sp", sinT.ap, self.rope[:, WIN:2 * WIN], writes=[sinT])
        P.dma("sp", mm_.ap, self.mmask, writes=[mm_])
        M1, M2, M3 = mm_.ap[:, 0:64], mm_.ap[:, 64:128], mm_.ap[:, 128:192]
        scale = 128.0 ** -0.5
        prep_ps = [self.ps[2], self.ps[3], self.ps[6], self.ps[7]]
        attn_ps = [self.ps[0], self.ps[1]]

        def normrope(scr_, jobs):
            s, rstd, kn, t1, t2 = scr_
            for (src, n0, gcol, tab0, outf, outb) in jobs:
                self.act(s.ap, src.ap[:, n0:n0 + 512], AF.Square, [src], [s])
                yield
                ssq = self.rot("prep_ps", prep_ps)
                self.mm(ssq.ap, self.ones, s.ap, True, True, [s, self.cm], [ssq])
                yield
                self.act(rstd.ap, ssq.ap, AF.Sqrt, [ssq, self.epst], [rstd], scale=1.0 / 128, bias=self.epsc)
                yield
                self.recip(rstd.ap, rstd.ap, [rstd], [rstd])
                yield
                self.stt(kn.ap, src.ap[:, n0:n0 + 512], gcol, rstd.ap, ALU.mult, ALU.mult, [src, rstd, self.cv], [kn])
                yield
                rps = self.rot("prep_ps", prep_ps)
                self.mm(rps.ap, self.RT, kn.ap, True, True, [kn, self.cm], [rps])
                yield
                self.tt("dve", t1.ap, rps.ap, sinT.ap[:, tab0:tab0 + 512], ALU.mult, [rps, sinT], [t1])
                yield
                self.tt("pool", t2.ap, kn.ap, cosT.ap[:, tab0:tab0 + 512], ALU.mult, [kn, cosT], [t2])
                yield
                self.tt("pool", outf.ap[:, n0:n0 + 512], t1.ap, t2.ap, ALU.add, [t1, t2], [outf])
                yield
                self.cp("act", outb.ap[:, n0:n0 + 512], outf.ap[:, n0:n0 + 512], [outf], [outb])
                yield

        def inter_gen(gens):
            gens = list(gens)
            while gens:
                for g in list(gens):
                    try:
                        next(g)
                        yield
                    except StopIteration:
                        gens.remove(g)

        def prep(h, d):
            kT, qT, krf, qrf, krb, qrb, vm = d["kT"], d["qT"], d["krf"], d["qrf"], d["krb"], d["qrb"], d["vm"]
            kmean, gm, mx8, sel, biasb, biasT = d["kmean"], d["gm"], d["mx8"], d["sel"], d["biasb"], d["biasT"]
            P.dma("sp", kT.ap, self.pT_d.ap[SEG["mk"] + h * 128:SEG["mk"] + (h + 1) * 128, :],
                  reads=[self.pT_d], writes=[kT])
            P.dma("sp", qT.ap, self.pT_d.ap[SEG["mq"] + h * 128:SEG["mq"] + (h + 1) * 128, TOK:WIN],
                  reads=[self.pT_d], writes=[qT])
            P.dma("sp", vm.ap, self.vtm_d.ap[:, 2048 + h * 128:2048 + (h + 1) * 128]
                  .rearrange("(n p) e -> p n e", p=128), reads=[self.vtm_d], writes=[vm])
            yield
            jobs = [(kT, tq * 512, self.kg, tq * 512, krf, krb) for tq in range(4)] + \
                   [(qT, tq * 512, self.qg, TOK + tq * 512, qrf, qrb) for tq in range(2)]
            for _ in inter_gen([normrope(nr_scr[i], jobs[i::2]) for i in range(2)]):
                yield
            P.op("dve", lambda e, o=kmean.ap, i=krf.ap.rearrange("p (n s) -> p n s", n=8):
                 e.tensor_reduce(out=o, in_=i, axis=AX.X, op=ALU.add), [krf], [kmean])
            yield
            gps = self.rot("prep_ps", prep_ps)
            for jj in range(8):
                self.mm(gps.ap[:, jj * 8:(jj + 1) * 8], qrf.ap[:, jj * 128:(jj + 1) * 128], kmean.ap, True, True,
                        [qrf, kmean], [gps])
            yield
            self.tt("dve", gm.ap, gps.ap[:, 0:64], M1, ALU.mult, [gps, mm_], [gm])
            yield
            self.tt("dve", gm.ap, gm.ap, M2, ALU.add, [gm, mm_], [gm])
            yield
            for jj in range(8):
                P.op("dve", lambda e, o=mx8.ap[:, jj * 8:(jj + 1) * 8], i=gm.ap[:, jj * 8:(jj + 1) * 8]:
                     e.max(out=o, in_=i), [gm], [mx8])
            yield
            self.tt("dve", sel.ap.rearrange("p (a b) -> p a b", a=8), gm.ap.rearrange("p (a b) -> p a b", a=8),
                    mx8.ap.rearrange("p (a b) -> p a b", a=8)[:, :, 3:4].to_broadcast([128, 8, 8]), ALU.is_ge,
                    [gm, mx8], [sel])
            yield
            self.ts("dve", sel.ap, sel.ap, -1.0, -NEG, ALU.add, ALU.mult, [sel], [sel])
            yield
            self.tt("dve", biasb.ap, sel.ap, M3, ALU.add, [sel, mm_], [biasb])
            yield
            bps = self.rot("prep_ps", prep_ps)
            bpb = bps.ap.bitcast(BF16)
            for jj in range(8):
                self.tr(bpb[0:8, jj * 128:(jj + 1) * 128], biasb.ap[:, jj * 8:(jj + 1) * 8], self.ident,
                        [biasb, self.cm], [bps])
            yield
            self.cp("act", biasT.ap, bpb[0:8, 0:TOK], [bps], [biasT])
            yield

        def attn(h, d):
            krb, qrb, vm, biasT = d["krb"], d["qrb"], d["vm"], d["biasT"]
            ops_, dps = self.ps[4], self.ps[5]
            for gq in range(2):
                qs = qrb.ap[:, gq * 512:(gq + 1) * 512]
                kt0 = 8 + gq * 4
                nkt = kt0 + 4

                def scores(kt, qs=qs, gq=gq):
                    sps = self.rot("attn_ps", attn_ps)
                    self.mm(sps.ap, krb.ap[:, kt * 128:(kt + 1) * 128], qs, True, False, [krb, qrb], [sps])
                    self.mm(sps.ap, self.indb.ap[:, (kt // 2) * 128:(kt // 2 + 1) * 128],
                            biasT.ap[:, gq * 512:(gq + 1) * 512], False, True, [self.indb, biasT], [sps])
                    return sps
                nxt = scores(0)
                yield
                for kt in range(nkt):
                    sps = nxt
                    if kt + 1 < nkt:
                        nxt = scores(kt + 1)
                        yield
                    p = self.rot("pT", pT)
                    self.act(p.ap, sps.ap, AF.Exp, [sps], [p], scale=scale)
                    yield
                    if kt >= kt0:
                        dd = kt - kt0
                        self.tt("dve", p.ap, p.ap, band.ap[:, dd * 512:(dd + 1) * 512], ALU.mult, [p, band], [p])
                        yield
                    self.mm(ops_.ap, vm.ap[:, kt, :], p.ap, kt == 0, kt == nkt - 1, [vm, p], [ops_])
                    self.mm(dps.ap, self.ones, p.ap, kt == 0, kt == nkt - 1, [self.cm, p], [dps])
                    yield
                self.recip(rden.ap, dps.ap, [dps], [rden])
                yield
                self.tt("dve", self.oT.ap[:, 16 + h, gq * 512:(gq + 1) * 512], ops_.ap, rden.ap, ALU.mult,
                        [ops_, rden], [self.oT])
                yield

        self.interleave([prep(0, sets[0])])
        for h in range(16):
            gens = [attn(h, sets[h % 2])]
            if h + 1 < 16:
                gens.append(prep(h + 1, sets[(h + 1) % 2]))
            self.interleave(gens)

    def phase4(self):
        P = self.P
        self.P.barrier(self.scr.ap)
        A, B = self.A, self.B
        A.reset(); B.reset()
        self.xn2 = A.take([128, 32, TOK], BF16)
        xn2 = self.xn2
        wbufs = [B.take([128, 32, 512], BF16), B.take([128, 32, 512], BF16)]
        if self.dbg:
            for c in range(32):
                P.dma("sp", self.oT_d.ap[c * 128:(c + 1) * 128, :], self.oT.ap[:, c, :], reads=[self.oT],
                      writes=[self.oT_d])
        sqb = [Tl(self.stage[2].ap.bitcast(BF16)[:, 0:512], self.stage[2].b),
               Tl(self.stage[3].ap.bitcast(BF16)[:, 0:512], self.stage[3].b)]
        x2f = [self.stage[0], self.stage[1]]
        pend = None
        for dg in range(8):
            wb = self.load_wgroup(wbufs, self.w_out, dg * 512, 512)
            for tq in range(2):
                for cc in range(4):
                    c = dg * 4 + cc
                    ps = self.psr(6)
                    for kc in range(32):
                        self.mm(ps.ap, wb.ap[:, kc, cc * 128:(cc + 1) * 128], self.oT.ap[:, kc, tq * 512:(tq + 1) * 512],
                                kc == 0, kc == 31, [wb, self.oT], [ps])
                    xc = self.rot("xc", self.xc)
                    P.dma("sp", xc.ap, self.xT[c * 128:(c + 1) * 128, TOK + tq * 512:TOK + (tq + 1) * 512], writes=[xc])
                    xf = self.rot("x2f", x2f)
                    self.tt("dve", xf.ap, ps.ap, xc.ap, ALU.add, [ps, xc], [xf])
                    P.dma("sp", self.x2T_d.ap[c * 128:(c + 1) * 128, tq * 512:(tq + 1) * 512], xf.ap, reads=[xf],
                          writes=[self.x2T_d])
                    self.cp("act", xn2.ap[:, c, tq * 512:(tq + 1) * 512], xf.ap, [xf], [xn2])
                    s = self.rot("sq4", sqb)
                    self.act(s.ap, xf.ap, AF.Square, [xf], [s])
                    if pend is not None:
                        pend()
                    pend = (lambda s=s, c=c, tq=tq: self.mm(self.ps[6 + tq].ap, self.ones, s.ap, c == 0, c == 31,
                                                            [s, self.cm], [self.ps[6 + tq]]))
        pend()
        rs = [self.stage[0], self.stage[1]]
        for tq in range(2):
            self.rstd_from_ssq(rs[tq], self.ps[6 + tq], 512, 1.0 / D)
        for c in range(32):
            for tq in range(2):
                v = xn2.ap[:, c, tq * 512:(tq + 1) * 512]
                self.stt(v, v, self.g2[:, c:c + 1], rs[tq].ap, ALU.mult, ALU.mult, [xn2, rs[tq], self.cv], [xn2])

    def phase5(self):
        P = self.P
        self.P.barrier(self.scr.ap)
        B, C = self.B, self.C
        B.reset(); C.reset()
        if not hasattr(self, "xn2"):
            self.A.reset()
            self.xn2 = self.A.take([128, 32, TOK], BF16)
            P.op("pool", lambda e, o=self.xn2.ap: e.memset(o, 0.5), [], [self.xn2])
        xn2 = self.xn2
        self.wb6 = [B.take([128, 32, 256], BF16), B.take([128, 32, 256], BF16)]
        wbufs = self.wb6
        self.thr = B.take([128, 64], F32)
        self.b_mark = B.off
        qpT = B.take([128, 16, 512], BF16)
        self.a_arr = C.take([128, 8, 8, 128], F32)
        self.b_arr = C.take([128, 8, 8, 128], F32)
        a_arr, b_arr = self.a_arr, self.b_arr

        def sb(name, shape, dt):
            return B.take(shape, dt)
        KT = sb("KT", [128, 2048], BF16)
        NCH = 4
        scr5 = []
        for i in range(NCH):
            scr5.append((sb("T12", [128, 32], F32), sb("tmp5", [128, 256], F32), sb("cand", [128, 256], F32),
                         sb("c24", [128, 24], F32), sb("e16", [128, 16], F32), sb("sc5", [128, 8], F32)))
        for i in range(4):
            P.dma("pool", KT.ap[:, i * 512:(i + 1) * 512], self.keysT[:, i * 512:(i + 1) * 512], writes=[KT])
        for tt_ in range(8):
            if tt_ % 4 == 0:
                tq = tt_ // 4
                for g in range(8):
                    wb = self.load_wgroup(wbufs, self.w_pq, g * 256, 256)
                    for cc in range(2):
                        ps = self.psr(6)
                        for kc in range(32):
                            self.mm(ps.ap, wb.ap[:, kc, cc * 128:(cc + 1) * 128],
                                    xn2.ap[:, kc, tq * 512:(tq + 1) * 512], kc == 0, kc == 31, [wb, xn2], [ps])
                        self.cp("act", qpT.ap[:, g * 2 + cc, :], ps.ap, [ps], [qpT])
            t0 = (tt_ % 4) * 128
            for h4 in range(4):
                ps = self.psr(6)
                for i in range(4):
                    hp = h4 * 4 + i
                    self.mm(ps.ap[:, i * 128:(i + 1) * 128], qpT.ap[:, hp, t0:t0 + 128],
                            KT.ap[:, hp * 128:(hp + 1) * 128], True, True, [qpT, KT], [ps])
                for p_, arr in ((0, a_arr), (1, b_arr)):
                    self.cp("act", arr.ap[:, tt_, h4 * 2:h4 * 2 + 2, :],
                            ps.ap.rearrange("p (h two k) -> p h two k", two=2, k=128)[:, :, p_, :], [ps], [arr])
        for tt_ in range(8):
            def chain(h, T12, tmp, cand, c24, e16, sc, tt_=tt_):
                for p_, arr in ((0, a_arr), (1, b_arr)):
                    s_ = arr.ap[:, tt_, h, :]
                    o = p_ * 16
                    P.op("dve", lambda e, o_=T12.ap[:, o:o + 8], i=s_: e.max(out=o_, in_=i), [arr], [T12])
                    yield
                    P.op("dve", lambda e, o_=tmp.ap[:, 0:128], r=T12.ap[:, o:o + 8], i=s_:
                         e.match_replace(out=o_, in_to_replace=r, in_values=i, imm_value=-BIG), [T12, arr], [tmp])
                    yield
                    P.op("dve", lambda e, o_=T12.ap[:, o + 8:o + 16], i=tmp.ap[:, 0:128]: e.max(out=o_, in_=i),
                         [tmp], [T12])
                    yield
                self.tt("dve", cand.ap.rearrange("p (a b) -> p a b", a=16),
                        T12.ap[:, 0:16].unsqueeze(2).to_broadcast([128, 16, 16]),
                        T12.ap[:, 16:32].unsqueeze(1).to_broadcast([128, 16, 16]), ALU.add, [T12], [cand])
                yield
                P.op("dve", lambda e, o_=c24.ap[:, 0:8], i=cand.ap: e.max(out=o_, in_=i), [cand], [c24])
                yield
                P.op("dve", lambda e, o_=tmp.ap, r=c24.ap[:, 0:8], i=cand.ap:
                     e.match_replace(out=o_, in_to_replace=r, in_values=i, imm_value=-BIG), [c24, cand], [tmp])
                yield
                P.op("dve", lambda e, o_=c24.ap[:, 8:16], i=tmp.ap: e.max(out=o_, in_=i), [tmp], [c24])
                yield
                P.op("dve", lambda e, o_=cand.ap, r=c24.ap[:, 8:16], i=tmp.ap:
                     e.match_replace(out=o_, in_to_replace=r, in_values=i, imm_value=-BIG), [c24, tmp], [cand])
                yield
                P.op("dve", lambda e, o_=c24.ap[:, 16:24], i=cand.ap: e.max(out=o_, in_=i), [cand], [c24])
                yield
                self.ts("dve", sc.ap[:, 0:1], c24.ap[:, 0:1], -1.0, None, ALU.mult, None, [c24], [sc])
                yield
                self.act(e16.ap, c24.ap[:, 0:16], AF.Exp, [c24, sc], [e16, sc], bias=sc.ap[:, 0:1],
                         accum=sc.ap[:, 1:2])
                yield
                self.act(sc.ap[:, 2:3], sc.ap[:, 1:2], AF.Ln, [sc], [sc])
                yield
                self.tt("dve", sc.ap[:, 3:4], c24.ap[:, 0:1], sc.ap[:, 2:3], ALU.add, [c24, sc], [sc])
                yield
                self.tt("dve", sc.ap[:, 4:5], c24.ap[:, 15:16], c24.ap[:, 16:17], ALU.add, [c24], [sc])
                yield
                self.stt(self.thr.ap[:, tt_ * 8 + h:tt_ * 8 + h + 1], sc.ap[:, 4:5], 0.5, sc.ap[:, 3:4],
                         ALU.mult, ALU.subtract, [sc], [self.thr])
                yield
                self.ts("dve", a_arr.ap[:, tt_, h, :], a_arr.ap[:, tt_, h, :], sc.ap[:, 3:4], None, ALU.subtract, None,
                        [a_arr, sc], [a_arr])
                yield
            for h0 in range(0, 8, NCH):
                self.interleave([chain(h0 + i, *scr5[i]) for i in range(NCH)])

    def phase6(self):
        P = self.P
        nc = self.nc
        xn2, a_arr, b_arr, thr = self.xn2, self.a_arr, self.b_arr, self.thr

        def sb(name, shape, dt):
            return self.B.take(shape, dt)
        self.P.barrier(self.scr.ap)
        self.B.off = self.b_mark
        c8 = [sb("c8_%d" % i, [128, 8], F32) for i in range(2)]
        E = [sb("E%d" % i, [128, 128], F32) for i in range(6)]
        Gm = [sb("Gm%d" % i, [128, 128], BF16) for i in range(6)]
        gl = [sb("gl%d" % i, [128, 512], F32) for i in range(2)]
        Gs = [sb("Gs%d" % i, [128, 512], BF16) for i in range(2)]
        PH0 = 8
        if PH0 < 8:
            ea3 = sb("ea3", [128, 8, 8 - PH0, 128], F32)
            EB3 = sb("EB3", [128, 8, 8 - PH0, 128], BF16)
            Ep = [sb("Ep%d" % i, [128, 128], BF16) for i in range(4)]
            for tt_ in range(8):
                self.act(ea3.ap[:, tt_], a_arr.ap[:, tt_, PH0:8, :], AF.Exp, [a_arr], [ea3])
                self.act(EB3.ap[:, tt_], b_arr.ap[:, tt_, PH0:8, :], AF.Exp, [b_arr], [EB3])
        wst = [Tl(self.stage[i].ap.bitcast(BF16)[:, 0:512], self.stage[i].b) for i in range(4)]
        wbufs = self.wb6
        wb = None
        for i1 in range(128):
            if i1 % 2 == 0:
                wb = self.load_wgroup(wbufs, self.uT, i1 * 128, 256)
            for tq in range(2):
                hps = self.psr(4)
                kc = 0
                gacc = self.rot("gps", self.ps[4:8])
                for t4 in range(4):
                    tt_ = tq * 4 + t4
                    c = self.rot("c8", c8)
                    self.tt("dve", c.ap, thr.ap[:, tt_ * 8:(tt_ + 1) * 8], a_arr.ap[:, tt_, :, i1], ALU.subtract,
                            [thr, a_arr], [c])
                    for h in range(8):
                        if h >= PH0:
                            e_ = self.rot("Ep", Ep)
                            self.ts("pool", e_.ap, EB3.ap[:, tt_, h - PH0, :], ea3.ap[:, tt_, h - PH0, i1:i1 + 1], 1.0,
                                    ALU.mult, ALU.mult, [EB3, ea3], [e_])
                        else:
                            e_ = self.rot("E", E)
                            self.act(e_.ap, b_arr.ap[:, tt_, h, :], AF.Exp, [b_arr, a_arr], [e_],
                                     bias=a_arr.ap[:, tt_, h, i1:i1 + 1])
                        g_ = self.rot("Gm", Gm)
                        self.stt(g_.ap, b_arr.ap[:, tt_, h, :], c.ap[:, h:h + 1], e_.ap, ALU.is_ge, ALU.mult,
                                 [b_arr, c, e_], [g_])
                        self.mm(gacc.ap[:, t4 * 128:(t4 + 1) * 128], self.ident, g_.ap, h == 0, h == 7,
                                [g_, self.cm], [gacc])
                        self.mm(hps.ap, wb.ap[:, kc, (i1 % 2) * 128:(i1 % 2 + 1) * 128],
                                xn2.ap[:, kc, tq * 512:(tq + 1) * 512], kc == 0, kc == 31, [wb, xn2], [hps])
                        kc += 1
                gs_ = self.rot("Gs", Gs)
                self.cp("dve", gs_.ap, gacc.ap, [gacc], [gs_])
                gps = self.rot("gps", self.ps[4:8])
                gpb = gps.ap.bitcast(BF16)
                for t4 in range(4):
                    self.tr(gpb[:, t4 * 128:(t4 + 1) * 128], gs_.ap[:, t4 * 128:(t4 + 1) * 128], self.ident,
                            [gs_, self.cm], [gps])
                g = self.rot("gl", gl)
                self.act(g.ap, hps.ap, AF.Gelu, [hps], [g])
                w = self.rot("wst", wst)
                self.tt("dve", w.ap, g.ap, gpb[:, 0:512], ALU.mult, [g, gps], [w])
                P.dma("sp", self.wT_d.ap[i1, :, tq * 512:(tq + 1) * 512], w.ap, reads=[w], writes=[self.wT_d])
        self.P.barrier(self.scr.ap)
        B = self.B
        B.reset()
        wt = [B.take([128, TOK], BF16) for _ in range(4)]
        vb = [B.take([128, 512], BF16) for _ in range(8)]
        self.A.reset(); self.C.reset()
        res = [self.A.take([128, TOK], BF16) for _ in range(32)] + [self.C.take([128, TOK], BF16) for _ in range(32)]
        while True:
            try:
                res.append(B.take([128, TOK], BF16))
            except AssertionError:
                break
        NRES = len(res)
        for dg in range(8):
            for i1 in range(128):
                if i1 < NRES:
                    w = res[i1]
                    if dg == 0:
                        P.dma("sp", w.ap, self.wT_d.ap[i1], reads=[self.wT_d], writes=[w])
                else:
                    w = self.rot("wt", wt)
                    P.dma("sp", w.ap, self.wT_d.ap[i1], reads=[self.wT_d], writes=[w])
                v = self.rot("vb", vb)
                P.dma("pool", v.ap, self.pv[i1 * 128:(i1 + 1) * 128, dg * 512:(dg + 1) * 512], writes=[v])
                for cc in range(4):
                    for tq in range(2):
                        ps = self.ps[cc * 2 + tq]
                        self.mm(ps.ap, v.ap[:, cc * 128:(cc + 1) * 128], w.ap[:, tq * 512:(tq + 1) * 512],
                                i1 == 0, i1 == 127, [v, w], [ps])
            for cc in range(4):
                for tq in range(2):
                    c = dg * 4 + cc
                    ps = self.ps[cc * 2 + tq]
                    xc = self.rot("xc", self.xc)
                    P.dma("sp", xc.ap, self.x2T_d.ap[c * 128:(c + 1) * 128, tq * 512:(tq + 1) * 512],
                          reads=[self.x2T_d], writes=[xc])
                    o = self.rot("o6", self.stage)
                    self.tt("dve", o.ap, ps.ap, xc.ap, ALU.add, [ps, xc], [o])
                    P.dma("sp", self.outT.ap[c * 128:(c + 1) * 128, tq * 512:(tq + 1) * 512], o.ap, reads=[o],
                          writes=[self.outT])

    def build(self):
        self.setup()
        phases = [self.phase1, self.phase2, self.phase3, self.phase4, self.phase5, self.phase6]
        for i, ph in enumerate(phases):
            if i + 1 > self.stop_after:
                break
            if self.only is not None and (i + 1) not in self.only:
                continue
            ph()
        if self.stop_after < 6:
            z = self.stage[0]
            self.P.op("dve", lambda e, o=z.ap: e.memset(o, 0.0), [], [z])
            self.P.dma("sp", self.outT.ap[0:128, 0:512], z.ap, reads=[z], writes=[self.outT])
        self.stats = self.P.emit()
        return self.nc


def _consts():
    ident = np.eye(128, dtype=np.float32)
    tri = (np.arange(128)[:, None] <= np.arange(128)[None, :]).astype(np.float32)
    RT = np.zeros((128, 128), np.float32)
    for d in range(16):
        RT[d + 16, d] = -1.0
        RT[d, d + 16] = 1.0
    ones = np.ones((128, 128), np.float32)
    cmat = np.concatenate([ident, tri, RT, ones], axis=1)
    ind = np.zeros((8, 8, 128), np.float32)
    for n in range(8):
        ind[n, n, :] = 1.0
    k = np.arange(128)[:, None]
    q = np.arange(512)[None, :]
    band = np.concatenate([((q - k) >= 128 * d).astype(np.float32) for d in range(4)], axis=1)
    return np.ascontiguousarray(cmat), np.ascontiguousarray(ind.reshape(8, 1024)), np.ascontiguousarray(band)


def _rope_tables(s):
    half = 16
    inv = (np.float32(500000.0) ** (-np.arange(half, dtype=np.float32) / np.float32(half))).astype(np.float32)
    pos = (np.arange(WIN) + (s - 1) * TOK).astype(np.float32)
    ang = (pos[:, None] * inv[None, :]).astype(np.float32)
    cos = np.cos(ang).astype(np.float32).T
    sin = np.sin(ang).astype(np.float32).T
    cosT = np.ones((128, WIN), np.float32)
    sinT = np.zeros((128, WIN), np.float32)
    cosT[0:16] = cos
    cosT[16:32] = cos
    sinT[0:16] = sin
    sinT[16:32] = sin
    return np.ascontiguousarray(np.concatenate([cosT, sinT], axis=1))


def _moba_masks(s):
    M1 = np.zeros((8, 8), np.float32)
    M2 = np.full((8, 8), -BIG, np.float32)
    M3 = np.full((8, 8), NEG, np.float32)
    nmin = 4 * (1 - s)
    for jj in range(8):
        r = (8 + jj) // 2
        for n in range(8):
            if nmin <= n < r:
                M1[jj, n] = 1.0
                M2[jj, n] = 0.0
                M3[jj, n] = 0.0
            elif n == r:
                M2[jj, n] = BIG
                M3[jj, n] = 0.0
    m = np.concatenate([M1.reshape(-1), M2.reshape(-1), M3.reshape(-1)])
    return np.ascontiguousarray(np.broadcast_to(m[None, :], (128, 192))).astype(np.float32)


def make_in_maps(x, norm_mix_g, w_in, w_gate_up, b_gate, gla_norm_g, q_norm_g, k_norm_g,
                 w_out, norm_ffn_g, peer_wq, peer_keys, peer_u, peer_v):
    f = np.float32
    x = np.asarray(x, f)
    cmat, ind, band = _consts()
    cvec = np.concatenate([
        np.asarray(norm_mix_g, f)[0].reshape(32, 128).T,
        np.asarray(norm_ffn_g, f)[0].reshape(32, 128).T,
        np.asarray(b_gate, f)[0].reshape(8, 128).T,
        np.asarray(gla_norm_g, f)[0].reshape(2, 128).T,
        np.asarray(q_norm_g, f)[0].reshape(1, 128).T,
        np.asarray(k_norm_g, f)[0].reshape(1, 128).T], axis=1)
    cvec = np.ascontiguousarray(cvec)
    keysT = np.ascontiguousarray(np.asarray(peer_keys, f)[0].transpose(3, 0, 1, 2).reshape(128, 2048))
    uT = np.ascontiguousarray(np.asarray(peer_u, f)[0].T)
    shared = dict(w_in=np.ascontiguousarray(np.asarray(w_in, f)[0]),
                  w_out=np.ascontiguousarray(np.asarray(w_out, f)[0]),
                  w_pq=np.ascontiguousarray(np.asarray(peer_wq, f)[0]),
                  uT=uT, pv=np.ascontiguousarray(np.asarray(peer_v, f)[0]),
                  keysT=keysT, wgu=np.ascontiguousarray(np.asarray(w_gate_up, f)[0]),
                  cvec=cvec, cmat=cmat, ind=ind, band=band)
    ropes = [_rope_tables(0), _rope_tables(1)]
    masks = [_moba_masks(0), _moba_masks(1)]
    maps = []
    for c in range(8):
        b, s = c // 2, c % 2
        xT = np.zeros((D, WIN), f)
        if s == 0:
            xT[:, TOK:] = x[b, 0:TOK].T
        else:
            xT[:, :] = x[b].T
        m = dict(shared)
        m.update(xT=xT, rope=ropes[s], mmask=masks[s])
        maps.append(m)
    return maps


def kernel(x, norm_mix_g, w_in, w_gate_up, b_gate, gla_norm_g, q_norm_g, k_norm_g,
           w_out, norm_ffn_g, peer_wq, peer_keys, peer_u, peer_v):
    maps = make_in_maps(x, norm_mix_g, w_in, w_gate_up, b_gate, gla_norm_g, q_norm_g, k_norm_g,
                        w_out, norm_ffn_g, peer_wq, peer_keys, peer_u, peer_v)
    bld = Builder()
    nc = bld.build()
    res = run_bass_kernel_spmd(nc, maps, core_ids=list(range(8)))
    out = np.zeros((4, 2048, D), np.float32)
    for c in range(8):
        b, s = c // 2, c % 2
        out[b, s * TOK:(s + 1) * TOK, :] = np.asarray(res.results[c]["outT"]).T
    return out
```

```python
import numpy as np
import concourse.bass as bass
import concourse.mybir as mybir
from concourse.bass_utils import run_bass_kernel_spmd

F32 = mybir.dt.float32
BF16 = mybir.dt.bfloat16
AF = mybir.ActivationFunctionType
ALU = mybir.AluOpType
AX = mybir.AxisListType

D = 4096
TOK = 1024
WIN = 2048
NEG = -32768.0
BIG = 1.0e30
EPS = 1e-6


class Buf:
    __slots__ = ("w", "rs", "x")

    def __init__(self):
        self.w = None
        self.rs = []
        self.x = False


class Tl:
    __slots__ = ("ap", "b")

    def __init__(self, ap, b=None):
        self.ap = ap
        self.b = b if b is not None else Buf()

    def __getitem__(self, k):
        return self.ap[k]


class Op:
    __slots__ = ("eng", "fn", "waits", "signal", "is_dma", "sem", "val")


def _bufs(xs):
    return [x.b if isinstance(x, Tl) else x for x in xs]


class Prog:
    ENGS = ("pe", "act", "dve", "pool", "sp")

    def __init__(self, nc, n_dma_sems=40):
        self.nc = nc
        self.ops = []
        self.n_dma_sems = n_dma_sems
        self.G = Buf()
        self.eng_obj = {"pe": nc.tensor, "act": nc.scalar, "dve": nc.vector,
                        "pool": nc.gpsimd, "sp": nc.sync}

    def op(self, eng, fn, reads=(), writes=(), dma=False, barrier=False):
        o = Op()
        o.eng, o.fn, o.is_dma, o.signal, o.sem, o.val = eng, fn, dma, False, None, None
        reads = _bufs(reads)
        writes = _bufs(writes)
        xr = [b for b in reads if b.x and b not in writes]
        if barrier:
            writes = writes + [self.G]
        else:
            reads = reads + [self.G]
        deps = {}
        oid = len(self.ops)
        for b in reads:
            if b.w is not None:
                deps[b.w] = True
        for b in writes:
            if b.w is not None:
                deps.setdefault(b.w, False)
            for r in b.rs:
                deps.setdefault(r, False)
        for b in xr:
            for r in b.rs:
                deps.setdefault(r, False)
        o.waits = deps
        self.ops.append(o)
        for b in writes:
            b.w = oid
            b.rs = []
        for b in reads:
            if b not in writes:
                b.rs.append(oid)
        return oid

    def dma(self, queue, out, in_, reads=(), writes=()):
        return self.op(queue, lambda e, o=out, i=in_: e.dma_start(out=o, in_=i), reads, writes, dma=True)

    def barrier(self, scratch):
        self.op("pool", lambda e, s=scratch: e.memset(s, 0.0), barrier=True)

    def emit(self):
        nc, ops = self.nc, self.ops
        eng_seq = {e: 0 for e in self.ENGS}
        nxt = 0
        nxt_sw = 0
        n_hw = self.n_dma_sems - 12
        slot_cnt = [0] * self.n_dma_sems
        slot_last = [None] * self.n_dma_sems
        key = [None] * len(ops)
        for i, o in enumerate(ops):
            if o.is_dma:
                if o.eng == "pool":
                    k = n_hw + nxt_sw
                    nxt_sw = (nxt_sw + 1) % (self.n_dma_sems - n_hw)
                else:
                    k = nxt
                    nxt = (nxt + 1) % n_hw
                slot_cnt[k] += 1
                key[i] = (("dma", k), slot_cnt[k])
                if slot_last[k] is not None:
                    o.waits[slot_last[k]] = True
                slot_last[k] = i
            else:
                eng_seq[o.eng] += 1
                key[i] = (o.eng, eng_seq[o.eng])
        seen = {e: {} for e in self.ENGS}
        final_waits = [None] * len(ops)
        for i, o in enumerate(ops):
            fw = {}
            sn = seen[o.eng]
            for p, raw in o.waits.items():
                po = ops[p]
                sk, order = key[p]
                if (not po.is_dma) and po.eng == o.eng:
                    if o.eng in ("pe", "sp"):
                        continue
                if sn.get(sk, 0) >= order:
                    continue
                if fw.get(sk, (0, None))[0] < order:
                    fw[sk] = (order, p)
            for sk, (order, p) in fw.items():
                sn[sk] = order
                ops[p].signal = True
            final_waits[i] = [p for (_, p) in fw.values()]
        LIMIT = 1500
        sems, cnt, nsem = {}, {}, [0]

        def fresh(sk):
            sems[sk] = nc.alloc_semaphore(name="s%d" % nsem[0])
            nsem[0] += 1
            cnt[sk] = 0
        for e in ("pe", "act", "dve", "pool"):
            fresh(e)
        for k in range(self.n_dma_sems):
            fresh(("dma", k))
        maxc = 0
        for i, o in enumerate(ops):
            e = self.eng_obj[o.eng]
            for p in final_waits[i]:
                e.wait_ge(ops[p].sem, ops[p].val)
            inst = o.fn(e)
            if o.is_dma or o.signal:
                sk = key[i][0] if o.is_dma else o.eng
                inc = 16 if o.is_dma else 1
                if cnt[sk] + inc > LIMIT:
                    fresh(sk)
                cnt[sk] += inc
                maxc = max(maxc, cnt[sk])
                o.sem, o.val = sems[sk], cnt[sk]
                inst.then_inc(sems[sk], inc)
        last = {}
        for i, o in enumerate(ops):
            if o.is_dma:
                last[key[i][0]] = o
        for sk, o in last.items():
            nc.sync.wait_ge(o.sem, o.val)
        return dict(n_ops=len(ops), max_cnt=maxc, n_sems=nsem[0])


class Arena:
    def __init__(self, ap_bf16):
        self.ap = ap_bf16
        self.n = ap_bf16.shape[1]
        self.off = 0

    def reset(self):
        self.off = 0

    def take(self, shape, dt):
        free = int(np.prod(shape[1:]))
        nb = free * (2 if dt == BF16 else 4)
        nb = (nb + 63) // 64 * 64
        assert self.off + nb // 2 <= self.n, ("arena overflow", shape, self.off, self.n)
        v = self.ap[0:shape[0], self.off:self.off + nb // 2]
        self.off += nb // 2
        if dt != BF16:
            v = v.bitcast(dt)
        v = v[:, 0:free]
        if len(shape) == 3:
            v = v.rearrange("p (a b) -> p a b", a=shape[1])
        elif len(shape) == 4:
            v = v.rearrange("p (a b c) -> p a b c", a=shape[1], b=shape[2])
        return Tl(v)


SEG = dict(gq=0, gk=1024, gv=2048, gg=4096, glr=6144, mq=6160, mk=8208, mv=10256)
IN_COLS = 12304


class Builder:
    def __init__(self, stop_after=99, dbg=False, only=None, small=()):
        self.stop_after = stop_after
        self.dbg = dbg
        self.only = only
        self.small = set(small)
        nc = bass.Bass("TRN2", target_bir_lowering=False)
        self.nc = nc
        self.P = Prog(nc)
        self._rot = {}

        def din(name, shape, dt=F32):
            if name in self.small:
                shape = [128, 128]
            return nc.dram_tensor(name, list(shape), dt, kind="ExternalInput").ap()

        self.xT = din("xT", [D, WIN])
        self.w_in = din("w_in", [D, IN_COLS])
        self.w_out = din("w_out", [D, D])
        self.w_pq = din("w_pq", [D, 2048])
        self.uT = din("uT", [D, 16384])
        self.pv = din("pv", [16384, D])
        self.keysT = din("keysT", [128, 16 * 128])
        self.wgu = din("wgu", [16, 1024])
        self.cvec = din("cvec", [128, 32 + 32 + 8 + 2 + 1 + 1])
        self.rope = din("rope", [128, 2 * WIN])
        self.mmask = din("mmask", [128, 3 * 64])
        self.cmat = din("cmat", [128, 4 * 128])
        self.ind = din("ind", [8, 8 * 128])
        self.band = din("band", [128, 2048])
        kind = "ExternalOutput" if dbg else "Internal"
        self.pT_d = Tl(nc.dram_tensor("pT_d", [IN_COLS, WIN], BF16, kind=kind).ap())
        self.vtm_d = Tl(nc.dram_tensor("vtm_d", [WIN, 4096], BF16, kind=kind).ap())
        self.x2T_d = Tl(nc.dram_tensor("x2T_d", [D, TOK], F32, kind=kind).ap())
        self.oT_d = Tl(nc.dram_tensor("oT_d", [D, TOK], BF16, kind=kind).ap()) if dbg else None
        self.wT_d = Tl(nc.dram_tensor("wT_d", [128, 128, TOK], BF16, kind="Internal").ap())
        self.outT = Tl(nc.dram_tensor("outT", [D, TOK], F32, kind="ExternalOutput").ap())

        def sb(name, shape, dt):
            return Tl(nc.alloc_sbuf_tensor(name, list(shape), dt).ap())

        self.A = Arena(nc.alloc_sbuf_tensor("arenaA", [128, 32768], BF16).ap())
        self.B = Arena(nc.alloc_sbuf_tensor("arenaB", [128, 32768], BF16).ap())
        self.C = Arena(nc.alloc_sbuf_tensor("arenaC", [128, 32768], BF16).ap())
        self.cm = sb("cm", [128, 512], BF16)
        self.ident = self.cm.ap[:, 0:128]
        self.tri = self.cm.ap[:, 128:256]
        self.RT = self.cm.ap[:, 256:384]
        self.ones = self.cm.ap[:, 384:512]
        self.cv = sb("cv", [128, 76], F32)
        self.nb = sb("nb", [128, 8], F32)
        self.indb = sb("indb", [8, 1024], BF16)
        self.scr = sb("scr", [128, 8], F32)
        self.stage = [sb("stage%d" % i, [128, 512], F32) for i in range(4)]
        self.xc = [sb("xc%d" % i, [128, 512], F32) for i in range(2)]
        self.cm_f = self.stage[0]
        self.ps = [Tl(nc.alloc_psum_tensor("ps%d" % i, [128, 512], F32).ap()) for i in range(8)]
        for p_ in self.ps:
            p_.b.x = True

    def rot(self, name, lst):
        i = self._rot.get(name, 0)
        self._rot[name] = i + 1
        return lst[i % len(lst)]

    def psr(self, n=4):
        return self.rot("ps%d" % n, self.ps[0:n])

    def mm(self, out, lhsT, rhs, start, stop, R, W):
        self.P.op("pe", lambda e, o=out, l=lhsT, r=rhs, s=start, t=stop:
                  e.matmul(o, lhsT=l, rhs=r, start=s, stop=t), R, W)

    def tr(self, out, in_, ident, R, W):
        self.P.op("pe", lambda e, o=out, i=in_, d=ident: e.transpose(o, i, d), R, W)

    def act(self, out, in_, func, R, W, scale=None, bias=None, accum=None):
        kw = {}
        if scale is not None:
            kw["scale"] = scale
        if bias is not None:
            kw["bias"] = bias
        if accum is not None:
            kw["accum_out"] = accum
        self.P.op("act", lambda e, o=out, i=in_, f=func, k=kw: e.activation(out=o, in_=i, func=f, **k), R, W)

    def tt(self, eng, out, in0, in1, op, R, W):
        self.P.op(eng, lambda e, o=out, a=in0, b=in1, p=op: e.tensor_tensor(out=o, in0=a, in1=b, op=p), R, W)

    def stt(self, out, in0, scalar, in1, op0, op1, R, W):
        self.P.op("dve", lambda e, o=out, a=in0, s=scalar, b=in1, p0=op0, p1=op1:
                  e.scalar_tensor_tensor(out=o, in0=a, scalar=s, in1=b, op0=p0, op1=p1), R, W)

    def ts(self, eng, out, in0, s1, s2, op0, op1, R, W):
        if op1 is None:
            self.P.op(eng, lambda e, o=out, a=in0, x=s1, p0=op0:
                      e.tensor_scalar(out=o, in0=a, scalar1=x, scalar2=None, op0=p0), R, W)
        else:
            self.P.op(eng, lambda e, o=out, a=in0, x=s1, y=s2, p0=op0, p1=op1:
                      e.tensor_scalar(out=o, in0=a, scalar1=x, scalar2=y, op0=p0, op1=p1), R, W)

    def cp(self, eng, out, in_, R, W):
        if eng == "act":
            self.P.op("act", lambda e, o=out, i=in_: e.copy(out=o, in_=i), R, W)
        else:
            self.P.op(eng, lambda e, o=out, i=in_: e.tensor_copy(out=o, in_=i), R, W)

    @staticmethod
    def interleave(gens):
        gens = list(gens)
        while gens:
            for g in list(gens):
                try:
                    next(g)
                except StopIteration:
                    gens.remove(g)

    def evac_eng(self):
        return self.rot("evac", ["act", "dve"])

    def recip(self, out, in_, R, W):
        self.P.op("dve", lambda e, o=out, i=in_: e.reciprocal(out=o, in_=i), R, W)

    def rstd_from_ssq(self, out_t, ssq_ps, n, inv_n):
        self.act(out_t.ap[:, 0:n], ssq_ps.ap[:, 0:n], AF.Sqrt, [ssq_ps, self.epst], [out_t], scale=inv_n, bias=self.epsc)
        self.recip(out_t.ap[:, 0:n], out_t.ap[:, 0:n], [out_t], [out_t])

    def setup(self):
        P = self.P
        P.dma("sp", self.cm_f.ap, self.cmat, writes=[self.cm_f])
        self.cp("dve", self.cm.ap, self.cm_f.ap, [self.cm_f], [self.cm])
        P.dma("sp", self.cv.ap, self.cvec, writes=[self.cv])
        self.ts("dve", self.nb.ap, self.cv.ap[:, 64:72], -1.0, None, ALU.mult, None, [self.cv], [self.nb])
        P.dma("pool", self.indb.ap, self.ind, writes=[self.indb])
        self.epst = Tl(self.nc.alloc_sbuf_tensor("epst", [128, 1], F32).ap())
        P.op("dve", lambda e, o=self.epst.ap: e.memset(o, EPS), [], [self.epst])
        self.epsc = self.epst.ap[:, 0:1]
        self.g1 = self.cv.ap[:, 0:32]
        self.g2 = self.cv.ap[:, 32:64]
        self.gng = self.cv.ap[:, 72:74]
        self.qg = self.cv.ap[:, 74:75]
        self.kg = self.cv.ap[:, 75:76]

    def load_wgroup(self, wbufs, wdram, c0, ncols):
        wb = self.rot("wg", wbufs)
        src = wdram[:, c0:c0 + ncols].rearrange("(c p) n -> p c n", p=128)
        half = 16
        self.P.dma("pool", wb.ap[:, 0:half, 0:ncols], src[:, 0:half, :], writes=[wb])
        self.P.dma("pool", wb.ap[:, half:32, 0:ncols], src[:, half:32, :], writes=[wb])
        return wb

    def phase1(self):
        P = self.P
        self.A.reset(); self.B.reset(); self.C.reset()
        hn = [self.A.take([128, 32, 512], BF16), self.A.take([128, 32, 512], BF16),
              self.B.take([128, 32, 512], BF16), self.B.take([128, 32, 512], BF16)]
        wbufs = [self.C.take([128, 32, 512], BF16), self.C.take([128, 32, 512], BF16)]
        def halves(t):
            v = t.ap.bitcast(BF16)
            return [Tl(v[:, 0:512]), Tl(v[:, 512:1024])]
        sq = halves(self.stage[2])
        stg = halves(self.stage[1]) + halves(self.stage[3])
        rstd = self.stage[0]

        def norm_gen(tq):
            ssq = self.ps[7]
            for kc in range(32):
                xc = self.rot("xc", self.xc)
                P.dma("sp", xc.ap, self.xT[kc * 128:(kc + 1) * 128, tq * 512:(tq + 1) * 512], writes=[xc])
                s = self.rot("sq", sq)
                self.act(s.ap, xc.ap, AF.Square, [xc], [s])
                self.mm(ssq.ap, self.ones, s.ap, kc == 0, kc == 31, [s, self.cm], [ssq])
                yield
            self.rstd_from_ssq(rstd, ssq, 512, 1.0 / D)
            yield
            for kc in range(32):
                xc = self.rot("xc", self.xc)
                P.dma("sp", xc.ap, self.xT[kc * 128:(kc + 1) * 128, tq * 512:(tq + 1) * 512], writes=[xc])
                self.stt(hn[tq].ap[:, kc, :], xc.ap, self.g1[:, kc:kc + 1], rstd.ap, ALU.mult, ALU.mult,
                         [xc, rstd, self.cv], [hn[tq]])
                yield

        groups = []
        for name, n, lay, allt in (("gq", 1024, "fm", False), ("gg", 2048, "fm", False), ("mq", 2048, "fm", False),
                                   ("gk", 1024, "fm", True), ("gv", 2048, "tm", True), ("glr", 16, "fm", True),
                                   ("mk", 2048, "fm", True), ("mv", 2048, "tm", True)):
            for g0 in range(0, n, 512):
                groups.append((name, SEG[name] + g0, min(512, n - g0), lay, allt, g0))

        def proj_gen():
            for (name, c0, ncols, lay, allt, g0) in groups:
                wb = self.load_wgroup(wbufs, self.w_in, c0, ncols)
                tqs = range(4) if allt else range(2, 4)
                if lay == "fm":
                    for tq in tqs:
                        for cc in range((ncols + 127) // 128):
                            m = min(128, ncols - cc * 128)
                            ps = self.psr(6)
                            for kc in range(32):
                                self.mm(ps.ap[0:m, :], wb.ap[:, kc, cc * 128:cc * 128 + m], hn[tq].ap[:, kc, :],
                                        kc == 0, kc == 31, [wb, hn[tq]], [ps])
                            st = self.rot("stg", stg)
                            self.cp(self.evac_eng(), st.ap[0:m, :], ps.ap[0:m, :], [ps], [st])
                            r0 = c0 + cc * 128
                            P.dma("sp", self.pT_d.ap[r0:r0 + m, tq * 512:(tq + 1) * 512], st.ap[0:m, :],
                                  reads=[st], writes=[self.pT_d])
                            yield
                else:
                    vcol = (0 if name == "gv" else 2048) + g0
                    for tq in tqs:
                        for t4 in range(4):
                            ps = self.psr(6)
                            for kc in range(32):
                                self.mm(ps.ap, hn[tq].ap[:, kc, t4 * 128:(t4 + 1) * 128], wb.ap[:, kc, :],
                                        kc == 0, kc == 31, [wb, hn[tq]], [ps])
                            st = self.rot("stg", stg)
                            self.cp(self.evac_eng(), st.ap, ps.ap, [ps], [st])
                            t0 = tq * 512 + t4 * 128
                            P.dma("sp", self.vtm_d.ap[t0:t0 + 128, vcol:vcol + 512], st.ap,
                                  reads=[st], writes=[self.vtm_d])
                            yield

        for tq in (2, 3):
            for _ in norm_gen(tq):
                pass

        def chain2():
            for tq in (0, 1):
                for _ in norm_gen(tq):
                    yield
        ng = chain2()
        for _ in proj_gen():
            for _ in range(4):
                next(ng, None)
        for _ in ng:
            pass

    def phase2(self):
        P = self.P
        self.P.barrier(self.scr.ap)
        A, B, C = self.A, self.B, self.C
        A.reset(); B.reset(); C.reset()
        self.oT = C.take([128, 32, TOK], BF16)
        glrT = A.take([16, WIN], BF16)
        wgu = A.take([16, 1024], BF16)
        sp = A.take([128, WIN], F32)
        cum = A.take([128, WIN], F32)
        enb = A.take([128, WIN], F32)
        eb = A.take([128, TOK], F32)
        rmask = A.take([128, WIN], BF16)
        kT = A.take([128, WIN], BF16)
        qT = A.take([128, TOK], BF16)
        qdec = A.take([128, TOK], BF16)
        kin = A.take([128, WIN], BF16)
        kst = B.take([128, WIN], BF16)
        kstm = B.take([128, 16, 128], BF16)
        vtm = B.take([128, 16, 256], BF16)
        dlast = B.take([128, 16], F32)
        S = B.take([128, 256], F32)
        Sb = B.take([128, 256], BF16)
        attm = [B.take([128, 128], BF16), B.take([128, 128], BF16)]
        oTf = B.take([128, 2, TOK], F32)
        gT = B.take([128, 2, TOK], BF16)
        sqb = [B.take([128, 512], BF16), B.take([128, 512], BF16)]
        rstd = B.take([128, 512], F32)
        sg = B.take([128, 512], F32)
        tmp = B.take([128, 512], F32)
        P.dma("sp", glrT.ap, self.pT_d.ap[SEG["glr"]:SEG["glr"] + 16, :], reads=[self.pT_d], writes=[glrT])
        P.dma("pool", wgu.ap, self.wgu, writes=[wgu])
        P.op("pool", lambda e, o=rmask.ap: e.memset(o, 1.0), [], [rmask])
        P.op("pool", lambda e, o=rmask.ap[:, 0:WIN:128]: e.memset(o, 0.0), [], [rmask])
        for h in range(8):
            P.dma("sp", kT.ap, self.pT_d.ap[SEG["gk"] + h * 128:SEG["gk"] + (h + 1) * 128, :],
                  reads=[self.pT_d], writes=[kT])
            P.dma("sp", qT.ap, self.pT_d.ap[SEG["gq"] + h * 128:SEG["gq"] + (h + 1) * 128, TOK:WIN],
                  reads=[self.pT_d], writes=[qT])
            P.dma("sp", vtm.ap, self.vtm_d.ap[:, h * 256:(h + 1) * 256].rearrange("(n p) e -> p n e", p=128),
                  reads=[self.vtm_d], writes=[vtm])
            P.dma("sp", gT.ap, self.pT_d.ap[SEG["gg"] + h * 256:SEG["gg"] + (h + 1) * 256, TOK:WIN]
                  .rearrange("(c p) t -> p c t", p=128), reads=[self.pT_d], writes=[gT])
            for tq in range(4):
                ps = self.psr()
                self.mm(ps.ap, wgu.ap[:, h * 128:(h + 1) * 128], glrT.ap[:, tq * 512:(tq + 1) * 512], True, True,
                        [wgu, glrT], [ps])
                self.act(sp.ap[:, tq * 512:(tq + 1) * 512], ps.ap, AF.Exp, [ps, self.nb], [sp],
                         scale=-1.0, bias=self.nb.ap[:, h:h + 1])
            self.act(sp.ap, sp.ap, AF.Ln, [sp], [sp], bias=1.0)
            P.op("dve", lambda e, o=cum.ap, m=rmask.ap, s=sp.ap:
                 e.tensor_tensor_scan(out=o, data0=m, data1=s, initial=0.0, op0=ALU.mult, op1=ALU.add),
                 [rmask, sp], [cum])
            self.act(eb.ap, cum.ap[:, TOK:WIN], AF.Exp, [cum], [eb], scale=-1.0 / 16)
            self.act(enb.ap, cum.ap, AF.Exp, [cum], [enb], scale=1.0 / 16)
            self.act(dlast.ap, cum.ap[:, 127:WIN:128], AF.Exp, [cum], [dlast], scale=-1.0 / 16)
            self.stt(qdec.ap, qT.ap, 128.0 ** -0.5, eb.ap, ALU.mult, ALU.mult, [qT, eb], [qdec])
            self.tt("dve", kin.ap, kT.ap, enb.ap, ALU.mult, [kT, enb], [kin])
            self.tt("pool", kst.ap.rearrange("p (n s) -> p n s", n=16), kin.ap.rearrange("p (n s) -> p n s", n=16),
                    dlast.ap.unsqueeze(2).to_broadcast([128, 16, 128]), ALU.mult, [kin, dlast], [kst])
            for n4 in range(4):
                ps = self.psr()
                pb = ps.ap.bitcast(BF16)
                for i in range(4):
                    n = n4 * 4 + i
                    self.tr(pb[:, i * 128:(i + 1) * 128], kst.ap[:, n * 128:(n + 1) * 128], self.ident,
                            [kst, self.cm], [ps])
                self.cp(self.evac_eng(), kstm.ap[:, n4 * 4:(n4 + 1) * 4, :],
                        pb[:, 0:512].rearrange("p (a b) -> p a b", a=4), [ps], [kstm])
            P.op("dve", lambda e, o=S.ap: e.memset(o, 0.0), [], [S])
            P.op("dve", lambda e, o=Sb.ap: e.memset(o, 0.0), [], [Sb])
            for n in range(16):
                if n >= 8:
                    c = n - 8
                    aps = self.psr()
                    self.mm(aps.ap[:, 0:128], kin.ap[:, n * 128:(n + 1) * 128], qdec.ap[:, c * 128:(c + 1) * 128],
                            True, True, [kin, qdec], [aps])
                    am = self.rot("attm", attm)
                    self.tt("dve", am.ap, aps.ap[:, 0:128], self.tri, ALU.mult, [aps, self.cm], [am])
                    ops_ = self.psr()
                    for half in range(2):
                        self.mm(ops_.ap[:, half * 128:(half + 1) * 128], vtm.ap[:, n, half * 128:(half + 1) * 128],
                                am.ap, True, False, [vtm, am], [ops_])
                        self.mm(ops_.ap[:, half * 128:(half + 1) * 128], Sb.ap[:, half * 128:(half + 1) * 128],
                                qdec.ap[:, c * 128:(c + 1) * 128], False, True, [Sb, qdec], [ops_])
                    self.cp("act", oTf.ap[:, :, c * 128:(c + 1) * 128],
                            ops_.ap[:, 0:256].rearrange("p (a b) -> p a b", a=2), [ops_], [oTf])
                if n < 15:
                    ups = self.psr()
                    self.mm(ups.ap[:, 0:256], kstm.ap[:, n, :], vtm.ap[:, n, :], True, True, [kstm, vtm], [ups])
                    self.stt(S.ap, S.ap, dlast.ap[:, n:n + 1], ups.ap[:, 0:256], ALU.mult, ALU.add,
                             [S, dlast, ups], [S])
                    self.cp("act", Sb.ap, S.ap, [S], [Sb])
            for tq in range(2):
                ssq = self.psr()
                for half in range(2):
                    s = self.rot("sqb", sqb)
                    self.act(s.ap, oTf.ap[:, half, tq * 512:(tq + 1) * 512], AF.Square, [oTf], [s])
                    self.mm(ssq.ap, self.ones, s.ap, half == 0, half == 1, [s, self.cm], [ssq])
                self.rstd_from_ssq(rstd, ssq, 512, 1.0 / 256)
                for half in range(2):
                    self.act(sg.ap, gT.ap[:, half, tq * 512:(tq + 1) * 512], AF.Silu, [gT], [sg])
                    self.stt(tmp.ap, oTf.ap[:, half, tq * 512:(tq + 1) * 512], self.gng[:, half:half + 1], rstd.ap,
                             ALU.mult, ALU.mult, [oTf, rstd, self.cv], [tmp])
                    self.tt("dve", self.oT.ap[:, 2 * h + half, tq * 512:(tq + 1) * 512], tmp.ap, sg.ap, ALU.mult,
                            [tmp, sg], [self.oT])

    def phase3(self):
        P = self.P
        self.P.barrier(self.scr.ap)
        A, B = self.A, self.B
        A.reset(); B.reset()
        sets = []
        for _ in range(2):
            d = dict(kT=A.take([128, WIN], BF16), qT=A.take([128, TOK], BF16), krf=A.take([128, WIN], F32),
                     qrf=A.take([128, TOK], F32), krb=A.take([128, WIN], BF16), qrb=A.take([128, TOK], BF16),
                     vm=A.take([128, 16, 128], BF16), kmean=B.take([128, 8], F32), gm=B.take([128, 64], F32),
                     mx8=B.take([128, 64], F32), sel=B.take([128, 64], F32), biasb=B.take([128, 64], BF16),
                     biasT=B.take([8, TOK], BF16))
            sets.append(d)
        cosT = B.take([128, WIN], F32)
        sinT = B.take([128, WIN], F32)
        mm_ = B.take([128, 192], F32)
        pT = [B.take([128, 512], BF16), B.take([128, 512], BF16), B.take([128, 512], BF16)]
        rden = B.take([128, 512], F32)
        band = B.take([128, 2048], BF16)
        nr_scr = [(B.take([128, 512], BF16), B.take([128, 512], F32), B.take([128, 512], BF16),
                   B.take([128, 512], F32), B.take([128, 512], F32)) for _ in range(2)]
        P.dma("pool", band.ap, self.band, writes=[band])
        P.dma("sp", cosT.ap, self.rope[:, 0:WIN], writes=[cosT])
        P.dma("sp", sinT.ap, self.rope[:, WIN:2 * WIN], writes=[sinT])
        P.dma("sp", mm_.ap, self.mmask, writes=[mm_])
        M1, M2, M3 = mm_.ap[:, 0:64], mm_.ap[:, 64:128], mm_.ap[:, 128:192]
        scale = 128.0 ** -0.5
        prep_ps = [self.ps[2], self.ps[3], self.ps[6], self.ps[7]]
        attn_ps = [self.ps[0], self.ps[1]]

        def normrope(scr_, jobs):
            s, rstd, kn, t1, t2 = scr_
            for (src, n0, gcol, tab0, outf, outb) in jobs:
                self.act(s.ap, src.ap[:, n0:n0 + 512], AF.Square, [src], [s])
                yield
                ssq = self.rot("prep_ps", prep_ps)
                self.mm(ssq.ap, self.ones, s.ap, True, True, [s, self.cm], [ssq])
                yield
                self.act(rstd.ap, ssq.ap, AF.Sqrt, [ssq, self.epst], [rstd], scale=1.0 / 128, bias=self.epsc)
                yield
                self.recip(rstd.ap, rstd.ap, [rstd], [rstd])
                yield
                self.stt(kn.ap, src.ap[:, n0:n0 + 512], gcol, rstd.ap, ALU.mult, ALU.mult, [src, rstd, self.cv], [kn])
                yield
                rps = self.rot("prep_ps", prep_ps)
                self.mm(rps.ap, self.RT, kn.ap, True, True, [kn, self.cm], [rps])
                yield
                self.tt("dve", t1.ap, rps.ap, sinT.ap[:, tab0:tab0 + 512], ALU.mult, [rps, sinT], [t1])
                yield
                self.tt("pool", t2.ap, kn.ap, cosT.ap[:, tab0:tab0 + 512], ALU.mult, [kn, cosT], [t2])
                yield
                self.tt("pool", outf.ap[:, n0:n0 + 512], t1.ap, t2.ap, ALU.add, [t1, t2], [outf])
                yield
                self.cp("act", outb.ap[:, n0:n0 + 512], outf.ap[:, n0:n0 + 512], [outf], [outb])
                yield

        def inter_gen(gens):
            gens = list(gens)
            while gens:
                for g in list(gens):
                    try:
                        next(g)
                        yield
                    except StopIteration:
                        gens.remove(g)

        def prep(h, d):
            kT, qT, krf, qrf, krb, qrb, vm = d["kT"], d["qT"], d["krf"], d["qrf"], d["krb"], d["qrb"], d["vm"]
            kmean, gm, mx8, sel, biasb, biasT = d["kmean"], d["gm"], d["mx8"], d["sel"], d["biasb"], d["biasT"]
            P.dma("sp", kT.ap, self.pT_d.ap[SEG["mk"] + h * 128:SEG["mk"] + (h + 1) * 128, :],
                  reads=[self.pT_d], writes=[kT])
            P.dma("sp", qT.ap, self.pT_d.ap[SEG["mq"] + h * 128:SEG["mq"] + (h + 1) * 128, TOK:WIN],
                  reads=[self.pT_d], writes=[qT])
            P.dma("sp", vm.ap, self.vtm_d.ap[:, 2048 + h * 128:2048 + (h + 1) * 128]
                  .rearrange("(n p) e -> p n e", p=128), reads=[self.vtm_d], writes=[vm])
            yield
            jobs = [(kT, tq * 512, self.kg, tq * 512, krf, krb) for tq in range(4)] + \
                   [(qT, tq * 512, self.qg, TOK + tq * 512, qrf, qrb) for tq in range(2)]
            for _ in inter_gen([normrope(nr_scr[i], jobs[i::2]) for i in range(2)]):
                yield
            P.op("dve", lambda e, o=kmean.ap, i=krf.ap.rearrange("p (n s) -> p n s", n=8):
                 e.tensor_reduce(out=o, in_=i, axis=AX.X, op=ALU.add), [krf], [kmean])
            yield
            gps = self.rot("prep_ps", prep_ps)
            for jj in range(8):
                self.mm(gps.ap[:, jj * 8:(jj + 1) * 8], qrf.ap[:, jj * 128:(jj + 1) * 128], kmean.ap, True, True,
                        [qrf, kmean], [gps])
            yield
            self.tt("dve", gm.ap, gps.ap[:, 0:64], M1, ALU.mult, [gps, mm_], [gm])
            yield
            self.tt("dve", gm.ap, gm.ap, M2, ALU.add, [gm, mm_], [gm])
            yield
            for jj in range(8):
                P.op("dve", lambda e, o=mx8.ap[:, jj * 8:(jj + 1) * 8], i=gm.ap[:, jj * 8:(jj + 1) * 8]:
                     e.max(out=o, in_=i), [gm], [mx8])
            yield
            self.tt("dve", sel.ap.rearrange("p (a b) -> p a b", a=8), gm.ap.rearrange("p (a b) -> p a b", a=8),
                    mx8.ap.rearrange("p (a b) -> p a b", a=8)[:, :, 3:4].to_broadcast([128, 8, 8]), ALU.is_ge,
                    [gm, mx8], [sel])
            yield
            self.ts("dve", sel.ap, sel.ap, -1.0, -NEG, ALU.add, ALU.mult, [sel], [sel])
            yield
            self.tt("dve", biasb.ap, sel.ap, M3, ALU.add, [sel, mm_], [biasb])
            yield
            bps = self.rot("prep_ps", prep_ps)
            bpb = bps.ap.bitcast(BF16)
            for jj in range(8):
                self.tr(bpb[0:8, jj * 128:(jj + 1) * 128], biasb.ap[:, jj * 8:(jj + 1) * 8], self.ident,
                        [biasb, self.cm], [bps])
            yield
            self.cp("act", biasT.ap, bpb[0:8, 0:TOK], [bps], [biasT])
            yield

        def attn(h, d):
            krb, qrb, vm, biasT = d["krb"], d["qrb"], d["vm"], d["biasT"]
            ops_, dps = self.ps[4], self.ps[5]
            for gq in range(2):
                qs = qrb.ap[:, gq * 512:(gq + 1) * 512]
                kt0 = 8 + gq * 4
                nkt = kt0 + 4

                def scores(kt, qs=qs, gq=gq):
                    sps = self.rot("attn_ps", attn_ps)
                    self.mm(sps.ap, krb.ap[:, kt * 128:(kt + 1) * 128], qs, True, False, [krb, qrb], [sps])
                    self.mm(sps.ap, self.indb.ap[:, (kt // 2) * 128:(kt // 2 + 1) * 128],
                            biasT.ap[:, gq * 512:(gq + 1) * 512], False, True, [self.indb, biasT], [sps])
                    return sps
                nxt = scores(0)
                yield
                for kt in range(nkt):
                    sps = nxt
                    if kt + 1 < nkt:
                        nxt = scores(kt + 1)
                        yield
                    p = self.rot("pT", pT)
                    self.act(p.ap, sps.ap, AF.Exp, [sps], [p], scale=scale)
                    yield
                    if kt >= kt0:
                        dd = kt - kt0
                        self.tt("dve", p.ap, p.ap, band.ap[:, dd * 512:(dd + 1) * 512], ALU.mult, [p, band], [p])
                        yield
                    self.mm(ops_.ap, vm.ap[:, kt, :], p.ap, kt == 0, kt == nkt - 1, [vm, p], [ops_])
                    self.mm(dps.ap, self.ones, p.ap, kt == 0, kt == nkt - 1, [self.cm, p], [dps])
                    yield
                self.recip(rden.ap, dps.ap, [dps], [rden])
                yield
                self.tt("dve", self.oT.ap[:, 16 + h, gq * 512:(gq + 1) * 512], ops_.ap, rden.ap, ALU.mult,
                        [ops_, rden], [self.oT])
                yield

        self.interleave([prep(0, sets[0])])
        for h in range(16):
            gens = [attn(h, sets[h % 2])]
            if h + 1 < 16:
                gens.append(prep(h + 1, sets[(h + 1) % 2]))
            self.interleave(gens)

    def phase4(self):
        P = self.P
        self.P.barrier(self.scr.ap)
        A, B = self.A, self.B
        A.reset(); B.reset()
        self.xn2 = A.take([128, 32, TOK], BF16)
        xn2 = self.xn2
        wbufs = [B.take([128, 32, 512], BF16), B.take([128, 32, 512], BF16)]
        if self.dbg:
            for c in range(32):
                P.dma("sp", self.oT_d.ap[c * 128:(c + 1) * 128, :], self.oT.ap[:, c, :], reads=[self.oT],
                      writes=[self.oT_d])
        sqb = [Tl(self.stage[2].ap.bitcast(BF16)[:, 0:512], self.stage[2].b),
               Tl(self.stage[3].ap.bitcast(BF16)[:, 0:512], self.stage[3].b)]
        x2f = [self.stage[0], self.stage[1]]
        pend = None
        for dg in range(8):
            wb = self.load_wgroup(wbufs, self.w_out, dg * 512, 512)
            for tq in range(2):
                for cc in range(4):
                    c = dg * 4 + cc
                    ps = self.psr(6)
                    for kc in range(32):
                        self.mm(ps.ap, wb.ap[:, kc, cc * 128:(cc + 1) * 128], self.oT.ap[:, kc, tq * 512:(tq + 1) * 512],
                                kc == 0, kc == 31, [wb, self.oT], [ps])
                    xc = self.rot("xc", self.xc)
                    P.dma("sp", xc.ap, self.xT[c * 128:(c + 1) * 128, TOK + tq * 512:TOK + (tq + 1) * 512], writes=[xc])
                    xf = self.rot("x2f", x2f)
                    self.tt("dve", xf.ap, ps.ap, xc.ap, ALU.add, [ps, xc], [xf])
                    P.dma("sp", self.x2T_d.ap[c * 128:(c + 1) * 128, tq * 512:(tq + 1) * 512], xf.ap, reads=[xf],
                          writes=[self.x2T_d])
                    self.cp("act", xn2.ap[:, c, tq * 512:(tq + 1) * 512], xf.ap, [xf], [xn2])
                    s = self.rot("sq4", sqb)
                    self.act(s.ap, xf.ap, AF.Square, [xf], [s])
                    if pend is not None:
                        pend()
                    pend = (lambda s=s, c=c, tq=tq: self.mm(self.ps[6 + tq].ap, self.ones, s.ap, c == 0, c == 31,
                                                            [s, self.cm], [self.ps[6 + tq]]))
        pend()
        rs = [self.stage[0], self.stage[1]]
        for tq in range(2):
            self.rstd_from_ssq(rs[tq], self.ps[6 + tq], 512, 1.0 / D)
        for c in range(32):
            for tq in range(2):
                v = xn2.ap[:, c, tq * 512:(tq + 1) * 512]
                self.stt(v, v, self.g2[:, c:c + 1], rs[tq].ap, ALU.mult, ALU.mult, [xn2, rs[tq], self.cv], [xn2])

    def phase5(self):
        P = self.P
        self.P.barrier(self.scr.ap)
        B, C = self.B, self.C
        B.reset(); C.reset()
        if not hasattr(self, "xn2"):
            self.A.reset()
            self.xn2 = self.A.take([128, 32, TOK], BF16)
            P.op("pool", lambda e, o=self.xn2.ap: e.memset(o, 0.5), [], [self.xn2])
        xn2 = self.xn2
        self.wb6 = [B.take([128, 32, 256], BF16), B.take([128, 32, 256], BF16)]
        wbufs = self.wb6
        self.thr = B.take([128, 64], F32)
        self.b_mark = B.off
        qpT = B.take([128, 16, 512], BF16)
        self.a_arr = C.take([128, 8, 8, 128], F32)
        self.b_arr = C.take([128, 8, 8, 128], F32)
        a_arr, b_arr = self.a_arr, self.b_arr

        def sb(name, shape, dt):
            return B.take(shape, dt)
        KT = sb("KT", [128, 2048], BF16)
        NCH = 4
        scr5 = []
        for i in range(NCH):
            scr5.append((sb("T12", [128, 32], F32), sb("tmp5", [128, 256], F32), sb("cand", [128, 256], F32),
                         sb("c24", [128, 24], F32), sb("e16", [128, 16], F32), sb("sc5", [128, 8], F32)))
        for i in range(4):
            P.dma("pool", KT.ap[:, i * 512:(i + 1) * 512], self.keysT[:, i * 512:(i + 1) * 512], writes=[KT])
        for tt_ in range(8):
            if tt_ % 4 == 0:
                tq = tt_ // 4
                for g in range(8):
                    wb = self.load_wgroup(wbufs, self.w_pq, g * 256, 256)
                    for cc in range(2):
                        ps = self.psr(6)
                        for kc in range(32):
                            self.mm(ps.ap, wb.ap[:, kc, cc * 128:(cc + 1) * 128],
                                    xn2.ap[:, kc, tq * 512:(tq + 1) * 512], kc == 0, kc == 31, [wb, xn2], [ps])
                        self.cp("act", qpT.ap[:, g * 2 + cc, :], ps.ap, [ps], [qpT])
            t0 = (tt_ % 4) * 128
            for h4 in range(4):
                ps = self.psr(6)
                for i in range(4):
                    hp = h4 * 4 + i
                    self.mm(ps.ap[:, i * 128:(i + 1) * 128], qpT.ap[:, hp, t0:t0 + 128],
                            KT.ap[:, hp * 128:(hp + 1) * 128], True, True, [qpT, KT], [ps])
                for p_, arr in ((0, a_arr), (1, b_arr)):
                    self.cp("act", arr.ap[:, tt_, h4 * 2:h4 * 2 + 2, :],
                            ps.ap.rearrange("p (h two k) -> p h two k", two=2, k=128)[:, :, p_, :], [ps], [arr])
        for tt_ in range(8):
            def chain(h, T12, tmp, cand, c24, e16, sc, tt_=tt_):
                for p_, arr in ((0, a_arr), (1, b_arr)):
                    s_ = arr.ap[:, tt_, h, :]
                    o = p_ * 16
                    P.op("dve", lambda e, o_=T12.ap[:, o:o + 8], i=s_: e.max(out=o_, in_=i), [arr], [T12])
                    yield
                    P.op("dve", lambda e, o_=tmp.ap[:, 0:128], r=T12.ap[:, o:o + 8], i=s_:
                         e.match_replace(out=o_, in_to_replace=r, in_values=i, imm_value=-BIG), [T12, arr], [tmp])
                    yield
                    P.op("dve", lambda e, o_=T12.ap[:, o + 8:o + 16], i=tmp.ap[:, 0:128]: e.max(out=o_, in_=i),
                         [tmp], [T12])
                    yield
                self.tt("dve", cand.ap.rearrange("p (a b) -> p a b", a=16),
                        T12.ap[:, 0:16].unsqueeze(2).to_broadcast([128, 16, 16]),
                        T12.ap[:, 16:32].unsqueeze(1).to_broadcast([128, 16, 16]), ALU.add, [T12], [cand])
                yield
                P.op("dve", lambda e, o_=c24.ap[:, 0:8], i=cand.ap: e.max(out=o_, in_=i), [cand], [c24])
                yield
                P.op("dve", lambda e, o_=tmp.ap, r=c24.ap[:, 0:8], i=cand.ap:
                     e.match_replace(out=o_, in_to_replace=r, in_values=i, imm_value=-BIG), [c24, cand], [tmp])
                yield
                P.op("dve", lambda e, o_=c24.ap[:, 8:16], i=tmp.ap: e.max(out=o_, in_=i), [tmp], [c24])
                yield
                P.op("dve", lambda e, o_=cand.ap, r=c24.ap[:, 8:16], i=tmp.ap:
                     e.match_replace(out=o_, in_to_replace=r, in_values=i, imm_value=-BIG), [c24, tmp], [cand])
                yield
                P.op("dve", lambda e, o_=c24.ap[:, 16:24], i=cand.ap: e.max(out=o_, in_=i), [cand], [c24])
                yield
                self.ts("dve", sc.ap[:, 0:1], c24.ap[:, 0:1], -1.0, None, ALU.mult, None, [c24], [sc])
                yield
                self.act(e16.ap, c24.ap[:, 0:16], AF.Exp, [c24, sc], [e16, sc], bias=sc.ap[:, 0:1],
                         accum=sc.ap[:, 1:2])
                yield
                self.act(sc.ap[:, 2:3], sc.ap[:, 1:2], AF.Ln, [sc], [sc])
                yield
                self.tt("dve", sc.ap[:, 3:4], c24.ap[:, 0:1], sc.ap[:, 2:3], ALU.add, [c24, sc], [sc])
                yield
                self.tt("dve", sc.ap[:, 4:5], c24.ap[:, 15:16], c24.ap[:, 16:17], ALU.add, [c24], [sc])
                yield
                self.stt(self.thr.ap[:, tt_ * 8 + h:tt_ * 8 + h + 1], sc.ap[:, 4:5], 0.5, sc.ap[:, 3:4],
                         ALU.mult, ALU.subtract, [sc], [self.thr])
                yield
                self.ts("dve", a_arr.ap[:, tt_, h, :], a_arr.ap[:, tt_, h, :], sc.ap[:, 3:4], None, ALU.subtract, None,
                        [a_arr, sc], [a_arr])
                yield
            for h0 in range(0, 8, NCH):
                self.interleave([chain(h0 + i, *scr5[i]) for i in range(NCH)])

    def phase6(self):
        P = self.P
        nc = self.nc
        xn2, a_arr, b_arr, thr = self.xn2, self.a_arr, self.b_arr, self.thr

        def sb(name, shape, dt):
            return self.B.take(shape, dt)
        self.P.barrier(self.scr.ap)
        self.B.off = self.b_mark
        c8 = [sb("c8_%d" % i, [128, 8], F32) for i in range(2)]
        E = [sb("E%d" % i, [128, 128], F32) for i in range(6)]
        Gm = [sb("Gm%d" % i, [128, 128], BF16) for i in range(6)]
        gl = [sb("gl%d" % i, [128, 512], F32) for i in range(2)]
        Gs = [sb("Gs%d" % i, [128, 512], BF16) for i in range(2)]
        PH0 = 7
        ea3 = sb("ea3", [128, 8, 8 - PH0, 128], F32)
        EB3 = sb("EB3", [128, 8, 8 - PH0, 128], BF16)
        Ep = [sb("Ep%d" % i, [128, 128], BF16) for i in range(3)]
        for tt_ in range(8):
            self.act(ea3.ap[:, tt_], a_arr.ap[:, tt_, PH0:8, :], AF.Exp, [a_arr], [ea3])
            self.act(EB3.ap[:, tt_], b_arr.ap[:, tt_, PH0:8, :], AF.Exp, [b_arr], [EB3])
        wst = [Tl(self.stage[i].ap.bitcast(BF16)[:, 0:512], self.stage[i].b) for i in range(4)]
        wbufs = self.wb6
        wb = None
        for i1 in range(128):
            if i1 % 2 == 0:
                wb = self.load_wgroup(wbufs, self.uT, i1 * 128, 256)
            for tq in range(2):
                hps = self.psr(4)
                kc = 0
                gacc = self.rot("gps", self.ps[4:8])
                for t4 in range(4):
                    tt_ = tq * 4 + t4
                    c = self.rot("c8", c8)
                    self.tt("dve", c.ap, thr.ap[:, tt_ * 8:(tt_ + 1) * 8], a_arr.ap[:, tt_, :, i1], ALU.subtract,
                            [thr, a_arr], [c])
                    g1s = {}
                    for h in range(PH0, 8):
                        g1 = self.rot("Ep", Ep)
                        self.stt(g1.ap, b_arr.ap[:, tt_, h, :], c.ap[:, h:h + 1], EB3.ap[:, tt_, h - PH0, :],
                                 ALU.is_ge, ALU.mult, [b_arr, c, EB3], [g1])
                        g1s[h] = g1
                    for h in range(8):
                        if h >= PH0:
                            g_ = self.rot("Gm", Gm)
                            self.ts("dve", g_.ap, g1s[h].ap, ea3.ap[:, tt_, h - PH0, i1:i1 + 1], None, ALU.mult, None,
                                    [g1s[h], ea3], [g_])
                            self.mm(gacc.ap[:, t4 * 128:(t4 + 1) * 128], self.ident, g_.ap, h == 0, h == 7,
                                    [g_, self.cm], [gacc])
                            self.mm(hps.ap, wb.ap[:, kc, (i1 % 2) * 128:(i1 % 2 + 1) * 128],
                                    xn2.ap[:, kc, tq * 512:(tq + 1) * 512], kc == 0, kc == 31, [wb, xn2], [hps])
                            kc += 1
                            continue
                        e_ = self.rot("E", E)
                        self.act(e_.ap, b_arr.ap[:, tt_, h, :], AF.Exp, [b_arr, a_arr], [e_],
                                 bias=a_arr.ap[:, tt_, h, i1:i1 + 1])
                        g_ = self.rot("Gm", Gm)
                        self.stt(g_.ap, b_arr.ap[:, tt_, h, :], c.ap[:, h:h + 1], e_.ap, ALU.is_ge, ALU.mult,
                                 [b_arr, c, e_], [g_])
                        self.mm(gacc.ap[:, t4 * 128:(t4 + 1) * 128], self.ident, g_.ap, h == 0, h == 7,
                                [g_, self.cm], [gacc])
                        self.mm(hps.ap, wb.ap[:, kc, (i1 % 2) * 128:(i1 % 2 + 1) * 128],
                                xn2.ap[:, kc, tq * 512:(tq + 1) * 512], kc == 0, kc == 31, [wb, xn2], [hps])
                        kc += 1
                gs_ = self.rot("Gs", Gs)
                self.cp("dve", gs_.ap, gacc.ap, [gacc], [gs_])
                gps = self.rot("gps", self.ps[4:8])
                gpb = gps.ap.bitcast(BF16)
                for t4 in range(4):
                    self.tr(gpb[:, t4 * 128:(t4 + 1) * 128], gs_.ap[:, t4 * 128:(t4 + 1) * 128], self.ident,
                            [gs_, self.cm], [gps])
                g = self.rot("gl", gl)
                self.act(g.ap, hps.ap, AF.Gelu, [hps], [g])
                w = self.rot("wst", wst)
                self.tt("dve", w.ap, g.ap, gpb[:, 0:512], ALU.mult, [g, gps], [w])
                P.dma("sp", self.wT_d.ap[i1, :, tq * 512:(tq + 1) * 512], w.ap, reads=[w], writes=[self.wT_d])
        self.P.barrier(self.scr.ap)
        B = self.B
        B.reset()
        wt = [B.take([128, TOK], BF16) for _ in range(4)]
        vb = [B.take([128, 512], BF16) for _ in range(8)]
        self.A.reset(); self.C.reset()
        res = [self.A.take([128, TOK], BF16) for _ in range(32)] + [self.C.take([128, TOK], BF16) for _ in range(32)]
        while True:
            try:
                res.append(B.take([128, TOK], BF16))
            except AssertionError:
                break
        NRES = len(res)
        for dg in range(8):
            for i1 in range(128):
                if i1 < NRES:
                    w = res[i1]
                    if dg == 0:
                        P.dma("sp", w.ap, self.wT_d.ap[i1], reads=[self.wT_d], writes=[w])
                else:
                    w = self.rot("wt", wt)
                    P.dma("sp", w.ap, self.wT_d.ap[i1], reads=[self.wT_d], writes=[w])
                v = self.rot("vb", vb)
                P.dma("pool", v.ap, self.pv[i1 * 128:(i1 + 1) * 128, dg * 512:(dg + 1) * 512], writes=[v])
                for cc in range(4):
                    for tq in range(2):
                        ps = self.ps[cc * 2 + tq]
                        self.mm(ps.ap, v.ap[:, cc * 128:(cc + 1) * 128], w.ap[:, tq * 512:(tq + 1) * 512],
                                i1 == 0, i1 == 127, [v, w], [ps])
            for cc in range(4):
                for tq in range(2):
                    c = dg * 4 + cc
                    ps = self.ps[cc * 2 + tq]
                    xc = self.rot("xc", self.xc)
                    P.dma("sp", xc.ap, self.x2T_d.ap[c * 128:(c + 1) * 128, tq * 512:(tq + 1) * 512],
                          reads=[self.x2T_d], writes=[xc])
                    o = self.rot("o6", self.stage)
                    self.tt("dve", o.ap, ps.ap, xc.ap, ALU.add, [ps, xc], [o])
                    P.dma("sp", self.outT.ap[c * 128:(c + 1) * 128, tq * 512:(tq + 1) * 512], o.ap, reads=[o],
                          writes=[self.outT])

    def build(self):
        self.setup()
        phases = [self.phase1, self.phase2, self.phase3, self.phase4, self.phase5, self.phase6]
        for i, ph in enumerate(phases):
            if i + 1 > self.stop_after:
                break
            if self.only is not None and (i + 1) not in self.only:
                continue
            ph()
        if self.stop_after < 6:
            z = self.stage[0]
            self.P.op("dve", lambda e, o=z.ap: e.memset(o, 0.0), [], [z])
            self.P.dma("sp", self.outT.ap[0:128, 0:512], z.ap, reads=[z], writes=[self.outT])
        self.stats = self.P.emit()
        return self.nc


def _consts():
    ident = np.eye(128, dtype=np.float32)
    tri = (np.arange(128)[:, None] <= np.arange(128)[None, :]).astype(np.float32)
    RT = np.zeros((128, 128), np.float32)
    for d in range(16):
        RT[d + 16, d] = -1.0
        RT[d, d + 16] = 1.0
    ones = np.ones((128, 128), np.float32)
    cmat = np.concatenate([ident, tri, RT, ones], axis=1)
    ind = np.zeros((8, 8, 128), np.float32)
    for n in range(8):
        ind[n, n, :] = 1.0
    k = np.arange(128)[:, None]
    q = np.arange(512)[None, :]
    band = np.concatenate([((q - k) >= 128 * d).astype(np.float32) for d in range(4)], axis=1)
    return np.ascontiguousarray(cmat), np.ascontiguousarray(ind.reshape(8, 1024)), np.ascontiguousarray(band)


def _rope_tables(s):
    half = 16
    inv = (np.float32(500000.0) ** (-np.arange(half, dtype=np.float32) / np.float32(half))).astype(np.float32)
    pos = (np.arange(WIN) + (s - 1) * TOK).astype(np.float32)
    ang = (pos[:, None] * inv[None, :]).astype(np.float32)
    cos = np.cos(ang).astype(np.float32).T
    sin = np.sin(ang).astype(np.float32).T
    cosT = np.ones((128, WIN), np.float32)
    sinT = np.zeros((128, WIN), np.float32)
    cosT[0:16] = cos
    cosT[16:32] = cos
    sinT[0:16] = sin
    sinT[16:32] = sin
    return np.ascontiguousarray(np.concatenate([cosT, sinT], axis=1))


def _moba_masks(s):
    M1 = np.zeros((8, 8), np.float32)
    M2 = np.full((8, 8), -BIG, np.float32)
    M3 = np.full((8, 8), NEG, np.float32)
    nmin = 4 * (1 - s)
    for jj in range(8):
        r = (8 + jj) // 2
        for n in range(8):
            if nmin <= n < r:
                M1[jj, n] = 1.0
                M2[jj, n] = 0.0
                M3[jj, n] = 0.0
            elif n == r:
                M2[jj, n] = BIG
                M3[jj, n] = 0.0
    m = np.concatenate([M1.reshape(-1), M2.reshape(-1), M3.reshape(-1)])
    return np.ascontiguousarray(np.broadcast_to(m[None, :], (128, 192))).astype(np.float32)


def make_in_maps(x, norm_mix_g, w_in, w_gate_up, b_gate, gla_norm_g, q_norm_g, k_norm_g,
                 w_out, norm_ffn_g, peer_wq, peer_keys, peer_u, peer_v):
    f = np.float32
    x = np.asarray(x, f)
    cmat, ind, band = _consts()
    cvec = np.concatenate([
        np.asarray(norm_mix_g, f)[0].reshape(32, 128).T,
        np.asarray(norm_ffn_g, f)[0].reshape(32, 128).T,
        np.asarray(b_gate, f)[0].reshape(8, 128).T,
        np.asarray(gla_norm_g, f)[0].reshape(2, 128).T,
        np.asarray(q_norm_g, f)[0].reshape(1, 128).T,
        np.asarray(k_norm_g, f)[0].reshape(1, 128).T], axis=1)
    cvec = np.ascontiguousarray(cvec)
    keysT = np.ascontiguousarray(np.asarray(peer_keys, f)[0].transpose(3, 0, 1, 2).reshape(128, 2048))
    uT = np.ascontiguousarray(np.asarray(peer_u, f)[0].T)
    shared = dict(w_in=np.ascontiguousarray(np.asarray(w_in, f)[0]),
                  w_out=np.ascontiguousarray(np.asarray(w_out, f)[0]),
                  w_pq=np.ascontiguousarray(np.asarray(peer_wq, f)[0]),
                  uT=uT, pv=np.ascontiguousarray(np.asarray(peer_v, f)[0]),
                  keysT=keysT, wgu=np.ascontiguousarray(np.asarray(w_gate_up, f)[0]),
                  cvec=cvec, cmat=cmat, ind=ind, band=band)
    ropes = [_rope_tables(0), _rope_tables(1)]
    masks = [_moba_masks(0), _moba_masks(1)]
    maps = []
    for c in range(8):
        b, s = c // 2, c % 2
        xT = np.zeros((D, WIN), f)
        if s == 0:
            xT[:, TOK:] = x[b, 0:TOK].T
        else:
            xT[:, :] = x[b].T
        m = dict(shared)
        m.update(xT=xT, rope=ropes[s], mmask=masks[s])
        maps.append(m)
    return maps


def kernel(x, norm_mix_g, w_in, w_gate_up, b_gate, gla_norm_g, q_norm_g, k_norm_g,
           w_out, norm_ffn_g, peer_wq, peer_keys, peer_u, peer_v):
    maps = make_in_maps(x, norm_mix_g, w_in, w_gate_up, b_gate, gla_norm_g, q_norm_g, k_norm_g,
                        w_out, norm_ffn_g, peer_wq, peer_keys, peer_u, peer_v)
    bld = Builder()
    nc = bld.build()
    res = run_bass_kernel_spmd(nc, maps, core_ids=list(range(8)))
    out = np.zeros((4, 2048, D), np.float32)
    for c in range(8):
        b, s = c // 2, c % 2
        out[b, s * TOK:(s + 1) * TOK, :] = np.asarray(res.results[c]["outT"]).T
    return out
```
